# Optimizing a Trainium2 kernel written in Bass

```python
import math, functools
import jax, jax.numpy as jnp
from jax import lax
import numpy as np

D_MODEL = 2048
BATCH = 4
SEQ = 2048
DEPTH = 1
DEC_BATCH = 128
DEC_SEQ = 1
PAST_LEN = 16384
PAGE_SIZE = 128

N_META = 16
CHUNK = 128
D_MIX = D_MODEL
D_SSM = D_MIX // 2
D_RET = D_MIX - D_SSM
SSM_HEADDIM = 64
SSM_HEADS = D_SSM // SSM_HEADDIM
SSM_GROUPS = 2
HEADS_PER_GROUP = SSM_HEADS // SSM_GROUPS
D_STATE = 128
CONV_W = 4
CONV_DIM = D_SSM + 2 * SSM_GROUPS * D_STATE
RET_HEADS = 4
RET_HEADDIM = D_RET // RET_HEADS
ROPE_BASE = 10000.0
D_FF = -(-(8 * D_MODEL) // (3 * 256)) * 256
PROJ_SPLITS = (D_SSM, D_SSM + CONV_DIM, D_SSM + CONV_DIM + SSM_HEADS,
               D_SSM + CONV_DIM + SSM_HEADS + D_RET,
               D_SSM + CONV_DIM + SSM_HEADS + 2 * D_RET,
               D_SSM + CONV_DIM + SSM_HEADS + 3 * D_RET)
PROJ_DIM = D_SSM + CONV_DIM + SSM_HEADS + 4 * D_RET
EPS = 1e-6

kernel_name = 'hymba_ssd_retention_step'


def normalize(x):
    xf = x.astype(jnp.float32)
    return xf * lax.rsqrt(jnp.mean(xf * xf, axis=-1, keepdims=True) + EPS)


def rmsnorm(x, g):
    return (normalize(x) * g.astype(jnp.float32)).astype(x.dtype)


def rotary(x, pos):
    half = x.shape[-1] // 2
    inv_freq = ROPE_BASE ** (-jnp.arange(half, dtype=jnp.float32) / half)
    ang = (pos[:, None] * inv_freq[None, :])[:, None, :]
    cos, sin = jnp.cos(ang), jnp.sin(ang)
    x1, x2 = x[..., :half], x[..., half:]
    return jnp.concatenate([x1 * cos - x2 * sin, x1 * sin + x2 * cos], axis=-1)


def causal_conv(xbc, buf, w, b):
    l = xbc.shape[1]
    full = jnp.concatenate([buf, xbc], axis=1)
    out = b
    for i in range(CONV_W):
        out = out + full[:, i:i + l] * w[i]
    return jax.nn.silu(out), full[:, l:]


def scan_chunks(step, state, xs, chunk):
    b, L = xs[0].shape[:2]
    n = L // chunk
    def split(a):
        return jnp.moveaxis(a.reshape(b, n, chunk, *a.shape[2:]), 1, 0)
    state, ys = lax.scan(step, state, tuple(split(a) for a in xs))
    ys = jnp.moveaxis(ys, 0, 1).reshape(b, L, *ys.shape[3:])
    return state, ys


def ssd_chunk(a_neg, h, inp):
    x, dt, bm, cm = inp
    b, c = x.shape[:2]
    x = x.reshape(b, c, SSM_GROUPS, HEADS_PER_GROUP, SSM_HEADDIM)
    dtg = dt.reshape(b, c, SSM_GROUPS, HEADS_PER_GROUP)
    hg = h.reshape(b, SSM_GROUPS, HEADS_PER_GROUP, SSM_HEADDIM, D_STATE)
    lcum = jnp.cumsum(dtg * a_neg.reshape(SSM_GROUPS, HEADS_PER_GROUP), axis=1)
    causal = jnp.tril(jnp.ones((c, c), dtype=bool))
    seg = lcum[:, :, None] - lcum[:, None, :]
    decay = jnp.exp(jnp.where(causal[None, :, :, None, None], seg, -jnp.inf))
    cb = jnp.einsum('bign,bjgn->bijg', cm, bm)
    w = cb[..., None] * decay * dtg[:, None]
    y = jnp.einsum('bijgh,bjghp->bighp', w, x)
    y = y + jnp.einsum('bign,bghpn->bighp', cm, hg) * jnp.exp(lcum)[..., None]
    last = lcum[:, -1]
    wts = jnp.exp(last[:, None] - lcum) * dtg
    hg = jnp.exp(last)[..., None, None] * hg + jnp.einsum('bjgh,bjgn,bjghp->bghpn', wts, bm, x)
    return hg.reshape(b, SSM_HEADS, SSM_HEADDIM, D_STATE), y.reshape(b, c, SSM_HEADS, SSM_HEADDIM)


def retention_chunk(s, inp):
    q, k, v = inp
    c = q.shape[1]
    lg = jnp.log1p(-(2.0 ** (-5.0 - jnp.arange(RET_HEADS, dtype=jnp.float32))))
    idx = jnp.arange(c, dtype=jnp.float32)
    diff = idx[:, None] - idx[None, :]
    decay = jnp.where((diff >= 0)[..., None], jnp.exp(jnp.maximum(diff, 0.0)[..., None] * lg), 0.0)
    scores = jnp.einsum('bihd,bjhd->bijh', q, k) * decay
    y = jnp.einsum('bijh,bjhv->bihv', scores, v)
    y = y + jnp.einsum('bihd,bhdv->bihv', q, s) * jnp.exp((idx + 1.0)[:, None] * lg)[..., None]
    wk = jnp.exp((c - 1.0 - idx)[:, None] * lg)
    s = jnp.exp(c * lg)[:, None, None] * s + jnp.einsum('bjhd,bjhv->bhdv', k * wk[..., None], v)
    return s, y


def mixers(u, pos, conv_buf, ssm_h, ret_s, segments, w_in, conv_w, conv_b, dt_bias, a_log, d_skip, ssm_norm_g, ret_norm_g):
    b, l, _ = u.shape
    f32 = jnp.float32
    proj = (u @ w_in).astype(f32)
    z, xbc, dt_raw, q, k, v, g = jnp.split(proj, PROJ_SPLITS, axis=-1)
    xbc, new_conv = causal_conv(xbc, conv_buf.astype(f32), conv_w.astype(f32), conv_b.astype(f32))
    xs, bm, cm = jnp.split(xbc, (D_SSM, D_SSM + SSM_GROUPS * D_STATE), axis=-1)
    xs = xs.reshape(b, l, SSM_HEADS, SSM_HEADDIM)
    bm = bm.reshape(b, l, SSM_GROUPS, D_STATE)
    cm = cm.reshape(b, l, SSM_GROUPS, D_STATE)
    dt = jax.nn.softplus(dt_raw + dt_bias.astype(f32))
    a_neg = -jnp.exp(a_log.astype(f32))
    q = rotary(q.reshape(b, l, RET_HEADS, RET_HEADDIM), pos)
    k = rotary(k.reshape(b, l, RET_HEADS, RET_HEADDIM), pos) * (RET_HEADDIM ** -0.5)
    v = v.reshape(b, l, RET_HEADS, RET_HEADDIM)
    ssm_h = ssm_h.astype(f32)
    ret_s = ret_s.astype(f32)
    ssd_step = functools.partial(ssd_chunk, a_neg)
    y_ssm, y_ret = [], []
    start = 0
    for length, chunk in segments:
        sl = slice(start, start + length)
        ssm_h, ys = scan_chunks(ssd_step, ssm_h, (xs[:, sl], dt[:, sl], bm[:, sl], cm[:, sl]), chunk)
        ret_s, yr = scan_chunks(retention_chunk, ret_s, (q[:, sl], k[:, sl], v[:, sl]), chunk)
        y_ssm.append(ys)
        y_ret.append(yr)
        start += length
    y1 = jnp.concatenate(y_ssm, axis=1) + d_skip.astype(f32)[:, None] * xs
    y1 = (y1.reshape(b, l, D_SSM) * jax.nn.silu(z)).reshape(b, l, SSM_GROUPS, D_SSM // SSM_GROUPS)
    y1 = normalize(y1).reshape(b, l, D_SSM) * ssm_norm_g.astype(f32)
    y2 = normalize(jnp.concatenate(y_ret, axis=1)).reshape(b, l, D_RET) * ret_norm_g.astype(f32) * jax.nn.silu(g)
    mix = jnp.concatenate([y1, y2], axis=-1).astype(u.dtype)
    return mix, new_conv, ssm_h, ret_s


def layer(h, pos, conv_buf, ssm_h, ret_s, segments, pre_mix_g, post_mix_g, pre_ffn_g, post_ffn_g,
          w_in, conv_w, conv_b, dt_bias, a_log, d_skip, ssm_norm_g, ret_norm_g, w_out, w_gate, w_up, w_down):
    u = rmsnorm(h, pre_mix_g)
    mix, conv_buf, ssm_h, ret_s = mixers(u, pos, conv_buf, ssm_h, ret_s, segments, w_in, conv_w, conv_b,
                                         dt_bias, a_log, d_skip, ssm_norm_g, ret_norm_g)
    h = h + rmsnorm(mix @ w_out, post_mix_g)
    f = rmsnorm(h, pre_ffn_g)
    f = (jax.nn.silu(f @ w_gate) * (f @ w_up)) @ w_down
    h = h + rmsnorm(f, post_ffn_g)
    return h, conv_buf, ssm_h, ret_s


def setup_inputs(seed: int = 0) -> dict:
    key = jax.random.key(seed)
    ks = jax.random.split(key, 24)
    f32 = jnp.float32
    def nrm(k, shape, scale):
        return jax.random.normal(k, shape, f32) * scale
    def gain(k, shape):
        return 1.0 + 0.05 * jax.random.normal(k, shape, f32)
    dt0 = jnp.exp(jax.random.uniform(ks[14], (DEPTH, SSM_HEADS), f32, math.log(1e-3), math.log(1e-1)))
    dt_bias = dt0 + jnp.log(-jnp.expm1(-dt0))
    return dict(
        x_prompt=nrm(ks[0], (BATCH, SEQ, D_MODEL), 1.0),
        x_sample=nrm(ks[1], (DEC_BATCH, DEC_SEQ, D_MODEL), 1.0),
        state_conv=nrm(ks[2], (DEPTH, DEC_BATCH, CONV_W - 1, CONV_DIM), 1.0),
        state_ssm=nrm(ks[3], (DEPTH, DEC_BATCH, SSM_HEADS, SSM_HEADDIM, D_STATE), 0.5),
        state_ret=nrm(ks[4], (DEPTH, DEC_BATCH, RET_HEADS, RET_HEADDIM, RET_HEADDIM), 0.5),
        meta_tokens=nrm(ks[5], (N_META, D_MODEL), 1.0),
        pre_mix_g=gain(ks[6], (DEPTH, D_MODEL)),
        post_mix_g=gain(ks[7], (DEPTH, D_MODEL)),
        pre_ffn_g=gain(ks[8], (DEPTH, D_MODEL)),
        post_ffn_g=gain(ks[9], (DEPTH, D_MODEL)),
        w_in=nrm(ks[10], (DEPTH, D_MODEL, PROJ_DIM), D_MODEL ** -0.5),
        conv_w=nrm(ks[11], (DEPTH, CONV_W, CONV_DIM), CONV_W ** -0.5),
        conv_b=nrm(ks[12], (DEPTH, CONV_DIM), 0.02),
        dt_bias=dt_bias,
        a_log=jnp.log(jax.random.uniform(ks[13], (DEPTH, SSM_HEADS), f32, 1.0, 16.0)),
        d_skip=gain(ks[15], (DEPTH, SSM_HEADS)),
        ssm_norm_g=gain(ks[16], (DEPTH, D_SSM)),
        ret_norm_g=gain(ks[17], (DEPTH, D_RET)),
        w_out=nrm(ks[18], (DEPTH, D_MIX, D_MODEL), D_MIX ** -0.5),
        w_gate=nrm(ks[19], (DEPTH, D_MODEL, D_FF), D_MODEL ** -0.5),
        w_up=nrm(ks[20], (DEPTH, D_MODEL, D_FF), D_MODEL ** -0.5),
        w_down=nrm(ks[21], (DEPTH, D_FF, D_MODEL), D_FF ** -0.5),
    )


def reference(x_prompt, x_sample, state_conv, state_ssm, state_ret, meta_tokens, pre_mix_g, post_mix_g,
              pre_ffn_g, post_ffn_g, w_in, conv_w, conv_b, dt_bias, a_log, d_skip, ssm_norm_g, ret_norm_g,
              w_out, w_gate, w_up, w_down):
    f32 = jnp.float32
    bp, seq = x_prompt.shape[:2]
    bs, dec_seq = x_sample.shape[:2]
    hp = jnp.concatenate([jnp.broadcast_to(meta_tokens[None].astype(x_prompt.dtype), (bp, N_META, D_MODEL)),
                          x_prompt], axis=1)
    pos_p = jnp.arange(N_META + seq, dtype=f32)
    seg_p = ((N_META, N_META), (seq, CHUNK))
    hs = x_sample
    pos_s = PAST_LEN + jnp.arange(dec_seq, dtype=f32)
    dec_chunk = CHUNK if dec_seq % CHUNK == 0 else dec_seq
    seg_s = ((dec_seq, dec_chunk),)
    p_conv, p_ssm, p_ret, s_conv, s_ssm, s_ret = [], [], [], [], [], []
    for i in range(DEPTH):
        lp = (pre_mix_g[i], post_mix_g[i], pre_ffn_g[i], post_ffn_g[i], w_in[i], conv_w[i], conv_b[i],
              dt_bias[i], a_log[i], d_skip[i], ssm_norm_g[i], ret_norm_g[i], w_out[i], w_gate[i], w_up[i], w_down[i])
        hp, c, s, r = layer(hp, pos_p,
                            jnp.zeros((bp, CONV_W - 1, CONV_DIM), f32),
                            jnp.zeros((bp, SSM_HEADS, SSM_HEADDIM, D_STATE), f32),
                            jnp.zeros((bp, RET_HEADS, RET_HEADDIM, RET_HEADDIM), f32),
                            seg_p, *lp)
        p_conv.append(c); p_ssm.append(s); p_ret.append(r)
        hs, c, s, r = layer(hs, pos_s, state_conv[i], state_ssm[i], state_ret[i], seg_s, *lp)
        s_conv.append(c); s_ssm.append(s); s_ret.append(r)
    y_prompt = hp[:, N_META:]
    y_sample = hs
    prompt_conv = jnp.stack(p_conv)
    prompt_ssm = jnp.stack(p_ssm)
    prompt_ret = jnp.stack(p_ret)
    sample_conv = jnp.stack(s_conv)
    sample_ssm = jnp.stack(s_ssm)
    sample_ret = jnp.stack(s_ret)
    return (y_prompt, y_sample, prompt_conv, prompt_ssm, prompt_ret, sample_conv, sample_ssm, sample_ret)
```

```python
import contextlib
import math
import numpy as np
import ml_dtypes
import concourse.bass as bass
import concourse.mybir as mybir
from concourse.bass_utils import run_bass_kernel_spmd

F32 = mybir.dt.float32
BF16 = mybir.dt.bfloat16
ALU = mybir.AluOpType
AF = mybir.ActivationFunctionType

D = 2048
NT = 9
TOK = 1040
DFF = 5632
EPS = 1e-6
GAM = [1.0 - 2.0 ** (-5 - h) for h in range(4)]
C_Z, C_X, C_B, C_C, C_DT, C_Q, C_K, C_V, C_G = 0, 1024, 2048, 2304, 2560, 2576, 3600, 4624, 5648


def rows(t):
    return 128 if t < 8 else 16


class Buf:
    __slots__ = ("name", "writers", "readers", "gen_deps")

    def __init__(self, name):
        self.name = name
        self.writers = []
        self.readers = []
        self.gen_deps = []


class Owner:
    __slots__ = ("name", "sem", "cnt")

    def __init__(self, name):
        self.name = name
        self.sem = None
        self.cnt = 0


class Op:
    __slots__ = ("eng", "fn", "idx", "deps", "dmadeps", "marked", "cum", "dma_sem", "dma_val")


class Sched:
    ENGS = ("pe", "act", "dve", "pool", "sp")
    CENGS = ("pe", "act", "dve", "pool")

    def __init__(self, nc, es):
        self.nc = nc
        self.es = es
        self.ops = {e: [] for e in self.ENGS}
        self.seen = {e: {f: -1 for f in self.ENGS} for e in self.ENGS}
        self.seen_dma = {e: {} for e in self.ENGS}
        self.nsem = 0
        self.esem = {e: self.new_sem("eng_" + e) for e in self.CENGS}
        self.owners = {}
        self.last_compute = {e: None for e in self.CENGS}

    def new_sem(self, name):
        self.nsem += 1
        return self.es.enter_context(self.nc.semaphore(f"{name}_{self.nsem}"))

    def owner(self, name):
        if name not in self.owners:
            self.owners[name] = Owner(name)
        return self.owners[name]

    def _mk(self, eng, fn):
        o = Op()
        o.eng = eng
        o.fn = fn
        o.idx = len(self.ops[eng])
        o.marked = False
        o.cum = 0
        o.dma_sem = None
        o.dma_val = 0
        o.deps = []
        o.dmadeps = []
        return o

    def op(self, eng, fn, reads=(), writes=(), pwrites=(), dma=None):
        if _HALT[0]:
            return None
        o = self._mk(eng, fn)
        raw = []
        for b in reads:
            raw.extend(b.writers)
        for b in writes:
            g = b.readers + b.writers
            raw.extend(g)
            b.gen_deps = g
        for b in pwrites:
            raw.extend(b.gen_deps)
        best = {}
        dmadeps = {}
        for d in raw:
            if d.dma_sem is not None:
                k = id(d.dma_sem)
                if k not in dmadeps or dmadeps[k][1] < d.dma_val:
                    dmadeps[k] = (d.dma_sem, d.dma_val)
            else:
                if d.eng == "pe" and eng == "pe" and dma is None:
                    continue
                if d.eng not in best or best[d.eng].idx < d.idx:
                    best[d.eng] = d
        for e, d in best.items():
            if self.seen[eng][e] >= d.idx:
                continue
            self.seen[eng][e] = d.idx
            o.deps.append(d)
        for k, (s, v) in dmadeps.items():
            if self.seen_dma[eng].get(k, 0) >= v:
                continue
            self.seen_dma[eng][k] = v
            o.dmadeps.append((s, v))
        if dma is not None:
            if dma.sem is None:
                dma.sem = self.new_sem("dma_" + dma.name)
            dma.cnt += 16
            o.dma_sem = dma.sem
            o.dma_val = dma.cnt
        elif eng in self.CENGS:
            self.last_compute[eng] = o
        for b in reads:
            b.readers.append(o)
        for b in writes:
            b.writers = [o]
            b.readers = []
        for b in pwrites:
            b.writers.append(o)
        self.ops[eng].append(o)
        return o

    def barrier(self, full=False):
        if _HALT[0]:
            return
        lasts = [o for o in self.last_compute.values() if o is not None]
        dms = [(w.sem, w.cnt) for w in self.owners.values() if w.sem is not None and w.cnt > 0 and (full or w.name not in ("w0", "w1"))]
        for e in self.ENGS:
            if e == "pool" and not full:
                continue
            o = self._mk(e, None)
            for d in lasts:
                if self.seen[e][d.eng] >= d.idx:
                    continue
                self.seen[e][d.eng] = d.idx
                o.deps.append(d)
            for (s, v) in dms:
                k = id(s)
                if self.seen_dma[e].get(k, 0) >= v:
                    continue
                self.seen_dma[e][k] = v
                o.dmadeps.append((s, v))
            self.ops[e].append(o)

    def finalize(self):
        self.barrier(full=True)
        for e in self.ENGS:
            for o in self.ops[e]:
                for d in o.deps:
                    d.marked = True
        for e in self.ENGS:
            c = 0
            for o in self.ops[e]:
                if o.marked and o.dma_sem is None:
                    c += 1
                o.cum = c

    def emit(self):
        nc = self.nc

        def replay(ename, e):
            for o in self.ops[ename]:
                for d in o.deps:
                    e.wait_ge(self.esem[d.eng], d.cum)
                for (s, v) in o.dmadeps:
                    e.wait_ge(s, v)
                if o.fn is None:
                    continue
                ins = o.fn(e)
                if o.dma_sem is not None:
                    ins.then_inc(o.dma_sem, 16)
                elif o.marked:
                    ins.then_inc(self.esem[ename], 1)

        with nc.Block() as block:
            @block.tensor
            def _(e):
                replay("pe", e)

            @block.scalar
            def _(e):
                replay("act", e)

            @block.vector
            def _(e):
                replay("dve", e)

            @block.gpsimd
            def _(e):
                replay("pool", e)

            @block.sync
            def _(e):
                replay("sp", e)


_UID = [0]


class Pool:
    def __init__(self, S, nc, es, name, shape, dtype, n, psum=False, owners=None):
        self.tiles = []
        _UID[0] += 1
        name = f"p_{name}_{_UID[0]}_"
        for i in range(n):
            if psum:
                t = es.enter_context(nc.psum_tensor(f"{name}{i}", shape, dtype))
            else:
                t = es.enter_context(nc.sbuf_tensor(f"{name}{i}", shape, dtype))
            ow = S.owner(owners[i]) if owners else None
            self.tiles.append((t, Buf(f"{name}{i}"), ow))
        self.i = 0

    def next(self):
        t = self.tiles[self.i % len(self.tiles)]
        self.i += 1
        return t


def bc(ap, axis, shape):
    return ap.unsqueeze(axis).broadcast_to(shape)


class _Stop(Exception):
    pass


_SUB = [99]


_HALT = [False]


def ckpt(k):
    if _SUB[0] == k:
        _HALT[0] = True


def build_program(dbg=False, stop=99):
    nc = bass.Bass("TRN2", target_bir_lowering=False)

    def din(name, shape, dt=F32):
        return nc.dram_tensor(name, shape, dt, kind="ExternalInput").ap()

    def dout(name, shape, dt=F32):
        return nc.dram_tensor(name, shape, dt, kind="ExternalOutput").ap()

    xw_d = din("xw", [TOK, D])
    xm_d = din("xm", [TOK, D])
    valid_d = din("validw", [128, NT])
    ropew_d = din("ropew", [128, NT * 256])
    ropem_d = din("ropem", [128, NT * 256])
    gamq_d = din("gamq", [128, 4])
    wk_d = din("wk", [128, 8])
    dp_d = din("dp", [128, 512])
    msk_d = din("msk", [128, 3 * 128])
    idf_d = din("idf", [128, 128])
    idb_d = din("idb", [128, 128], BF16)
    sel_d = din("sel", [16, 8 * 128])
    i16b_d = din("i16b", [128, 256])
    w_in_d = din("w_in", [D, 6672])
    w_out_d = din("w_out", [D, D])
    w_gate_d = din("w_gate", [D, DFF])
    w_up_d = din("w_up", [D, DFF])
    w_down_d = din("w_down", [DFF, D])
    g_premix_d = din("pre_mix_g", [1, D])
    g_postmix_d = din("post_mix_g", [1, D])
    g_preffn_d = din("pre_ffn_g", [1, D])
    g_postffn_d = din("post_ffn_g", [1, D])
    convw_d = din("conv_w", [4, 1536])
    convb_d = din("conv_b", [1, 1536])
    dtb_d = din("dt_bias", [1, 16])
    alog_d = din("a_log", [1, 16])
    dskip_d = din("d_skip", [1, 16])
    ssmg_d = din("ssm_norm_g", [1, 1024])
    retg_d = din("ret_norm_g", [1, 1024])
    sconv_d = din("st_conv", [16, 3, 1536])
    sssm_d = din("st_ssm", [16, 16, 64, 128])
    sret_d = din("st_ret", [16, 4, 256, 256])

    y_d = dout("y", [TOK, D])
    pconv_d = dout("p_conv", [3, 1536])
    pssm_d = dout("p_ssm", [16, 64, 128])
    pret_d = dout("p_ret", [4, 256, 256])
    oconv_d = dout("s_conv", [16, 3, 1536])
    ossm_d = dout("s_ssm", [16, 16, 64, 128])
    oret_d = dout("s_ret", [16, 4, 256, 256])
    h1_d = nc.dram_tensor("h1_scr", [TOK, D], F32).ap()
    dd_d = nc.dram_tensor("d_scr", [TOK, D], F32).ap()
    dbg_d = {}

    es = contextlib.ExitStack()
    with es:
        S = Sched(nc, es)

        def sb(st, name, shape, dt):
            _UID[0] += 1
            return st.enter_context(nc.sbuf_tensor(f"s_{name}_{_UID[0]}", shape, dt))

        identb = sb(es, "identb", [128, 128], BF16)
        identf = sb(es, "identf", [128, 128], F32)
        msk = sb(es, "msk", [128, 3, 128], F32)
        Umask, Lmask, ones = msk[:, 0, :], msk[:, 1, :], msk[:, 2, :]
        one1 = sb(es, "one1", [128, 1], F32)
        epsc = sb(es, "epsc", [128, 1], F32)
        valid = sb(es, "valid", [128, NT], F32)
        gamq = sb(es, "gamq", [128, 4], F32)
        wk = sb(es, "wk", [128, 8], F32)
        convw = sb(es, "convw", [128, 12, 4], F32)
        convb = sb(es, "convb", [128, 12], F32)
        dtb_b = sb(es, "dtb_b", [128, 16], F32)
        a_b = sb(es, "a_b", [128, 16], F32)
        dskip_b = sb(es, "dskip_b", [128, 16], F32)
        sel = sb(es, "sel", [16, 8, 128], F32)
        i16b = sb(es, "i16b", [128, 16, 16], F32)
        uT = sb(es, "uT", [128, 16, TOK], BF16)
        wp = Pool(S, nc, es, "wsl", [128, 16 * 512], BF16, 2, owners=["w0", "w1"])
        esm = contextlib.ExitStack()
        mixT = sb(esm, "mixT", [128, 16, TOK], BF16)
        b_uT = [Buf(f"uT{t}") for t in range(NT)]
        b_mixT = [Buf(f"mixT{t}") for t in range(NT)]
        b_const = Buf("const")
        oc = S.owner("const")

        def cload(dst, src):
            S.op("sp", lambda e, dst=dst, src=src: e.dma_start(out=dst, in_=src), pwrites=[b_const], dma=oc)

        cload(identb[:], idb_d)
        cload(identf[:], idf_d)
        cload(msk[:].rearrange("p a b -> p (a b)"), msk_d)
        cload(valid[:], valid_d)
        cload(sel[:].rearrange("p a b -> p (a b)"), sel_d)
        cload(i16b[:].rearrange("p a b -> p (a b)"), i16b_d)
        cload(gamq[:], gamq_d)
        cload(wk[:], wk_d)
        cload(dtb_b[:], dtb_d.partition_broadcast(128))
        cload(a_b[:], alog_d.partition_broadcast(128))
        cload(dskip_b[:], dskip_d.partition_broadcast(128))
        with nc.allow_non_contiguous_dma(reason="tiny conv param transposes"):
            for ct in range(12):
                cload(convw[:, ct, :], convw_d[:, ct * 128:(ct + 1) * 128].rearrange("i p -> p i"))
                cload(convb[:, ct:ct + 1], convb_d[:, ct * 128:(ct + 1) * 128].rearrange("i p -> p i"))
        S.op("dve", lambda e: e.memset(one1[:], 1.0), pwrites=[b_const])
        S.op("dve", lambda e: e.memset(epsc[:], EPS), pwrites=[b_const])
        S.barrier()
        S.op("act", lambda e: e.activation(out=a_b[:], in_=a_b[:], func=AF.Exp), pwrites=[b_const])
        S.barrier()
        S.op("dve", lambda e: e.tensor_scalar(out=a_b[:], in0=a_b[:], scalar1=-1.0, scalar2=None, op0=ALU.mult), pwrites=[b_const])
        S.barrier()

        pf = Pool(S, nc, es, "pf", [128, 512], F32, 5, psum=True)
        pacc = Pool(S, nc, es, "pacc", [128, 512], F32, 1, psum=True)
        pb = Pool(S, nc, es, "pb", [128, 8, 128], BF16, 2, psum=True)

        cnt = {"ev": 0}

        def evac_copy(out, in_, reads, writes=(), pwrites=()):
            cnt["ev"] += 1
            if cnt["ev"] % 2:
                S.op("act", lambda e: e.activation(out=out, in_=in_, func=AF.Copy), reads=reads, writes=writes, pwrites=pwrites)
            else:
                S.op("dve", lambda e: e.tensor_copy(out=out, in_=in_), reads=reads, writes=writes, pwrites=pwrites)

        def rstd_from_ss(ss, b_ss, n, inv_n):
            S.op("act", lambda e: e.activation(out=ss[:n], in_=ss[:n], func=AF.Ln, bias=epsc[:n], scale=inv_n), reads=[b_ss], writes=[b_ss])
            S.op("act", lambda e: e.activation(out=ss[:n], in_=ss[:n], func=AF.Exp, scale=-0.5), reads=[b_ss], writes=[b_ss])

        def transposes_to_T(src, b_src, n, nk, dstT, b_dst, first_dst, kbase, c0, ident=None):
            k = 0
            first = first_dst
            while k < nk:
                m = min(8, nk - k)
                pt, b_pt, _ = pb.next()
                for j in range(m):
                    S.op("pe", lambda e, pt=pt, j=j, kk=k + j: e.transpose(out=pt[:, j, :n], in_=src[:n, kk * 128:(kk + 1) * 128], identity=identb[:n, :n]),
                         reads=[b_src], writes=[b_pt] if j == 0 else [], pwrites=[] if j == 0 else [b_pt])
                evac_copy(dstT[:, kbase + k:kbase + k + m, c0:c0 + n], pt[:, 0:m, :n], reads=[b_pt],
                          writes=[b_dst] if first else [], pwrites=[] if first else [b_dst])
                first = False
                k += m

        wpool_state = {}

        def load_w(wp, pieces, kt):
            wt, b_w, ow = wp.next()
            first = True
            for (src, width, off, tot) in pieces:
                view = wt[:, 0:kt * tot].rearrange("p (k c) -> p k c", c=tot)
                S.op("pool", lambda e, view=view, src=src, off=off, width=width: e.dma_start(
                    out=view[:, :, off:off + width], in_=src.rearrange("(k p) c -> p k c", p=128)),
                    writes=[b_w] if first else [], pwrites=[] if first else [b_w], dma=ow)
                first = False
            tot = pieces[0][3]
            return wt[:, 0:kt * tot].rearrange("p (k c) -> p k c", c=tot), b_w

        _DEFER = []

        def flush_defer():
            while _DEFER:
                _DEFER.pop(0)()

        def proj_tok(actT, b_act, wv, b_w, ncols, tiles, evac, kt=16):
            for t in tiles:
                n = rows(t)
                ps, b_ps, _ = pf.next()
                for k in range(kt):
                    S.op("pe", lambda e, ps=ps, k=k, t=t, n=n: e.matmul(ps[:n, 0:ncols], lhsT=actT[:, k, t * 128:t * 128 + n], rhs=wv[:, k, 0:ncols], start=(k == 0), stop=(k == kt - 1)),
                         reads=[b_act[t], b_w], writes=[b_ps] if k == 0 else [], pwrites=[] if k == 0 else [b_ps])
                while _DEFER:
                    _DEFER.pop(0)()
                d = evac(t, n, ps, b_ps)
                if d is not None:
                    _DEFER.append(d)

        def proj_feat(actT, b_act, wv, b_w, ncoltiles, groups, evac, kt=16):
            for ct in range(ncoltiles):
                for (c0, n) in groups:
                    ps, b_ps, _ = pf.next()
                    tl = sorted(set(range(c0 // 128, (c0 + n - 1) // 128 + 1)))
                    for k in range(kt):
                        S.op("pe", lambda e, ps=ps, k=k, ct=ct, c0=c0, n=n: e.matmul(ps[:, 0:n], lhsT=wv[:, k, ct * 128:(ct + 1) * 128], rhs=actT[:, k, c0:c0 + n], start=(k == 0), stop=(k == kt - 1)),
                             reads=[b_act[t] for t in tl] + [b_w], writes=[b_ps] if k == 0 else [], pwrites=[] if k == 0 else [b_ps])
                    evac(ct, c0, n, ps, b_ps)

        def norm_phase(x_dram, gain_dram, dstT, b_dst):
            with contextlib.ExitStack() as ph:
                xs_p = Pool(S, nc, ph, "xs", [128, D], F32, 4, owners=["xs0", "xs1", "xs2", "xs3"])
                u_p = Pool(S, nc, ph, "ub", [128, D], BF16, 3)
                ss_p = Pool(S, nc, ph, "ss", [128, 1], F32, 4)
                junk = sb(ph, "junk", [128, D], BF16)
                b_junk = Buf("junk")
                gt = sb(ph, "gt", [128, D], F32)
                b_gt = Buf("gt")
                S.op("sp", lambda e: e.dma_start(out=gt[:], in_=gain_dram.partition_broadcast(128)), writes=[b_gt], dma=S.owner("gt0"))
                def norm_tile(t):
                    n = rows(t)
                    xs, b_xs, ow = xs_p.next()
                    S.op("sp", lambda e, xs=xs, t=t, n=n: e.dma_start(out=xs[:n], in_=x_dram[t * 128:t * 128 + n, :]), writes=[b_xs], dma=ow)
                    ss, b_ss, _ = ss_p.next()
                    S.op("dve", lambda e, ss=ss: e.memset(ss[:], 0.0), writes=[b_ss])
                    S.op("act", lambda e, xs=xs, ss=ss, n=n: e.activation(out=junk[:n], in_=xs[:n], func=AF.Square, accum_out=ss[:n]),
                         reads=[b_xs, b_ss], writes=[b_junk, b_ss])
                    rstd_from_ss(ss, b_ss, n, 1.0 / D)
                    u, b_u, _ = u_p.next()
                    S.op("dve", lambda e, u=u, xs=xs, ss=ss, n=n: e.scalar_tensor_tensor(out=u[:n], in0=xs[:n], scalar=ss[:n, 0:1], in1=gt[:n], op0=ALU.mult, op1=ALU.mult),
                         reads=[b_xs, b_ss, b_gt], writes=[b_u])
                    yield
                    transposes_to_T(u, b_u, n, 16, dstT, b_dst[t], True, 0, t * 128)
                    yield
                gens = [norm_tile(t) for t in range(NT)]
                next(gens[0])
                for t in range(NT):
                    if t + 1 < NT:
                        next(gens[t + 1])
                    next(gens[t])
                S.barrier()

        def mixer_pass(full, st):
            hT, hTb, b_hT, b_hTb = st["hT"], st["hTb"], st["b_hT"], st["b_hTb"]
            Sst, Sb, b_S, b_Sb = st["S"], st["Sb"], st["b_S"], st["b_Sb"]
            convtail, b_ct = st["convtail"], st["b_ct"]
            ptail, b_pt_ = st["ptail"], st["b_ptail"]
            chunk_tiles = list(range(8)) if full else list(range(9))
            NSEQ = 1024 if full else 1040
            with contextlib.ExitStack() as ph:
                pssd = contextlib.ExitStack()
                dt_all = sb(pssd, "dt_all", [128, NT, 16], F32)
                dta_all = sb(pssd, "dta_all", [128, NT, 16], F32)
                exps = sb(pssd, "exps", [128, NT, 48], F32)
                b_dt = [Buf(f"dt{t}") for t in range(NT)]
                b_ex = [Buf(f"ex{t}") for t in range(NT)]
                tmp16_p = Pool(S, nc, pssd, "tmp16", [128, 16], F32, 2)

                wv, b_w = load_w(wp, [(w_in_d[:, C_DT:C_DT + 16], 16, 0, 16)], 16)

                def evac_dt(t, n, ps, b_ps):
                    tm, b_tm, _ = tmp16_p.next()
                    S.op("dve", lambda e: e.tensor_tensor(out=tm[:n], in0=ps[:n, 0:16], in1=dtb_b[:n], op=ALU.add), reads=[b_ps], writes=[b_tm])
                    S.op("act", lambda e: e.activation(out=tm[:n], in_=tm[:n], func=AF.Exp), reads=[b_tm], writes=[b_tm])
                    S.op("act", lambda e: e.activation(out=dt_all[:n, t, :], in_=tm[:n], func=AF.Ln, bias=one1[:n]), reads=[b_tm], writes=[b_dt[t]])
                    if not full:
                        S.op("dve", lambda e: e.tensor_scalar(out=dt_all[:n, t, :], in0=dt_all[:n, t, :], scalar1=valid[:n, t:t + 1], scalar2=None, op0=ALU.mult),
                             reads=[b_dt[t]], writes=[b_dt[t]])
                    S.op("dve", lambda e: e.tensor_tensor(out=dta_all[:n, t, :], in0=dt_all[:n, t, :], in1=a_b[:n], op=ALU.mult), reads=[b_dt[t]], writes=[b_dt[t]])
                    if t in chunk_tiles:
                        def deferred():
                            pe_, b_pe, _ = pf.next()
                            if full:
                                S.op("pe", lambda e: e.matmul(pe_[:n, 0:16], lhsT=Umask[:n, :n], rhs=dta_all[:n, t, :], start=True, stop=True), reads=[b_dt[t]], writes=[b_pe])
                            S.op("pe", lambda e: e.matmul(pe_[:n, 16:32], lhsT=Lmask[:n, :n], rhs=dta_all[:n, t, :], start=True, stop=True), reads=[b_dt[t]],
                                 writes=[] if full else [b_pe], pwrites=[b_pe] if full else [])
                            S.op("pe", lambda e: e.matmul(pe_[:, 32:48], lhsT=ones[:n, :], rhs=dta_all[:n, t, :], start=True, stop=True), reads=[b_dt[t]], pwrites=[b_pe])
                            lo = 0 if full else 16
                            S.op("act", lambda e: e.activation(out=exps[:n, t, lo:32], in_=pe_[:n, lo:32], func=AF.Exp), reads=[b_pe], writes=[b_ex[t]])
                            S.op("act", lambda e: e.activation(out=exps[:, t, 32:48], in_=pe_[:, 32:48], func=AF.Exp), reads=[b_pe], pwrites=[b_ex[t]])
                        return deferred
                    return None

                proj_tok(uT, b_uT, wv, b_w, 16, list(range(NT)), evac_dt)
                flush_defer()
                ckpt(1)
                if full:
                    t_p = Pool(S, nc, pssd, "tp", [128, 512], F32, 2)
                    mx_p = Pool(S, nc, pssd, "mxp", [128, 512], BF16, 2)
                    ss_p = Pool(S, nc, pssd, "ssg", [128, 1], F32, 2)
                    junk = sb(pssd, "junkg", [128, 512], BF16)
                    b_junk = Buf("junkg")
                    dI = sb(pssd, "dI", [128, 16, 128], BF16)
                    b_dI = Buf("dI")
                    for hh_ in range(16):
                        S.op("dve", lambda e, hh_=hh_: e.tensor_scalar(out=dI[:, hh_, :], in0=identb[:, :], scalar1=dskip_b[:, hh_:hh_ + 1], scalar2=None, op0=ALU.mult),
                             writes=[b_dI] if hh_ == 0 else [], pwrites=[] if hh_ == 0 else [b_dI])
                    szs = sb(pssd, "szs", [16, 2, 512], F32)
                    b_szs = [Buf("szs0"), Buf("szs1")]
                    histT = sb(pssd, "histT", [128, 12, 48], F32)
                    b_histT = Buf("histT")
                    xnew = sb(pssd, "xnew", [128, 12, 16], F32)
                    xbcs = sb(pssd, "xbcs", [128, 12, 16], F32)
                    b_xnew = [Buf(f"xnew{i}") for i in range(12)]
                    b_xbcs = [Buf(f"xbcs{i}") for i in range(12)]
                    decq = sb(pssd, "decq", [128, 8, 16], F32)
                    b_decq = Buf("decq")
                    dtaT = sb(pssd, "dtaT", [16, 16], F32)
                    b_dtaT = Buf("dtaT")
                    accs_p = Pool(S, nc, pssd, "accs", [128, 16], F32, 2)
                    hist_scope = contextlib.ExitStack()
                    hist_tok = sb(hist_scope, "hist_tok", [48, 1536], F32)
                    b_hist = Buf("hist_tok")
                    S.op("sp", lambda e: e.dma_start(out=hist_tok[:], in_=sconv_d.rearrange("t i c -> (t i) c")), writes=[b_hist], dma=S.owner("gt0"))
                    S.op("sp", lambda e: e.dma_start(out=oconv_d[:, 0:2, :], in_=sconv_d[:, 1:3, :]), dma=S.owner("st0"))
                    for q4 in range(3):
                        psh, b_psh, _ = pf.next()
                        for j in range(4):
                            ct = 4 * q4 + j
                            S.op("pe", lambda e, psh=psh, ct=ct, j=j: e.transpose(out=psh[:, 48 * j:48 * j + 48], in_=hist_tok[:48, ct * 128:(ct + 1) * 128], identity=identf[:48, :48]),
                                 reads=[b_hist], writes=[b_psh] if j == 0 else [], pwrites=[] if j == 0 else [b_psh])
                        S.op("dve", lambda e, psh=psh, q4=q4: e.tensor_copy(out=histT[:, 4 * q4:4 * q4 + 4, :], in_=psh[:, 0:192].rearrange("p (a b) -> p a b", b=48)), reads=[b_psh],
                             writes=[b_histT] if q4 == 0 else [], pwrites=[] if q4 == 0 else [b_histT])
                    psd, b_psd, _ = pf.next()
                    S.op("pe", lambda e: e.transpose(out=psd[:16, 0:16], in_=dta_all[:16, 8, :], identity=identf[:16, :16]), reads=[b_dt[8]], writes=[b_psd])
                    S.op("dve", lambda e: e.tensor_copy(out=dtaT[:, :], in_=psd[:16, 0:16]), reads=[b_psd], writes=[b_dtaT])
                    psq, b_psq, _ = pf.next()
                    for hp in range(8):
                        S.op("pe", lambda e, hp=hp: e.matmul(psq[:, 16 * hp:16 * hp + 16], lhsT=sel[:16, hp, :], rhs=dtaT[:16, :16], start=True, stop=True), reads=[b_dtaT],
                             writes=[b_psq] if hp == 0 else [], pwrites=[] if hp == 0 else [b_psq])
                    S.op("act", lambda e: e.activation(out=decq[:].rearrange("p a b -> p (a b)"), in_=psq[:, 0:128], func=AF.Exp), reads=[b_psq], writes=[b_decq])
                    S.barrier()
                    hist_scope.close()

                def sample_ssd(g, pg, gate_norm_ssd):
                    xs_s = sb(pg, "xs_s", [16, 768], F32)
                    b_xs_s = Buf("xs_s")
                    xdt_s = sb(pg, "xdt_s", [16, 512], BF16)
                    b_xdt_s = Buf("xdt_s")
                    Bsel = sb(pg, "Bsel", [16, 16, 128], BF16)
                    Csel = sb(pg, "Csel", [16, 16, 128], F32)
                    b_Bsel, b_Csel = Buf("Bsel"), Buf("Csel")
                    Cb = sb(pg, "Cb", [128, 16, 128], F32)
                    b_Cb = Buf("Cb")
                    ycol = sb(pg, "ycol", [128, 4, 16], F32)
                    b_ycol = Buf("ycol")
                    y_s = sb(pg, "y_s", [16, 512], F32)
                    b_y_s = Buf("y_s")
                    junk = sb(pg, "junks", [128, 128], F32)
                    b_junk = Buf("junks")
                    hb_p = Pool(S, nc, pg, "hb", [128, 8, 4, 128], F32, 2, owners=["sa0", "sa1"])
                    ps1, b_ps1, _ = pf.next()
                    for j in range(4):
                        S.op("pe", lambda e, j=j: e.transpose(out=ps1[:16, 128 * j:128 * j + 128], in_=xbcs[:, 4 * g + j, :], identity=identf[:, :]), reads=[b_xbcs[4 * g + j]],
                             writes=[b_ps1] if j == 0 else [], pwrites=[] if j == 0 else [b_ps1])
                    S.op("dve", lambda e: e.tensor_copy(out=xs_s[:, 0:512], in_=ps1[:16, :]), reads=[b_ps1], writes=[b_xs_s])
                    ps2, b_ps2, _ = pf.next()
                    S.op("pe", lambda e: e.transpose(out=ps2[:16, 0:128], in_=xbcs[:, 8 + g, :], identity=identf[:, :]), reads=[b_xbcs[8 + g]], writes=[b_ps2])
                    S.op("pe", lambda e: e.transpose(out=ps2[:16, 128:256], in_=xbcs[:, 10 + g, :], identity=identf[:, :]), reads=[b_xbcs[10 + g]], pwrites=[b_ps2])
                    S.op("dve", lambda e: e.tensor_copy(out=xs_s[:, 512:768], in_=ps2[:16, 0:256]), reads=[b_ps2], pwrites=[b_xs_s])
                    S.op("dve", lambda e: e.tensor_tensor(out=xdt_s[:, :].rearrange("p (h q) -> p h q", q=64), in0=xs_s[:, 0:512].rearrange("p (h q) -> p h q", q=64),
                                                          in1=bc(dt_all[:16, 8, 8 * g:8 * g + 8], 2, [16, 8, 64]), op=ALU.mult), reads=[b_xs_s, b_dt[8]], writes=[b_xdt_s])
                    S.op("dve", lambda e: e.tensor_tensor(out=Bsel[:, :, :], in0=bc(identf[:16, :16], 2, [16, 16, 128]), in1=bc(xs_s[:, 512:640], 1, [16, 16, 128]), op=ALU.mult), reads=[b_xs_s], writes=[b_Bsel])
                    S.op("dve", lambda e: e.tensor_tensor(out=Csel[:, :, :], in0=bc(identf[:16, :16], 2, [16, 16, 128]), in1=bc(xs_s[:, 640:768], 1, [16, 16, 128]), op=ALU.mult), reads=[b_xs_s], writes=[b_Csel])
                    for j in range(4):
                        psc_, b_psc_, _ = pf.next()
                        S.op("pe", lambda e, j=j, psc_=psc_: e.matmul(psc_[:, :].rearrange("p (a b) -> p a b", b=128), lhsT=ones[:16, :], rhs=Csel[:16, 4 * j:4 * j + 4, :], start=True, stop=True), reads=[b_Csel], writes=[b_psc_])
                        S.op("dve", lambda e, j=j, psc_=psc_: e.tensor_copy(out=Cb[:, 4 * j:4 * j + 4, :], in_=psc_[:, :].rearrange("p (a b) -> p a b", b=128)), reads=[b_psc_],
                             writes=[b_Cb] if j == 0 else [], pwrites=[] if j == 0 else [b_Cb])

                    def bload(bi):
                        hb, b_hb, ow = hb_p.next()
                        for tl in range(8):
                            S.op("sp", lambda e, tl=tl: e.dma_start(out=hb[:, tl], in_=sssm_d[8 * bi + tl, 8 * g:8 * g + 8].rearrange("(hp h2) p n -> (h2 p) hp n", h2=2)),
                                 writes=[b_hb] if tl == 0 else [], pwrites=[] if tl == 0 else [b_hb], dma=ow)
                        return hb, b_hb, ow
                    loaded = [bload(0), bload(1)]

                    def batch(bi):
                        hb, b_hb, ow = loaded[bi]

                        def tok(tl):
                            t = 8 * bi + tl
                            for hp in range(4):
                                pu, b_pu, _ = pf.next()
                                S.op("pe", lambda e, hp=hp, pu=pu: e.matmul(pu[:, 0:128], lhsT=xdt_s[:16, 128 * hp:128 * hp + 128], rhs=Bsel[:16, t, :], start=True, stop=True), reads=[b_xdt_s, b_Bsel], writes=[b_pu])
                                S.op("dve", lambda e, hp=hp, pu=pu: e.scalar_tensor_tensor(out=hb[:, tl, hp, :], in0=hb[:, tl, hp, :], scalar=decq[:, 4 * g + hp, t:t + 1], in1=pu[:, 0:128], op0=ALU.mult, op1=ALU.add),
                                     reads=[b_hb, b_decq, b_pu], pwrites=[b_hb])
                                S.op("dve", lambda e, hp=hp: e.scalar_tensor_tensor(out=junk[:, :], in0=hb[:, tl, hp, :], scalar=1.0, in1=Cb[:, t, :], op0=ALU.mult, op1=ALU.mult, accum_out=ycol[:, hp, t:t + 1]),
                                     reads=[b_hb, b_Cb], writes=[b_junk], pwrites=[b_ycol])
                        for tl in range(8):
                            tok(tl)
                        for tl in range(8):
                            S.op("sp", lambda e, tl=tl: e.dma_start(out=ossm_d[8 * bi + tl, 8 * g:8 * g + 8].rearrange("(hp h2) p n -> (h2 p) hp n", h2=2), in_=hb[:, tl]), reads=[b_hb], dma=ow)
                    for bi in range(2):
                        batch(bi)
                    psy, b_psy, _ = pf.next()
                    for hp in range(4):
                        S.op("pe", lambda e, hp=hp: e.transpose(out=psy[:16, 128 * hp:128 * hp + 128], in_=ycol[:, hp, :], identity=identf[:, :]), reads=[b_ycol],
                             writes=[b_psy] if hp == 0 else [], pwrites=[] if hp == 0 else [b_psy])
                    S.op("dve", lambda e: e.tensor_copy(out=y_s[:, :], in_=psy[:16, :]), reads=[b_psy], writes=[b_y_s])
                    ssmg_s = sb(pg, "ssmg_s", [16, 512], F32)
                    b_ssmg_s = Buf("ssmg_s")
                    S.op("sp", lambda e: e.dma_start(out=ssmg_s[:], in_=ssmg_d[:, 512 * g:512 * g + 512].partition_broadcast(16)), writes=[b_ssmg_s], dma=S.owner("gt0"))
                    gate_norm_ssd(8, 16, y_s, b_y_s, xs_s[:, 0:512], b_xs_s, g=g, szt=szs[:, g, :], b_szt=b_szs[g], ssmg=ssmg_s, b_ssmg=b_ssmg_s)

                def ssd_group(g):
                    with contextlib.ExitStack() as pg:
                        xbcg = sb(pg, "xbcg", [128, 6, TOK], BF16)
                        b_xbc = [Buf(f"xbc{i}") for i in range(6)]
                        pxc = contextlib.ExitStack()
                        pre_p = Pool(S, nc, pxc, "pre", [128, TOK + 3], F32, 2)
                        acc_p = Pool(S, nc, pxc, "acc", [128, TOK], F32, 2)
                        pre_cur = {}

                        def evac_xbc_factory(ctglob_list, slot_list):
                            def evac(ct, c0, n, ps, b_ps):
                                ctg = ctglob_list[ct]
                                sl = slot_list[ct]
                                if c0 == 0:
                                    pre, b_pre, _ = pre_p.next()
                                    pre_cur[ct] = (pre, b_pre)
                                    if full:
                                        S.op("dve", lambda e: e.tensor_copy(out=pre[:, 0:3], in_=convtail[:, ctg, :]), reads=[b_ct], writes=[b_pre])
                                    else:
                                        S.op("dve", lambda e: e.memset(pre[:, 0:3], 0.0), writes=[b_pre])
                                pre, b_pre = pre_cur[ct]
                                S.op("act", lambda e: e.activation(out=pre[:, 3 + c0:3 + c0 + n], in_=ps[:, 0:n], func=AF.Copy), reads=[b_ps], pwrites=[b_pre])
                                if c0 + n == TOK:
                                    N = NSEQ
                                    acc, b_acc, _ = acc_p.next()
                                    S.op("dve", lambda e: e.tensor_scalar(out=acc[:, 0:N], in0=pre[:, 0:N], scalar1=convw[:, ctg, 0:1], scalar2=convb[:, ctg:ctg + 1], op0=ALU.mult, op1=ALU.add),
                                         reads=[b_pre], writes=[b_acc])
                                    for i in range(1, 4):
                                        S.op("dve", lambda e, i=i: e.scalar_tensor_tensor(out=acc[:, 0:N], in0=pre[:, i:i + N], scalar=convw[:, ctg, i:i + 1], in1=acc[:, 0:N], op0=ALU.mult, op1=ALU.add),
                                             reads=[b_pre, b_acc], writes=[b_acc])
                                    S.op("act", lambda e: e.activation(out=xbcg[:, sl, 0:N], in_=acc[:, 0:N], func=AF.Silu), reads=[b_acc], writes=[b_xbc[sl]])
                                    if full:
                                        S.op("dve", lambda e: e.tensor_copy(out=ptail[:, ctg, :], in_=pre[:, N:N + 3]), reads=[b_pre], pwrites=[b_pt_])
                                        xn = pre[:, 1027:1043]
                                        S.op("dve", lambda e: e.tensor_copy(out=xnew[:, ctg, :], in_=xn), reads=[b_pre], writes=[b_xnew[ctg]])
                                        accs, b_accs, _ = accs_p.next()
                                        S.op("dve", lambda e: e.tensor_scalar(out=accs[:, :], in0=xn, scalar1=convw[:, ctg, 3:4], scalar2=convb[:, ctg:ctg + 1], op0=ALU.mult, op1=ALU.add),
                                             reads=[b_pre], writes=[b_accs])
                                        for i in range(3):
                                            S.op("dve", lambda e, i=i: e.scalar_tensor_tensor(out=accs[:, :], in0=histT[:, ctg, :].rearrange("p (t i) -> p t i", i=3)[:, :, i], scalar=convw[:, ctg, i:i + 1],
                                                                                             in1=accs[:, :], op0=ALU.mult, op1=ALU.add), reads=[b_histT, b_accs], writes=[b_accs])
                                        S.op("act", lambda e: e.activation(out=xbcs[:, ctg, :], in_=accs[:, :], func=AF.Silu), reads=[b_accs], writes=[b_xbcs[ctg]])
                                    else:
                                        S.op("dve", lambda e: e.tensor_copy(out=convtail[:, ctg, :], in_=pre[:, N:N + 3]), reads=[b_pre], pwrites=[b_ct])
                            return evac

                        groups = [(0, 352), (352, 352), (704, 336)]
                        wv, b_w = load_w(wp, [(w_in_d[:, C_X + 512 * g:C_X + 512 * g + 512], 512, 0, 512)], 16)
                        proj_feat(uT, b_uT, wv, b_w, 4, groups, evac_xbc_factory([4 * g + i for i in range(4)], [0, 1, 2, 3]))
                        if full:
                            wv, b_w = load_w(wp, [(w_in_d[:, C_B + 128 * g:C_B + 128 * g + 128], 128, 0, 256), (w_in_d[:, C_C + 128 * g:C_C + 128 * g + 128], 128, 128, 256)], 16)
                            proj_feat(uT, b_uT, wv, b_w, 2, groups, evac_xbc_factory([8 + g, 10 + g], [4, 5]))
                        else:
                            wv, b_w = load_w(wp, [(w_in_d[:, C_B + 128 * g:C_B + 128 * g + 128], 128, 0, 256), (w_in_d[:, C_C + 128 * g:C_C + 128 * g + 128], 128, 128, 256)], 16)
                            proj_feat(uT, b_uT, wv, b_w, 1, groups, evac_xbc_factory([8 + g], [4]))
                            psc_, b_psc_, _ = pf.next()
                            for k in range(16):
                                S.op("pe", lambda e, k=k: e.matmul(psc_[:, 0:16], lhsT=wv[:, k, 128:256], rhs=uT[:, k, 1024:1040], start=(k == 0), stop=(k == 15)),
                                     reads=[b_uT[8], b_w], writes=[b_psc_] if k == 0 else [], pwrites=[] if k == 0 else [b_psc_])
                            S.op("act", lambda e: e.activation(out=convtail[:, 10 + g, :], in_=psc_[:, 13:16], func=AF.Copy), reads=[b_psc_], pwrites=[b_ct])
                        ckpt(2)
                        S.barrier()
                        pxc.close()
                        sz = None
                        if full:
                            sz = sb(pg, "sz", [128, NT, 512], F32)
                            b_sz = [Buf(f"sz{t}") for t in range(NT)]
                            ssmg_c = sb(pg, "ssmg", [128, 512], F32)
                            b_ssmg_c = Buf("ssmg")
                            S.op("sp", lambda e: e.dma_start(out=ssmg_c[:], in_=ssmg_d[:, 512 * g:512 * g + 512].partition_broadcast(128)), writes=[b_ssmg_c], dma=S.owner("gt0"))
                            wv, b_w = load_w(wp, [(w_in_d[:, C_Z + 512 * g:C_Z + 512 * g + 512], 512, 0, 512)], 16)

                            def evac_z(t, n, ps, b_ps):
                                S.op("act", lambda e: e.activation(out=sz[:n, t, :], in_=ps[:n, :], func=AF.Silu), reads=[b_ps], writes=[b_sz[t]])
                                if t == 8:
                                    S.op("act", lambda e: e.activation(out=szs[:16, g, :], in_=ps[:16, :], func=AF.Silu), reads=[b_ps], writes=[b_szs[g]])
                            proj_tok(uT, b_uT, wv, b_w, 512, list(range(NT)), evac_z)

                        xtok_p = Pool(S, nc, pg, "xtok", [128, 512], BF16, 3)
                        xdt_p = Pool(S, nc, pg, "xdt", [128, 512], BF16, 2)
                        xw_p = Pool(S, nc, pg, "xwp", [128, 512], BF16, 2)
                        bm_p = Pool(S, nc, pg, "bmt", [128, 128], BF16, 2)
                        if full:
                            R_p = Pool(S, nc, pg, "Rp", [128, 8, 128], F32, 1)
                            es_p = Pool(S, nc, pg, "esg", [128, 8, 128], F32, 1)
                            wT_p = Pool(S, nc, pg, "wTp", [128, 8, 128], BF16, 2)
                            cb_p = Pool(S, nc, pg, "cbp", [128, 128], F32, 2)
                            y_p = Pool(S, nc, pg, "yp", [128, 512], F32, 2)

                        def gate_norm_ssd(t, n, y, b_y, xtok, b_xtok, g=g, szt=None, b_szt=None, ssmg=None, b_ssmg=None):
                            folded = szt is None
                            if szt is None:
                                szt, b_szt, ssmg, b_ssmg = sz[:, t, :], b_sz[t], ssmg_c, b_ssmg_c
                            if folded:
                                return gate_norm_tail(t, n, y, b_y, g, szt, b_szt, ssmg, b_ssmg)
                            tt, b_tt, _ = t_p.next()
                            S.op("pool", lambda e: e.tensor_tensor(out=tt[:n].rearrange("p (h q) -> p h q", q=64), in0=xtok[:n].rearrange("p (h q) -> p h q", q=64),
                                                                   in1=bc(dskip_b[:n, 8 * g:8 * g + 8], 2, [n, 8, 64]), op=ALU.mult), reads=[b_xtok], writes=[b_tt])
                            S.op("dve", lambda e: e.tensor_tensor(out=y[:n], in0=y[:n], in1=tt[:n], op=ALU.add), reads=[b_y, b_tt], writes=[b_y])
                            gate_norm_tail(t, n, y, b_y, g, szt, b_szt, ssmg, b_ssmg)

                        def gate_norm_tail(t, n, y, b_y, g, szt, b_szt, ssmg, b_ssmg):
                            S.op("dve", lambda e: e.tensor_tensor(out=y[:n], in0=y[:n], in1=szt[:n], op=ALU.mult), reads=[b_y, b_szt], writes=[b_y])
                            ss, b_ss, _ = ss_p.next()
                            S.op("dve", lambda e: e.memset(ss[:], 0.0), writes=[b_ss])
                            S.op("act", lambda e: e.activation(out=junk[:n], in_=y[:n], func=AF.Square, accum_out=ss[:n]), reads=[b_y, b_ss], writes=[b_junk, b_ss])
                            rstd_from_ss(ss, b_ss, n, 1.0 / 512)
                            mx, b_mx, _ = mx_p.next()
                            S.op("dve", lambda e: e.scalar_tensor_tensor(out=mx[:n], in0=y[:n], scalar=ss[:n, 0:1], in1=ssmg[:n], op0=ALU.mult, op1=ALU.mult),
                                 reads=[b_y, b_ss, b_ssmg], writes=[b_mx])
                            transposes_to_T(mx, b_mx, n, 4, mixT, b_mixT[t], (g == 0), 4 * g, t * 128)

                        def ssd_chunk(t):
                            c = rows(t)
                            c0 = t * 128
                            pt, b_ptt, _ = pb.next()
                            for j in range(5):
                                S.op("pe", lambda e, j=j: e.transpose(out=pt[:c, j, :], in_=xbcg[:, j, c0:c0 + c], identity=identb[:, :]),
                                     reads=[b_xbc[j]], writes=[b_ptt] if j == 0 else [], pwrites=[] if j == 0 else [b_ptt])
                            xtok = b_xtok = None
                            if full:
                                xtok, b_xtok, _ = xtok_p.next()
                                S.op("dve", lambda e: e.tensor_copy(out=xtok[:c].rearrange("p (a b) -> p a b", b=128), in_=pt[:c, 0:4, :]), reads=[b_ptt], writes=[b_xtok])
                            xdt, b_xdt, _ = xdt_p.next()
                            S.op("dve", lambda e: e.tensor_tensor(out=xdt[:c].rearrange("p (h q) -> p h q", q=64), in0=pt[:c, 0:4, :].rearrange("p a (h2 q) -> p (a h2) q", q=64),
                                                                  in1=bc(dt_all[:c, t, 8 * g:8 * g + 8], 2, [c, 8, 64]), op=ALU.mult), reads=[b_ptt, b_dt[t]], writes=[b_xdt])
                            ckpt(30)
                            bmt, b_bmt, _ = bm_p.next()
                            S.op("dve", lambda e: e.tensor_copy(out=bmt[:c], in_=pt[:c, 4, :]), reads=[b_ptt], writes=[b_bmt])
                            ckpt(300)
                            xwt, b_xwt, _ = xw_p.next()
                            S.op("dve", lambda e: e.tensor_tensor(out=xwt[:c].rearrange("p (h q) -> p h q", q=64), in0=xdt[:c].rearrange("p (h q) -> p h q", q=64),
                                                                  in1=bc(exps[:c, t, 16 + 8 * g:16 + 8 * g + 8], 2, [c, 8, 64]), op=ALU.mult), reads=[b_xdt, b_ex[t]], writes=[b_xwt])
                            if full:
                                pc, b_pc, _ = pf.next()
                                S.op("pe", lambda e: e.matmul(pc[:c, 0:c], lhsT=xbcg[:, 4, c0:c0 + c], rhs=xbcg[:, 5, c0:c0 + c], start=True, stop=True), reads=[b_xbc[4], b_xbc[5]], writes=[b_pc])
                                cb, b_cb, _ = cb_p.next()
                                S.op("dve", lambda e: e.tensor_tensor(out=cb[:c, :c], in0=pc[:c, 0:c], in1=Umask[:c, :c], op=ALU.mult), reads=[b_pc], writes=[b_cb])
                                R, b_R, _ = R_p.next()
                                S.op("pool", lambda e: e.tensor_tensor(out=R[:c, :, :c], in0=bc(Umask[:c, :c], 1, [c, 8, c]), in1=bc(dta_all[:c, t, 8 * g:8 * g + 8], 2, [c, 8, c]), op=ALU.mult),
                                     reads=[b_dt[t]], writes=[b_R])
                                esg, b_esg, _ = es_p.next()
                                for hh in range(2):
                                    psg, b_psg, _ = pf.next()
                                    S.op("pe", lambda e, hh=hh, psg=psg: e.matmul(psg[:c, 0:4 * c].rearrange("p (h i) -> p h i", i=c), lhsT=Lmask[:c, :c], rhs=R[:c, 4 * hh:4 * hh + 4, :c], start=True, stop=True),
                                         reads=[b_R], writes=[b_psg])
                                    S.op("act", lambda e, hh=hh, psg=psg: e.activation(out=esg[:c, 4 * hh:4 * hh + 4, :c], in_=psg[:c, 0:4 * c].rearrange("p (h i) -> p h i", i=c), func=AF.Exp),
                                         reads=[b_psg], writes=[b_esg] if hh == 0 else [], pwrites=[] if hh == 0 else [b_esg])
                            yield
                            if full:
                                wT, b_wT, _ = wT_p.next()
                                S.op("dve", lambda e: e.tensor_tensor(out=wT[:c, :, :c], in0=esg[:c, :, :c], in1=bc(cb[:c, :c], 1, [c, 8, c]), op=ALU.mult), reads=[b_esg, b_cb], writes=[b_wT])
                            yield
                            if full:
                                pya, b_pya, _ = pf.next()
                                for h in range(8):
                                    S.op("pe", lambda e, h=h: e.matmul(pya[:c, 64 * h:64 * h + 64], lhsT=wT[:c, h, :c], rhs=xdt[:c, 64 * h:64 * h + 64], start=True, stop=False),
                                         reads=[b_wT, b_xdt], writes=[b_pya] if h == 0 else [], pwrites=[] if h == 0 else [b_pya])
                                    S.op("pe", lambda e, h=h: e.matmul(pya[:c, 64 * h:64 * h + 64], lhsT=dI[:c, 8 * g + h, :c], rhs=xtok[:c, 64 * h:64 * h + 64], start=False, stop=True),
                                         reads=[b_dI, b_xtok], pwrites=[b_pya])
                                pyb, b_pyb, _ = pf.next()
                                S.op("pe", lambda e: e.matmul(pyb[:c, :], lhsT=xbcg[:, 5, c0:c0 + c], rhs=hTb[:, g, :], start=True, stop=True), reads=[b_xbc[5], b_hTb[g]], writes=[b_pyb])
                                tt, b_tt, _ = t_p.next()
                                S.op("dve", lambda e: e.tensor_tensor(out=tt[:c].rearrange("p (h q) -> p h q", q=64), in0=pyb[:c, :].rearrange("p (h q) -> p h q", q=64),
                                                                      in1=bc(exps[:c, t, 8 * g:8 * g + 8], 2, [c, 8, 64]), op=ALU.mult), reads=[b_pyb, b_ex[t]], writes=[b_tt])
                                y, b_y, _ = y_p.next()
                                S.op("dve", lambda e: e.tensor_tensor(out=y[:c], in0=pya[:c, :], in1=tt[:c], op=ALU.add), reads=[b_pya, b_tt], writes=[b_y])
                            ckpt(31)
                            pu, b_pu, _ = pf.next()
                            S.op("pe", lambda e: e.matmul(pu[:, :], lhsT=bmt[:c, :], rhs=xwt[:c, :], start=True, stop=True), reads=[b_bmt, b_xwt], writes=[b_pu])
                            S.op("pool", lambda e: e.tensor_tensor(out=hT[:, g, :].rearrange("p (h q) -> p h q", q=64), in0=hT[:, g, :].rearrange("p (h q) -> p h q", q=64),
                                                                  in1=bc(exps[:, t, 32 + 8 * g:32 + 8 * g + 8], 2, [128, 8, 64]), op=ALU.mult), reads=[b_hT[g], b_ex[t]], writes=[b_hT[g]])
                            ckpt(32)
                            S.op("dve", lambda e: e.tensor_tensor(out=hT[:, g, :], in0=hT[:, g, :], in1=pu[:, :], op=ALU.add), reads=[b_hT[g], b_pu], writes=[b_hT[g]])
                            S.op("act", lambda e: e.activation(out=hTb[:, g, :], in_=hT[:, g, :], func=AF.Copy), reads=[b_hT[g]], writes=[b_hTb[g]])
                            yield
                            if full:
                                gate_norm_ssd(t, c, y, b_y, xtok, b_xtok)
                            yield
                        gens = [ssd_chunk(t) for t in chunk_tiles]
                        nG = len(gens)
                        next(gens[0])
                        next(gens[0])
                        for i in range(nG):
                            if i + 1 < nG:
                                next(gens[i + 1])
                            next(gens[i])
                            if i + 1 < nG:
                                next(gens[i + 1])
                            if i >= 1:
                                next(gens[i - 1])
                        next(gens[nG - 1])
                        S.barrier()
                        ckpt(4)
                        if full:
                            gate_fns[g] = gate_norm_ssd
                gate_fns = {}
                for g in range(2):
                    ssd_group(g)
                ckpt(5)
                if full:
                    for g in range(2):
                        with contextlib.ExitStack() as psm:
                            sample_ssd(g, psm, gate_fns[g])
                            S.barrier()
                    with contextlib.ExitStack() as psm:
                        xn_tok = sb(psm, "xn_tok", [16, 1536], F32)
                        b_xn_tok = Buf("xn_tok")
                        for q4 in range(3):
                            psx, b_psx, _ = pf.next()
                            for j in range(4):
                                ct = 4 * q4 + j
                                S.op("pe", lambda e, psx=psx, ct=ct, j=j: e.transpose(out=psx[:16, 128 * j:128 * j + 128], in_=xnew[:, ct, :], identity=identf[:, :]), reads=[b_xnew[ct]],
                                     writes=[b_psx] if j == 0 else [], pwrites=[] if j == 0 else [b_psx])
                            S.op("dve", lambda e, psx=psx, q4=q4: e.tensor_copy(out=xn_tok[:, 512 * q4:512 * q4 + 512], in_=psx[:16, :]), reads=[b_psx],
                                 writes=[b_xn_tok] if q4 == 0 else [], pwrites=[] if q4 == 0 else [b_xn_tok])
                        S.op("sp", lambda e: e.dma_start(out=oconv_d[:, 2, :], in_=xn_tok[:, :]), reads=[b_xn_tok], dma=S.owner("st0"))
                        S.barrier()
                S.barrier()
                pssd.close()

                if full:
                    qks_all = sb(ph, "qks_all", [16, 4, 3, 256], F32)
                    b_qks_all = [Buf(f"qks{h}") for h in range(4)]
                    sgs = sb(ph, "sgs", [16, 4, 256], F32)
                    b_sgs = [Buf(f"sgs{h}") for h in range(4)]
                gate_env = {}

                def ret_gate1(c, py, b_py):
                    yr, b_yr, _ = gate_env["yr_p"].next()
                    S.op("dve", lambda e: e.tensor_copy(out=yr[:c], in_=py[:c, 0:256]), reads=[b_py], writes=[b_yr])
                    return yr, b_yr

                def ret_gate(h, t, c, py, b_py, sgx, b_sgx):
                    yr, b_yr = ret_gate1(c, py, b_py)
                    ret_gate2(h, t, c, yr, b_yr, sgx, b_sgx)

                def ret_gate2(h, t, c, yr, b_yr, sgx, b_sgx):
                    ge = gate_env
                    c0 = t * 128
                    ss, b_ss, _ = ge["ss_p"].next()
                    S.op("dve", lambda e: e.memset(ss[:], 0.0), writes=[b_ss])
                    jr, b_jr, rg, b_rg = ge["junkr"], ge["b_junkr"], ge["retg"], ge["b_retg"]
                    S.op("act", lambda e: e.activation(out=jr[:c], in_=yr[:c], func=AF.Square, accum_out=ss[:c]), reads=[b_yr, b_ss], writes=[b_jr, b_ss])
                    rstd_from_ss(ss, b_ss, c, 1.0 / 256)
                    t2, b_t2, _ = ge["t2_p"].next()
                    S.op("dve", lambda e: e.scalar_tensor_tensor(out=t2[:c], in0=yr[:c], scalar=ss[:c, 0:1], in1=rg[:c, 256 * h:256 * h + 256], op0=ALU.mult, op1=ALU.mult),
                         reads=[b_yr, b_ss, b_rg], writes=[b_t2])
                    m2, b_m2, _ = ge["m2_p"].next()
                    S.op("dve", lambda e: e.tensor_tensor(out=m2[:c], in0=t2[:c], in1=sgx[:c], op=ALU.mult), reads=[b_t2, b_sgx], writes=[b_m2])
                    transposes_to_T(m2, b_m2, c, 2, mixT, b_mixT[t], False, 8 + 2 * h, c0)

                def sample_ret_all(psr):
                    retg_s = sb(psr, "retg_s", [16, 1024], F32)
                    b_retg_s = Buf("retg_s")
                    S.op("sp", lambda e: e.dma_start(out=retg_s[:], in_=retg_d.partition_broadcast(16)), writes=[b_retg_s], dma=S.owner("gt0"))
                    gate_env.update(ss_p=Pool(S, nc, psr, "ssr2", [128, 1], F32, 2), t2_p=Pool(S, nc, psr, "t2p2", [128, 256], F32, 2), yr_p=Pool(S, nc, psr, "yrp2", [128, 256], F32, 2), m2_p=Pool(S, nc, psr, "m2p2", [128, 256], BF16, 2),
                                    junkr=sb(psr, "junkr2", [128, 256], BF16), b_junkr=Buf("junkr2"), retg=retg_s, b_retg=b_retg_s)
                    qTs_p = Pool(S, nc, psr, "qTs", [128, 2, 16], F32, 2)
                    qTsel_p = Pool(S, nc, psr, "qTsel", [128, 2, 16, 16], BF16, 2)
                    vsel_p = Pool(S, nc, psr, "vsel", [16, 16, 256], BF16, 2)
                    kb_p = Pool(S, nc, psr, "kbp", [16, 256], BF16, 2)
                    snb_p = Pool(S, nc, psr, "snb", [128, 512], BF16, 3)
                    sb_p = Pool(S, nc, psr, "sbat", [128, 8, 2, 256], F32, 2, owners=["sa0", "sa1"])

                    rloaded = {}

                    def rload(h, bi):
                        sbt, b_sbt, ow = sb_p.next()
                        for tl in range(8):
                            S.op("sp", lambda e, tl=tl: e.dma_start(out=sbt[:, tl], in_=sret_d[8 * bi + tl, h].rearrange("(a p) v -> p a v", p=128)),
                                 writes=[b_sbt] if tl == 0 else [], pwrites=[] if tl == 0 else [b_sbt], dma=ow)
                        rloaded[(h, bi)] = (sbt, b_sbt, ow)
                    rload(0, 0)

                    def head(h):
                        qks = qks_all[:, h]
                        b_qks = b_qks_all[h]
                        qTs, b_qTs, _ = qTs_p.next()
                        qTsel, b_qTsel, _ = qTsel_p.next()
                        psq_, b_psq_, _ = pf.next()
                        for dt_ in range(2):
                            S.op("pe", lambda e, dt_=dt_: e.transpose(out=psq_[:, 16 * dt_:16 * dt_ + 16], in_=qks[:16, 0, 128 * dt_:128 * dt_ + 128], identity=identf[:16, :16]), reads=[b_qks],
                                 writes=[b_psq_] if dt_ == 0 else [], pwrites=[] if dt_ == 0 else [b_psq_])
                        S.op("dve", lambda e: e.tensor_copy(out=qTs[:].rearrange("p a b -> p (a b)"), in_=psq_[:, 0:32]), reads=[b_psq_], writes=[b_qTs])
                        for dt_ in range(2):
                            S.op("dve", lambda e, dt_=dt_: e.tensor_tensor(out=qTsel[:, dt_, :, :], in0=bc(qTs[:, dt_, :], 1, [128, 16, 16]), in1=i16b[:, :, :], op=ALU.mult), reads=[b_qTs],
                                 writes=[b_qTsel] if dt_ == 0 else [], pwrites=[] if dt_ == 0 else [b_qTsel])
                        po, b_po, _ = pacc.next()
                        kb, b_kb, _ = kb_p.next()
                        S.op("act", lambda e: e.activation(out=kb[:, :], in_=qks[:16, 1, :], func=AF.Copy), reads=[b_qks], writes=[b_kb])
                        vsel, b_vsel, _ = vsel_p.next()
                        S.op("dve", lambda e: e.tensor_tensor(out=vsel[:, :, :], in0=bc(identf[:16, :16], 2, [16, 16, 256]), in1=bc(qks[:16, 2, :], 1, [16, 16, 256]), op=ALU.mult), reads=[b_qks], writes=[b_vsel])

                        def batch(bi):
                            sbt, b_sbt, ow = rloaded.pop((h, bi))
                            nk = 2 * h + bi + 1
                            if nk < 8:
                                rload(nk // 2, nk % 2)

                            def tok(tl):
                                t = 8 * bi + tl
                                pu, b_pu, _ = pf.next()
                                vt, b_vt = vsel[:, t, :], b_vsel
                                for dt_ in range(2):
                                    S.op("pe", lambda e, dt_=dt_: e.matmul(pu[:, 256 * dt_:256 * dt_ + 256], lhsT=kb[:16, 128 * dt_:128 * dt_ + 128], rhs=vt[:16, :], start=True, stop=True), reads=[b_kb, b_vt],
                                         writes=[b_pu] if dt_ == 0 else [], pwrites=[] if dt_ == 0 else [b_pu])
                                if pend:
                                    pend.pop(0)()
                                S.op("dve", lambda e: e.scalar_tensor_tensor(out=sbt[:, tl].rearrange("p a b -> p (a b)"), in0=sbt[:, tl].rearrange("p a b -> p (a b)"), scalar=GAM[h], in1=pu[:, :], op0=ALU.mult, op1=ALU.add),
                                     reads=[b_sbt, b_pu], pwrites=[b_sbt])
                                snb, b_snb, _ = snb_p.next()
                                S.op("act", lambda e: e.activation(out=snb[:, :], in_=sbt[:, tl].rearrange("p a b -> p (a b)"), func=AF.Copy), reads=[b_sbt], writes=[b_snb])

                                def readout():
                                    for dt_ in range(2):
                                        S.op("pe", lambda e, dt_=dt_: e.matmul(po[:16, 0:256], lhsT=qTsel[:, dt_, t, :], rhs=snb[:, 256 * dt_:256 * dt_ + 256], start=(t == 0 and dt_ == 0), stop=(t == 15 and dt_ == 1)),
                                             reads=[b_qTsel, b_snb], writes=[b_po] if (t == 0 and dt_ == 0) else [], pwrites=[] if (t == 0 and dt_ == 0) else [b_po])
                                pend.append(readout)
                            for tl in range(8):
                                tok(tl)
                            if bi == 1:
                                while pend:
                                    pend.pop(0)()
                            for tl in range(8):
                                S.op("sp", lambda e, tl=tl: e.dma_start(out=oret_d[8 * bi + tl, h].rearrange("(a p) v -> p a v", p=128), in_=sbt[:, tl]), reads=[b_sbt], dma=ow)
                        pend = []
                        for bi in range(2):
                            batch(bi)
                        ret_gate(h, 8, 16, po, b_po, sgs[:, h, :], b_sgs[h])
                    for h in range(4):
                        head(h)

                with contextlib.ExitStack() as pr:
                    rope = sb(pr, "rope", [128, NT, 2, 128], F32)
                    b_rope = Buf("rope")
                    S.op("sp", lambda e: e.dma_start(out=rope[:].rearrange("p a b c -> p (a b c)"), in_=(ropem_d if full else ropew_d)), writes=[b_rope], dma=S.owner("gt0"))
                    dp = sb(pr, "dp", [128, 4, 128], F32)
                    S.op("sp", lambda e: e.dma_start(out=dp[:].rearrange("p a b -> p (a b)"), in_=dp_d), pwrites=[b_rope], dma=S.owner("gt0"))
                    retg = sb(pr, "retg", [128, 1024], F32)
                    S.op("sp", lambda e: e.dma_start(out=retg[:], in_=retg_d.partition_broadcast(128)), pwrites=[b_rope], dma=S.owner("gt0"))
                    ktok = sb(pr, "ktok", [128, NT, 256], BF16)
                    vw = sb(pr, "vw", [128, NT, 256], BF16)
                    b_ktok = [Buf(f"ktok{t}") for t in range(NT)]
                    b_vw = [Buf(f"vw{t}") for t in range(NT)]
                    rot_p = Pool(S, nc, pr, "rot", [128, 2, 2, 128], F32, 2)
                    rt_p = Pool(S, nc, pr, "rtp", [128, 4, 2, 128], F32, 1)
                    if full:
                        qsT = sb(pr, "qsT", [128, 2, TOK], BF16)
                        kT = sb(pr, "kT", [128, 2, TOK], BF16)
                        vtok = sb(pr, "vtok", [128, NT, 256], BF16)
                        sg = sb(pr, "sg", [128, NT, 256], F32)
                        b_qsT = [Buf(f"qsT{t}") for t in range(NT)]
                        b_kT = [Buf(f"kT{t}") for t in range(NT)]
                        b_vtok = [Buf(f"vtok{t}") for t in range(NT)]
                        b_sg = [Buf(f"sg{t}") for t in range(NT)]
                        qtok_p = Pool(S, nc, pr, "qtok", [128, 256], BF16, 2)
                        sc_p = Pool(S, nc, pr, "scp", [128, 128], BF16, 2)
                        t2_p = Pool(S, nc, pr, "t2p", [128, 256], F32, 2)
                        yr_p = Pool(S, nc, pr, "yrp", [128, 256], F32, 3)
                        m2_p = Pool(S, nc, pr, "m2p", [128, 256], BF16, 2)
                        ss_p = Pool(S, nc, pr, "ssr", [128, 1], F32, 2)
                        junkr = sb(pr, "junkr", [128, 256], BF16)
                        b_junkr = Buf("junkr")
                        gate_env.update(ss_p=ss_p, t2_p=t2_p, yr_p=yr_p, m2_p=m2_p, junkr=junkr, b_junkr=b_junkr, retg=retg, b_retg=b_rope)

                    def rotary(ps, b_ps, col0, nq, n, t, rot, b_rot):
                        src = ps[:n, col0:col0 + nq * 256].rearrange("p (a h f) -> p a h f", h=2, f=128)
                        x1, x2 = src[:, :, 0, :], src[:, :, 1, :]
                        cos = bc(rope[:n, t, 0, :], 1, [n, nq, 128])
                        sin = bc(rope[:n, t, 1, :], 1, [n, nq, 128])
                        rt, b_rt, _ = rt_p.next()
                        S.op("dve", lambda e: e.tensor_tensor(out=rt[:n, 0, 0:nq, :], in0=x1, in1=cos, op=ALU.mult), reads=[b_ps, b_rope], writes=[b_rt])
                        S.op("dve", lambda e: e.tensor_tensor(out=rt[:n, 1, 0:nq, :], in0=x2, in1=sin, op=ALU.mult), reads=[b_ps, b_rope], pwrites=[b_rt])
                        S.op("dve", lambda e: e.tensor_tensor(out=rt[:n, 2, 0:nq, :], in0=x1, in1=sin, op=ALU.mult), reads=[b_ps, b_rope], pwrites=[b_rt])
                        S.op("dve", lambda e: e.tensor_tensor(out=rt[:n, 3, 0:nq, :], in0=x2, in1=cos, op=ALU.mult), reads=[b_ps, b_rope], pwrites=[b_rt])
                        S.op("dve", lambda e: e.tensor_tensor(out=rot[:n, 0:nq, 0, :], in0=rt[:n, 0, 0:nq, :], in1=rt[:n, 1, 0:nq, :], op=ALU.subtract), reads=[b_rt], writes=[b_rot])
                        S.op("dve", lambda e: e.tensor_tensor(out=rot[:n, 0:nq, 1, :], in0=rt[:n, 2, 0:nq, :], in1=rt[:n, 3, 0:nq, :], op=ALU.add), reads=[b_rt], pwrites=[b_rot])

                    def ret_inproj(h):
                        if full:
                            qks = qks_all[:, h]
                            b_qks = b_qks_all[h]
                            wv, b_w = load_w(wp, [(w_in_d[:, C_Q + 256 * h:C_Q + 256 * h + 256], 256, 0, 512), (w_in_d[:, C_K + 256 * h:C_K + 256 * h + 256], 256, 256, 512)], 16)

                            def evac_qk(t, n, ps, b_ps):
                                rot, b_rot, _ = rot_p.next()
                                rotary(ps, b_ps, 0, 2, n, t, rot, b_rot)
                                S.op("act", lambda e: e.activation(out=ktok[:n, t, :], in_=rot[:n, 1].rearrange("p a b -> p (a b)"), func=AF.Copy), reads=[b_rot], writes=[b_ktok[t]])
                                if t == 8:
                                    S.op("dve", lambda e: e.tensor_copy(out=qks[:16, 0, :], in_=rot[:16, 0].rearrange("p a b -> p (a b)")), reads=[b_rot], writes=[b_qks])
                                    S.op("dve", lambda e: e.tensor_scalar(out=qks[:16, 1, :], in0=rot[:16, 1].rearrange("p a b -> p (a b)"), scalar1=0.0625, scalar2=None, op0=ALU.mult), reads=[b_rot], pwrites=[b_qks])
                                if t < 8:
                                    qtok, b_qtok, _ = qtok_p.next()
                                    S.op("dve", lambda e: e.tensor_scalar(out=qtok[:n], in0=rot[:n, 0].rearrange("p a b -> p (a b)"), scalar1=gamq[:n, h:h + 1], scalar2=None, op0=ALU.mult),
                                         reads=[b_rot], writes=[b_qtok])
                                    def deferred():
                                        transposes_to_T(qtok, b_qtok, n, 2, qsT, b_qsT[t], True, 0, t * 128)
                                        transposes_to_T(ktok[:, t, :], b_ktok[t], n, 2, kT, b_kT[t], True, 0, t * 128)
                                    return deferred
                                return None
                            wvA, b_wA = wv, b_w
                            wv, b_w = load_w(wp, [(w_in_d[:, C_V + 256 * h:C_V + 256 * h + 256], 256, 0, 512), (w_in_d[:, C_G + 256 * h:C_G + 256 * h + 256], 256, 256, 512)], 16)

                            def evac_vg(t, n, ps, b_ps):
                                S.op("act", lambda e: e.activation(out=vtok[:n, t, :], in_=ps[:n, 0:256], func=AF.Copy), reads=[b_ps], writes=[b_vtok[t]])
                                if t == 8:
                                    S.op("act", lambda e: e.activation(out=qks[:16, 2, :], in_=ps[:16, 0:256], func=AF.Copy), reads=[b_ps], pwrites=[b_qks])
                                if t < 8:
                                    S.op("act", lambda e: e.activation(out=vw[:n, t, :], in_=ps[:n, 0:256], func=AF.Copy, scale=wk[:n, h:h + 1]), reads=[b_ps], writes=[b_vw[t]])
                                S.op("act", lambda e: e.activation(out=sg[:n, t, :], in_=ps[:n, 256:512], func=AF.Silu), reads=[b_ps], writes=[b_sg[t]])
                                if t == 8:
                                    S.op("act", lambda e: e.activation(out=sgs[:16, h, :], in_=ps[:16, 256:512], func=AF.Silu), reads=[b_ps], writes=[b_sgs[h]])
                            for t in range(NT):
                                proj_tok(uT, b_uT, wvA, b_wA, 512, [t], evac_qk)
                                proj_tok(uT, b_uT, wv, b_w, 512, [t], evac_vg)
                                yield
                            flush_defer()
                        else:
                            wv, b_w = load_w(wp, [(w_in_d[:, C_K + 256 * h:C_K + 256 * h + 256], 256, 0, 512), (w_in_d[:, C_V + 256 * h:C_V + 256 * h + 256], 256, 256, 512)], 16)

                            def evac_kv(t, n, ps, b_ps):
                                rot, b_rot, _ = rot_p.next()
                                rotary(ps, b_ps, 0, 1, n, t, rot, b_rot)
                                S.op("act", lambda e: e.activation(out=ktok[:n, t, :], in_=rot[:n, 0].rearrange("p a b -> p (a b)"), func=AF.Copy), reads=[b_rot], writes=[b_ktok[t]])
                                wcol = h if t < 8 else 4 + h
                                S.op("dve", lambda e: e.tensor_scalar(out=vw[:n, t, :], in0=ps[:n, 256:512], scalar1=wk[:n, wcol:wcol + 1], scalar2=None, op0=ALU.mult), reads=[b_ps], writes=[b_vw[t]])
                            for t in range(NT):
                                proj_tok(uT, b_uT, wv, b_w, 512, [t], evac_kv)
                                yield


                    def ret_head(h, nxt):
                      with contextlib.ExitStack() as prh:
                        gc = GAM[h]

                        def ret_chunk(t):
                            c = rows(t)
                            c0 = t * 128
                            if full:
                                psc, b_psc, _ = pf.next()
                                for dt_ in range(2):
                                    S.op("pe", lambda e, dt_=dt_: e.matmul(psc[:c, 0:c], lhsT=kT[:, dt_, c0:c0 + c], rhs=qsT[:, dt_, c0:c0 + c], start=(dt_ == 0), stop=(dt_ == 1)),
                                         reads=[b_kT[t], b_qsT[t]], writes=[b_psc] if dt_ == 0 else [], pwrites=[] if dt_ == 0 else [b_psc])
                                sc, b_sc, _ = sc_p.next()
                                S.op("dve", lambda e: e.tensor_tensor(out=sc[:c, :c], in0=psc[:c, 0:c], in1=dp[:c, h, :c], op=ALU.mult), reads=[b_psc, b_rope], writes=[b_sc])
                            yield
                            if full:
                                py, b_py, _ = pf.next()
                                S.op("pe", lambda e: e.matmul(py[:c, 0:256], lhsT=sc[:c, :c], rhs=vtok[:c, t, :], start=True, stop=False), reads=[b_sc, b_vtok[t]], writes=[b_py])
                                for dt_ in range(2):
                                    S.op("pe", lambda e, dt_=dt_: e.matmul(py[:c, 0:256], lhsT=qsT[:, dt_, c0:c0 + c], rhs=Sb[:, h, dt_, :], start=False, stop=(dt_ == 1)),
                                         reads=[b_qsT[t], b_Sb[h]], pwrites=[b_py])
                                yr, b_yr = ret_gate1(c, py, b_py)
                            pu, b_pu, _ = pf.next()
                            for dt_ in range(2):
                                S.op("pe", lambda e, dt_=dt_: e.matmul(pu[:, 256 * dt_:256 * dt_ + 256], lhsT=ktok[:c, t, 128 * dt_:128 * dt_ + 128], rhs=vw[:c, t, :], start=True, stop=True),
                                     reads=[b_ktok[t], b_vw[t]], writes=[b_pu] if dt_ == 0 else [], pwrites=[] if dt_ == 0 else [b_pu])
                            gcc = gc ** c
                            S.op("dve", lambda e: e.scalar_tensor_tensor(out=Sst[:, h].rearrange("p a b -> p (a b)"), in0=Sst[:, h].rearrange("p a b -> p (a b)"), scalar=gcc, in1=pu[:, :], op0=ALU.mult, op1=ALU.add),
                                 reads=[b_S[h], b_pu], writes=[b_S[h]])
                            S.op("act", lambda e: e.activation(out=Sb[:, h].rearrange("p a b -> p (a b)"), in_=Sst[:, h].rearrange("p a b -> p (a b)"), func=AF.Copy), reads=[b_S[h]], writes=[b_Sb[h]])
                            yield
                            if full:
                                ret_gate2(h, t, c, yr, b_yr, sg[:, t, :], b_sg[t])
                            yield
                        gens = [ret_chunk(t) for t in chunk_tiles]
                        nG = len(gens)
                        next(gens[0])
                        for i in range(nG):
                            if i + 1 < nG:
                                next(gens[i + 1])
                            next(gens[i])
                            if nxt is not None and i >= 2:
                                next(nxt, None)
                            if i >= 1:
                                next(gens[i - 1])
                        next(gens[nG - 1])
                        if nxt is not None:
                            for _ in nxt:
                                pass
                    for _ in ret_inproj(0):
                        pass
                    for h in range(4):
                        ret_head(h, ret_inproj(h + 1) if h < 3 else None)
                S.barrier()
                if full:
                    with contextlib.ExitStack() as psr:
                        sample_ret_all(psr)
                        S.barrier()

        with contextlib.ExitStack() as stA:
            st = {}
            st["hT"] = sb(stA, "hT", [128, 2, 512], F32)
            st["hTb"] = sb(stA, "hTb", [128, 2, 512], BF16)
            st["S"] = sb(stA, "Sst", [128, 4, 2, 256], F32)
            st["Sb"] = sb(stA, "Sb", [128, 4, 2, 256], BF16)
            st["convtail"] = sb(stA, "convtail", [128, 12, 3], F32)
            st["ptail"] = sb(stA, "ptail", [128, 12, 3], F32)
            st["b_hT"] = [Buf("hT0"), Buf("hT1")]
            st["b_hTb"] = [Buf("hTb0"), Buf("hTb1")]
            st["b_S"] = [Buf(f"S{h}") for h in range(4)]
            st["b_Sb"] = [Buf(f"Sb{h}") for h in range(4)]
            st["b_ct"] = Buf("convtail")
            st["b_ptail"] = Buf("ptail")
            S.op("dve", lambda e: e.memset(st["hT"][:].rearrange("p a b -> p (a b)"), 0.0), writes=st["b_hT"])
            S.op("dve", lambda e: e.memset(st["hTb"][:].rearrange("p a b -> p (a b)"), 0.0), writes=st["b_hTb"])
            S.op("dve", lambda e: e.memset(st["S"][:].rearrange("p a b c -> p (a b c)"), 0.0), writes=st["b_S"])
            S.op("dve", lambda e: e.memset(st["Sb"][:].rearrange("p a b c -> p (a b c)"), 0.0), writes=st["b_Sb"])
            S.barrier()

            if stop > 0:
                norm_phase(xw_d, g_premix_d, uT, b_uT)
            if stop > 1:
                mixer_pass(False, st)
            if stop > 2:
                norm_phase(xm_d, g_premix_d, uT, b_uT)
                mixer_pass(True, st)

            with contextlib.ExitStack() as ph:
                osb = sb(ph, "osb", [128, 8, 128], F32)
                b_osb = Buf("osb")
                for i in range(8):
                    ps, b_ps, _ = pf.next()
                    g, j = i // 4, i % 4
                    S.op("pe", lambda e, ps=ps, g=g, j=j: e.transpose(out=ps[:, 0:128], in_=st["hT"][:, g, 128 * j:128 * j + 128], identity=identf[:, :]), reads=[st["b_hT"][g]], writes=[b_ps])
                    evac_copy(osb[:, i, :], ps[:, 0:128], reads=[b_ps], writes=[b_osb] if i == 0 else [], pwrites=[] if i == 0 else [b_osb])
                S.op("sp", lambda e: e.dma_start(out=pssm_d.rearrange("(i h2) p n -> (h2 p) i n", h2=2), in_=osb[:]), reads=[b_osb], dma=S.owner("st0"))
                for h in range(4):
                    S.op("sp", lambda e, h=h: e.dma_start(out=pret_d[h].rearrange("(a p) v -> p a v", p=128), in_=st["S"][:, h]), reads=[st["b_S"][h]], dma=S.owner("st0"))
                pcs = sb(ph, "pcs", [3, 1536], F32)
                b_pcs = Buf("pcs")
                for q4 in range(3):
                    ps, b_ps, _ = pf.next()
                    for j in range(4):
                        ct = 4 * q4 + j
                        S.op("pe", lambda e, ps=ps, ct=ct, j=j: e.transpose(out=ps[:3, 128 * j:128 * j + 128], in_=st["ptail"][:, ct, :], identity=identf[:, :]),
                             reads=[st["b_ptail"]], writes=[b_ps] if j == 0 else [], pwrites=[] if j == 0 else [b_ps])
                    evac_copy(pcs[:, 512 * q4:512 * q4 + 512], ps[:3, :], reads=[b_ps], writes=[b_pcs] if q4 == 0 else [], pwrites=[] if q4 == 0 else [b_pcs])
                S.op("sp", lambda e: e.dma_start(out=pconv_d, in_=pcs[:]), reads=[b_pcs], dma=S.owner("st0"))
                S.barrier()

        def out_proj_phase():
            with contextlib.ExitStack() as ph:
                o_all = sb(ph, "o_all", [128, NT, D], F32)
                b_o = [Buf(f"o{t}") for t in range(NT)]
                uview = uT[:].rearrange("p a b -> p (a b)")[:, 0:16384].bitcast(F32)
                xslots = [(uview[:, 2048 * i:2048 * i + 2048], Buf(f"xu{i}"), S.owner(f"xu{i}")) for i in range(4)]
                xloaded = {}

                def xload(t):
                    xs, b_xs, ow = xslots[t % len(xslots)]
                    n = rows(t)
                    S.op("sp", lambda e: e.dma_start(out=xs[:n], in_=xm_d[t * 128:t * 128 + n, :]), writes=[b_xs], dma=ow)
                    xloaded[t] = (xs, b_xs)
                for t in range(4):
                    xload(t)
                with contextlib.ExitStack() as p3:
                    ss_p = Pool(S, nc, p3, "ss", [128, 1], F32, 4)
                    junk = sb(p3, "junk", [128, D], BF16)
                    b_junk = Buf("junk")
                    gpm = sb(p3, "gpm", [128, D], F32)
                    b_g = Buf("gains")
                    S.op("sp", lambda e: e.dma_start(out=gpm[:], in_=g_postmix_d.partition_broadcast(128)), writes=[b_g], dma=S.owner("gt0"))

                    def tileA(t):
                        n = rows(t)
                        xs, b_xs = xloaded.pop(t)
                        o = o_all[:, t, :]
                        ss, b_ss, _ = ss_p.next()
                        S.op("dve", lambda e: e.memset(ss[:], 0.0), writes=[b_ss])
                        S.op("act", lambda e: e.activation(out=junk[:n], in_=o[:n], func=AF.Square, accum_out=ss[:n]), reads=[b_o[t], b_ss], writes=[b_junk, b_ss])
                        rstd_from_ss(ss, b_ss, n, 1.0 / D)
                        S.op("dve", lambda e: e.scalar_tensor_tensor(out=o[:n], in0=o[:n], scalar=ss[:n, 0:1], in1=gpm[:n], op0=ALU.mult, op1=ALU.mult), reads=[b_o[t], b_ss, b_g], writes=[b_o[t]])
                        S.op("dve", lambda e: e.tensor_tensor(out=o[:n], in0=o[:n], in1=xs[:n], op=ALU.add), reads=[b_o[t], b_xs], writes=[b_o[t]])
                        S.op("sp", lambda e: e.dma_start(out=h1_d[t * 128:t * 128 + n, :], in_=o[:n]), reads=[b_o[t]], dma=S.owner("st0"))

                    for cb in range(4):
                        wv, b_w = load_w(wp, [(w_out_d[:, 512 * cb:512 * cb + 512], 512, 0, 512)], 16)

                        def evac(t, n, ps, b_ps, cb=cb):
                            if cb < 3:
                                evac_copy(o_all[:n, t, 512 * cb:512 * cb + 512], ps[:n, :], reads=[b_ps], writes=[b_o[t]] if cb == 0 else [], pwrites=[] if cb == 0 else [b_o[t]])
                            else:
                                S.op("act", lambda e: e.activation(out=o_all[:n, t, 512 * cb:512 * cb + 512], in_=ps[:n, :], func=AF.Copy), reads=[b_ps], pwrites=[b_o[t]])
                                tileA(t)
                                if t + 4 < NT:
                                    xload(t + 4)
                        proj_tok(mixT, b_mixT, wv, b_w, 512, list(range(NT)), evac)
                    S.barrier()
                with contextlib.ExitStack() as p4:
                    u_p = Pool(S, nc, p4, "ub", [128, D], BF16, 3)
                    ss_p = Pool(S, nc, p4, "ss", [128, 1], F32, 4)
                    junk = sb(p4, "junk", [128, D], BF16)
                    b_junk = Buf("junk")
                    gpf = sb(p4, "gpf", [128, D], F32)
                    b_g = Buf("gains")
                    S.op("sp", lambda e: e.dma_start(out=gpf[:], in_=g_preffn_d.partition_broadcast(128)), writes=[b_g], dma=S.owner("gt0"))

                    def tileB(t):
                        n = rows(t)
                        o = o_all[:, t, :]
                        ss2, b_ss2, _ = ss_p.next()
                        S.op("dve", lambda e: e.memset(ss2[:], 0.0), writes=[b_ss2])
                        S.op("act", lambda e: e.activation(out=junk[:n], in_=o[:n], func=AF.Square, accum_out=ss2[:n]), reads=[b_o[t], b_ss2], writes=[b_junk, b_ss2])
                        rstd_from_ss(ss2, b_ss2, n, 1.0 / D)
                        u, b_u, _ = u_p.next()
                        S.op("dve", lambda e: e.scalar_tensor_tensor(out=u[:n], in0=o[:n], scalar=ss2[:n, 0:1], in1=gpf[:n], op0=ALU.mult, op1=ALU.mult), reads=[b_o[t], b_ss2, b_g], writes=[b_u])
                        yield
                        transposes_to_T(u, b_u, n, 16, uT, b_uT[t], True, 0, t * 128)
                        yield
                    gens = [tileB(t) for t in range(NT)]
                    next(gens[0])
                    for t in range(NT):
                        if t + 1 < NT:
                            next(gens[t + 1])
                        next(gens[t])
                    S.barrier()

        def ffn_phase():
            groups = [(0, 352), (352, 352), (704, 336)]
            with contextlib.ExitStack() as ph:
                actT = sb(ph, "actT", [128, 44, TOK], BF16)
                b_actg = [Buf(f"actg{i}") for i in range(3)]
                with contextlib.ExitStack() as p2:
                    sg_p = Pool(S, nc, p2, "sgp", [128, 512], F32, 3)

                    def block(j):
                        wv, b_w = load_w(wp, [(w_gate_d[:, 256 * j:256 * j + 256], 256, 0, 512), (w_up_d[:, 256 * j:256 * j + 256], 256, 256, 512)], 16)
                        for ct in range(2):
                            for gi, (c0, n) in enumerate(groups):
                                tl = sorted(set(range(c0 // 128, (c0 + n - 1) // 128 + 1)))
                                pg_, b_pg, _ = pf.next()
                                pu_, b_pu, _ = pf.next()
                                for (ps, b_ps, off) in ((pg_, b_pg, 0), (pu_, b_pu, 256)):
                                    for k in range(16):
                                        S.op("pe", lambda e, ps=ps, k=k, ct=ct, c0=c0, n=n, off=off: e.matmul(ps[:, 0:n], lhsT=wv[:, k, off + ct * 128:off + ct * 128 + 128], rhs=uT[:, k, c0:c0 + n], start=(k == 0), stop=(k == 15)),
                                             reads=[b_uT[t] for t in tl] + [b_w], writes=[b_ps] if k == 0 else [], pwrites=[] if k == 0 else [b_ps])
                                sgt, b_sgt, _ = sg_p.next()
                                S.op("act", lambda e, pg_=pg_, sgt=sgt, n=n: e.activation(out=sgt[:, 0:n], in_=pg_[:, 0:n], func=AF.Silu), reads=[b_pg], writes=[b_sgt])
                                first = (j == 0 and ct == 0 and gi == 0)
                                S.op("dve", lambda e, pu_=pu_, sgt=sgt, n=n, c0=c0, ct=ct: e.tensor_tensor(out=actT[:, 2 * j + ct, c0:c0 + n], in0=pu_[:, 0:n], in1=sgt[:, 0:n], op=ALU.mult), reads=[b_pu, b_sgt],
                                     writes=[b_actg[0]] if first else [], pwrites=[] if first else [b_actg[0]])
                    for j in range(22):
                        block(j)
                    S.barrier(full=True)
                with contextlib.ExitStack() as p3:
                    wp2 = Pool(S, nc, p3, "wsd", [128, 44 * 256], BF16, 2, owners=["wd0", "wd1"])
                    dst_p = Pool(S, nc, p3, "dst", [128, 256], F32, 2, owners=["ds0", "ds1"])
                    b_actt = [b_actg[0] for t in range(NT)]
                    uview = uT[:].rearrange("p a b -> p (a b)")[:, 0:16384].bitcast(F32)
                    hslots = [(uview[:, 2048 * i:2048 * i + 2048], Buf(f"hv{i}"), S.owner(f"xu{i}")) for i in range(4)]
                    w0, w1 = wp.tiles[0][0], wp.tiles[1][0]
                    gtv = w0[:, 0:4096].bitcast(F32)
                    junkv = w0[:, 4096:6144]
                    ssv = w0[:, 6144:6176].bitcast(F32)
                    w1v = w1[:, 0:8192].bitcast(F32)
                    dslots = [(w1v[:, 2048 * i:2048 * i + 2048], Buf(f"dv{i}"), S.owner(f"dl{i}"), None) for i in range(2)]
                    wd0, b_wd0 = wp2.tiles[0][0], wp2.tiles[0][1]
                    wd0v = wd0[:, 0:8192].bitcast(F32)
                    dslots += [(wd0v[:, 2048 * i:2048 * i + 2048], Buf(f"dv{2 + i}"), S.owner(f"dl{2 + i}"), b_wd0) for i in range(2)]
                    b_gtv, b_junkv = Buf("gtv"), Buf("junkv")
                    b_ssv = [Buf(f"ssv{i}") for i in range(4)]
                    b_dd = [Buf(f"dd{t}") for t in range(NT)]
                    S.op("sp", lambda e: e.dma_start(out=gtv, in_=g_postffn_d.partition_broadcast(128)), writes=[b_gtv], dma=S.owner("gt0"))
                    hloaded, dloaded = {}, {}

                    def hload(t):
                        hl, b_hl, ow = hslots[t % 4]
                        n = rows(t)
                        S.op("sp", lambda e: e.dma_start(out=hl[:n], in_=h1_d[t * 128:t * 128 + n, :]), writes=[b_hl], dma=ow)
                        hloaded[t] = (hl, b_hl)

                    def dload(t):
                        dl, b_dl, ow, b_extra = dslots[t % 4]
                        n = rows(t)
                        S.op("sp", lambda e: e.dma_start(out=dl[:n, 0:1792], in_=dd_d[t * 128:t * 128 + n, 0:1792]), reads=[b_dd[t]],
                             writes=[b_dl] + ([b_extra] if b_extra is not None else []), dma=ow)
                        dloaded[t] = (dl, b_dl, ow)
                    for t in range(4):
                        hload(t)

                    def dblock(cb):
                        wv, b_w = load_w(wp2, [(w_down_d[:, 256 * cb:256 * cb + 256], 256, 0, 256)], 44)

                        def evac(t, n, ps, b_ps):
                            dst, b_dst, ow = dst_p.next()
                            S.op("dve", lambda e: e.tensor_copy(out=dst[:n], in_=ps[:n, 0:256]), reads=[b_ps], writes=[b_dst])
                            S.op("sp", lambda e: e.dma_start(out=dd_d[t * 128:t * 128 + n, 256 * cb:256 * cb + 256], in_=dst[:n]), reads=[b_dst], dma=ow,
                                 writes=[b_dd[t]] if cb == 0 else [], pwrites=[] if cb == 0 else [b_dd[t]])

                        def evac_last(t, n, ps, b_ps):
                            dl, b_dl, ow_d = dloaded.pop(t)
                            hl, b_hl = hloaded.pop(t)
                            S.op("dve", lambda e: e.tensor_copy(out=dl[:n, 1792:2048], in_=ps[:n, 0:256]), reads=[b_ps], pwrites=[b_dl])
                            ss, b_ss = ssv[:, t % 4:t % 4 + 1], b_ssv[t % 4]
                            S.op("dve", lambda e: e.memset(ss, 0.0), writes=[b_ss])
                            S.op("act", lambda e: e.activation(out=junkv[:n], in_=dl[:n], func=AF.Square, accum_out=ss[:n]), reads=[b_dl, b_ss], writes=[b_junkv, b_ss])
                            rstd_from_ss(ss, b_ss, n, 1.0 / D)
                            S.op("dve", lambda e: e.scalar_tensor_tensor(out=dl[:n], in0=dl[:n], scalar=ss[:n, 0:1], in1=gtv[:n], op0=ALU.mult, op1=ALU.mult), reads=[b_dl, b_ss, b_gtv], writes=[b_dl])
                            S.op("dve", lambda e: e.tensor_tensor(out=dl[:n], in0=dl[:n], in1=hl[:n], op=ALU.add), reads=[b_dl, b_hl], writes=[b_dl])
                            S.op("pool", lambda e: e.dma_start(out=y_d[t * 128:t * 128 + n, :], in_=dl[:n]), reads=[b_dl], dma=ow_d)
                            if t + 4 < NT:
                                dload(t + 4)
                            if t + 4 < NT:
                                hload(t + 4)
                        if cb == 7:
                            for t_ in range(4):
                                dload(t_)
                        proj_tok(actT, b_actt, wv, b_w, 256, list(range(NT)), evac_last if cb == 7 else evac, kt=44)
                    for cb in range(8):
                        dblock(cb)
                    S.barrier(full=True)

        def final_phase():
            with contextlib.ExitStack() as ph:
                h_p = Pool(S, nc, ph, "hl", [128, D], F32, 4, owners=["xs0", "xs1", "xs2", "xs3"])
                d_p = Pool(S, nc, ph, "dl", [128, D], F32, 4, owners=["dl0", "dl1", "dl2", "dl3"])
                ss_p = Pool(S, nc, ph, "ss", [128, 1], F32, 4)
                junk = sb(ph, "junk", [128, D], BF16)
                b_junk = Buf("junk")
                gt = sb(ph, "gt", [128, D], F32)
                b_gt = Buf("gt")
                S.op("sp", lambda e: e.dma_start(out=gt[:], in_=g_postffn_d.partition_broadcast(128)), writes=[b_gt], dma=S.owner("gt0"))

                floaded = {}

                def fload(t):
                    n = rows(t)
                    hl, b_hl, ow_h = h_p.next()
                    dl, b_dl, ow_d = d_p.next()
                    S.op("sp", lambda e: e.dma_start(out=hl[:n], in_=h1_d[t * 128:t * 128 + n, :]), writes=[b_hl], dma=ow_h)
                    S.op("sp", lambda e: e.dma_start(out=dl[:n], in_=dd_d[t * 128:t * 128 + n, :]), writes=[b_dl], dma=ow_d)
                    floaded[t] = (hl, b_hl, ow_h, dl, b_dl, ow_d)

                def tile(t):
                    n = rows(t)
                    hl, b_hl, ow_h, dl, b_dl, ow_d = floaded.pop(t)
                    ss, b_ss, _ = ss_p.next()
                    S.op("dve", lambda e: e.memset(ss[:], 0.0), writes=[b_ss])
                    S.op("act", lambda e: e.activation(out=junk[:n], in_=dl[:n], func=AF.Square, accum_out=ss[:n]), reads=[b_dl, b_ss], writes=[b_junk, b_ss])
                    rstd_from_ss(ss, b_ss, n, 1.0 / D)
                    S.op("dve", lambda e: e.scalar_tensor_tensor(out=dl[:n], in0=dl[:n], scalar=ss[:n, 0:1], in1=gt[:n], op0=ALU.mult, op1=ALU.mult), reads=[b_dl, b_ss, b_gt], writes=[b_dl])
                    S.op("dve", lambda e: e.tensor_tensor(out=dl[:n], in0=dl[:n], in1=hl[:n], op=ALU.add), reads=[b_dl, b_hl], writes=[b_dl])
                    S.op("sp", lambda e: e.dma_start(out=y_d[t * 128:t * 128 + n, :], in_=dl[:n]), reads=[b_dl], dma=ow_d)
                for t in range(3):
                    fload(t)
                for t in range(NT):
                    tile(t)
                    if t + 3 < NT:
                        fload(t + 3)
                S.barrier()

        if stop > 3:
            out_proj_phase()
        S.barrier()
        if not dbg:
            esm.close()
        if stop > 4:
            ffn_phase()

        _HALT[0] = False
        if dbg:
            dbg_d["mixT"] = dout("dbg_mixT", [128, 16 * TOK], BF16)
            S.op("sp", lambda e: e.dma_start(out=dbg_d["mixT"], in_=mixT[:].rearrange("p a b -> p (a b)")), reads=b_mixT, dma=S.owner("st0"))
            dbg_d["uT"] = dout("dbg_uT", [128, 16 * TOK], BF16)
            S.op("sp", lambda e: e.dma_start(out=dbg_d["uT"], in_=uT[:].rearrange("p a b -> p (a b)")), reads=b_uT, dma=S.owner("st0"))
            S.barrier()

        S.finalize()
        with nc.allow_non_contiguous_dma(reason="small strided parameter / state layouts"):
            S.emit()
    return nc


def _tables():
    f32 = np.float32
    lg = np.log1p(-(2.0 ** (-5.0 - np.arange(4)))).astype(np.float64)
    i = np.arange(128)
    gamq = np.exp((i[:, None] + 1.0) * lg[None, :]).astype(f32)
    wk128 = (np.exp((127.0 - i[:, None]) * lg[None, :]) / 16.0).astype(f32)
    wk16 = np.zeros((128, 4), f32)
    wk16[:16] = (np.exp((15.0 - np.arange(16)[:, None]) * lg[None, :]) / 16.0).astype(f32)
    wk = np.concatenate([wk128, wk16], axis=1)
    dp = np.zeros((128, 4, 128), f32)
    for h in range(4):
        m = (i[:, None] <= i[None, :])
        dp[:, h, :] = np.where(m, np.exp(-(i[:, None] + 1.0) * lg[h]) / 16.0, 0.0)
    U = (i[:, None] <= i[None, :]).astype(f32)
    L = (i[:, None] > i[None, :]).astype(f32)
    msk = np.concatenate([U, L, np.ones((128, 128), f32)], axis=1)
    sel = np.zeros((16, 8, 128), f32)
    for hp in range(8):
        for q in range(128):
            sel[2 * hp + q // 64, hp, q] = 1.0
    i16b = np.broadcast_to(np.eye(16, dtype=f32).reshape(1, 256), (128, 256)).copy()
    return gamq, wk, dp.reshape(128, 512), msk, sel.reshape(16, 1024), i16b


def _rope_table(pos):
    f32 = np.float32
    half = 128
    inv_freq = (f32(10000.0) ** (-(np.arange(half, dtype=f32)) / f32(half))).astype(f32)
    ang = (pos.astype(f32)[:, None] * inv_freq[None, :]).astype(f32)
    cs = np.stack([np.cos(ang), np.sin(ang)], axis=1).astype(f32)
    out = np.zeros((NT * 128, 2, 128), f32)
    out[:TOK] = cs
    return np.ascontiguousarray(out.reshape(NT, 128, 2, 128).transpose(1, 0, 2, 3).reshape(128, NT * 256))


_CACHE = {}


def kernel(x_prompt, x_sample, state_conv, state_ssm, state_ret, meta_tokens, pre_mix_g, post_mix_g,
           pre_ffn_g, post_ffn_g, w_in, conv_w, conv_b, dt_bias, a_log, d_skip, ssm_norm_g, ret_norm_g,
           w_out, w_gate, w_up, w_down, _dbg=False, _stop=99, _cores=None, _trace=False):
    f32 = np.float32
    A = lambda a: np.ascontiguousarray(np.asarray(a, dtype=f32))
    x_prompt, x_sample, meta_tokens = A(x_prompt), A(x_sample), A(meta_tokens)
    gamq, wk, dp, msk, sel, i16b = _tables()
    shared = dict(
        gamq=gamq, wk=wk, dp=dp, msk=msk, sel=sel, i16b=i16b, idf=np.eye(128, dtype=f32), idb=np.eye(128).astype(ml_dtypes.bfloat16),
        w_in=A(w_in[0]), w_out=A(w_out[0]), w_gate=A(w_gate[0]), w_up=A(w_up[0]), w_down=A(w_down[0]),
        pre_mix_g=A(pre_mix_g), post_mix_g=A(post_mix_g), pre_ffn_g=A(pre_ffn_g), post_ffn_g=A(post_ffn_g),
        conv_w=A(conv_w[0]), conv_b=A(conv_b), dt_bias=A(dt_bias), a_log=A(a_log), d_skip=A(d_skip),
        ssm_norm_g=A(ssm_norm_g), ret_norm_g=A(ret_norm_g))
    in_maps = []
    for c in range(8):
        b, half = c // 2, c % 2
        hp = np.concatenate([meta_tokens, x_prompt[b]], axis=0)
        if half == 0:
            xw = np.concatenate([np.zeros((1024, D), f32), meta_tokens], axis=0)
            posw = np.concatenate([np.zeros(1024, f32), np.arange(16, dtype=f32)])
            validw = np.concatenate([np.zeros(1024, f32), np.ones(128, f32)])
            main = hp[16:1040]
            posm = np.arange(16, 1040, dtype=f32)
        else:
            xw = hp[0:1040]
            posw = np.arange(1040, dtype=f32)
            validw = np.ones(1152, f32)
            main = hp[1040:2064]
            posm = np.arange(1040, 2064, dtype=f32)
        xm = np.concatenate([main, x_sample[16 * c:16 * c + 16, 0]], axis=0)
        posm = np.concatenate([posm, np.full(16, 16384.0, f32)])
        m = dict(shared)
        m.update(xw=A(xw), xm=A(xm), validw=A(validw.reshape(NT, 128).T),
                 ropew=_rope_table(posw), ropem=_rope_table(posm),
                 st_conv=A(state_conv[0, 16 * c:16 * c + 16]), st_ssm=A(state_ssm[0, 16 * c:16 * c + 16]),
                 st_ret=A(state_ret[0, 16 * c:16 * c + 16]))
        in_maps.append(m)
    key = ("nc", _dbg, _stop, _SUB[0])
    if key not in _CACHE:
        _CACHE[key] = build_program(dbg=_dbg, stop=_stop)
    nc = _CACHE[key]
    if _cores is not None:
        res = run_bass_kernel_spmd(nc, [in_maps[c] for c in _cores], core_ids=list(range(len(_cores))), trace=_trace)
        if _trace:
            print("exec_time_ns", res.exec_time_ns)
        return res.results
    res = run_bass_kernel_spmd(nc, in_maps, core_ids=list(range(8)))
    R = res.results
    y_prompt = np.zeros((4, 2048, D), f32)
    y_sample = np.zeros((128, 1, D), f32)
    p_conv = np.zeros((1, 4, 3, 1536), f32)
    p_ssm = np.zeros((1, 4, 16, 64, 128), f32)
    p_ret = np.zeros((1, 4, 4, 256, 256), f32)
    s_conv = np.zeros((1, 128, 3, 1536), f32)
    s_ssm = np.zeros((1, 128, 16, 64, 128), f32)
    s_ret = np.zeros((1, 128, 4, 256, 256), f32)
    for c in range(8):
        b, half = c // 2, c % 2
        y = np.asarray(R[c]["y"])
        y_prompt[b, 1024 * half:1024 * half + 1024] = y[:1024]
        y_sample[16 * c:16 * c + 16, 0] = y[1024:1040]
        if half == 1:
            p_conv[0, b] = np.asarray(R[c]["p_conv"])
            p_ssm[0, b] = np.asarray(R[c]["p_ssm"])
            p_ret[0, b] = np.asarray(R[c]["p_ret"])
        s_conv[0, 16 * c:16 * c + 16] = np.asarray(R[c]["s_conv"])
        s_ssm[0, 16 * c:16 * c + 16] = np.asarray(R[c]["s_ssm"])
        s_ret[0, 16 * c:16 * c + 16] = np.asarray(R[c]["s_ret"])
    if _dbg:
        return R
    return (y_prompt, y_sample, p_conv, p_ssm, p_ret, s_conv, s_ssm, s_ret)
```

```python
import contextlib
import math
import numpy as np
import ml_dtypes
import concourse.bass as bass
import concourse.mybir as mybir
from concourse.bass_utils import run_bass_kernel_spmd

F32 = mybir.dt.float32
BF16 = mybir.dt.bfloat16
ALU = mybir.AluOpType
AF = mybir.ActivationFunctionType

D = 2048
NT = 9
TOK = 1040
DFF = 5632
EPS = 1e-6
GAM = [1.0 - 2.0 ** (-5 - h) for h in range(4)]
C_Z, C_X, C_B, C_C, C_DT, C_Q, C_K, C_V, C_G = 0, 1024, 2048, 2304, 2560, 2576, 3600, 4624, 5648


def rows(t):
    return 128 if t < 8 else 16


class Buf:
    __slots__ = ("name", "writers", "readers", "gen_deps")

    def __init__(self, name):
        self.name = name
        self.writers = []
        self.readers = []
        self.gen_deps = []


class Owner:
    __slots__ = ("name", "sem", "cnt")

    def __init__(self, name):
        self.name = name
        self.sem = None
        self.cnt = 0


class Op:
    __slots__ = ("eng", "fn", "idx", "deps", "dmadeps", "marked", "cum", "dma_sem", "dma_val")


class Sched:
    ENGS = ("pe", "act", "dve", "pool", "sp")
    CENGS = ("pe", "act", "dve", "pool")

    def __init__(self, nc, es):
        self.nc = nc
        self.es = es
        self.ops = {e: [] for e in self.ENGS}
        self.seen = {e: {f: -1 for f in self.ENGS} for e in self.ENGS}
        self.seen_dma = {e: {} for e in self.ENGS}
        self.nsem = 0
        self.esem = {e: self.new_sem("eng_" + e) for e in self.CENGS}
        self.owners = {}
        self.last_compute = {e: None for e in self.CENGS}

    def new_sem(self, name):
        self.nsem += 1
        return self.es.enter_context(self.nc.semaphore(f"{name}_{self.nsem}"))

    def owner(self, name):
        if name not in self.owners:
            self.owners[name] = Owner(name)
        return self.owners[name]

    def _mk(self, eng, fn):
        o = Op()
        o.eng = eng
        o.fn = fn
        o.idx = len(self.ops[eng])
        o.marked = False
        o.cum = 0
        o.dma_sem = None
        o.dma_val = 0
        o.deps = []
        o.dmadeps = []
        return o

    def op(self, eng, fn, reads=(), writes=(), pwrites=(), dma=None):
        if _HALT[0]:
            return None
        o = self._mk(eng, fn)
        raw = []
        for b in reads:
            raw.extend(b.writers)
        for b in writes:
            g = b.readers + b.writers
            raw.extend(g)
            b.gen_deps = g
        for b in pwrites:
            raw.extend(b.gen_deps)
        best = {}
        dmadeps = {}
        for d in raw:
            if d.dma_sem is not None:
                k = id(d.dma_sem)
                if k not in dmadeps or dmadeps[k][1] < d.dma_val:
                    dmadeps[k] = (d.dma_sem, d.dma_val)
            else:
                if d.eng == "pe" and eng == "pe" and dma is None:
                    continue
                if d.eng not in best or best[d.eng].idx < d.idx:
                    best[d.eng] = d
        for e, d in best.items():
            if self.seen[eng][e] >= d.idx:
                continue
            self.seen[eng][e] = d.idx
            o.deps.append(d)
        for k, (s, v) in dmadeps.items():
            if self.seen_dma[eng].get(k, 0) >= v:
                continue
            self.seen_dma[eng][k] = v
            o.dmadeps.append((s, v))
        if dma is not None:
            if dma.sem is None:
                dma.sem = self.new_sem("dma_" + dma.name)
            dma.cnt += 16
            o.dma_sem = dma.sem
            o.dma_val = dma.cnt
        elif eng in self.CENGS:
            self.last_compute[eng] = o
        for b in reads:
            b.readers.append(o)
        for b in writes:
            b.writers = [o]
            b.readers = []
        for b in pwrites:
            b.writers.append(o)
        self.ops[eng].append(o)
        return o

    def barrier(self, full=False):
        if _HALT[0]:
            return
        lasts = [o for o in self.last_compute.values() if o is not None]
        dms = [(w.sem, w.cnt) for w in self.owners.values() if w.sem is not None and w.cnt > 0 and (full or w.name not in ("w0", "w1"))]
        for e in self.ENGS:
            if e == "pool" and not full:
                continue
            o = self._mk(e, None)
            for d in lasts:
                if self.seen[e][d.eng] >= d.idx:
                    continue
                self.seen[e][d.eng] = d.idx
                o.deps.append(d)
            for (s, v) in dms:
                k = id(s)
                if self.seen_dma[e].get(k, 0) >= v:
                    continue
                self.seen_dma[e][k] = v
                o.dmadeps.append((s, v))
            self.ops[e].append(o)

    def finalize(self):
        self.barrier(full=True)
        for e in self.ENGS:
            for o in self.ops[e]:
                for d in o.deps:
                    d.marked = True
        for e in self.ENGS:
            c = 0
            for o in self.ops[e]:
                if o.marked and o.dma_sem is None:
                    c += 1
                o.cum = c

    def emit(self):
        nc = self.nc

        def replay(ename, e):
            for o in self.ops[ename]:
                for d in o.deps:
                    e.wait_ge(self.esem[d.eng], d.cum)
                for (s, v) in o.dmadeps:
                    e.wait_ge(s, v)
                if o.fn is None:
                    continue
                ins = o.fn(e)
                if o.dma_sem is not None:
                    ins.then_inc(o.dma_sem, 16)
                elif o.marked:
                    ins.then_inc(self.esem[ename], 1)

        with nc.Block() as block:
            @block.tensor
            def _(e):
                replay("pe", e)

            @block.scalar
            def _(e):
                replay("act", e)

            @block.vector
            def _(e):
                replay("dve", e)

            @block.gpsimd
            def _(e):
                replay("pool", e)

            @block.sync
            def _(e):
                replay("sp", e)


_UID = [0]


class Pool:
    def __init__(self, S, nc, es, name, shape, dtype, n, psum=False, owners=None):
        self.tiles = []
        _UID[0] += 1
        name = f"p_{name}_{_UID[0]}_"
        for i in range(n):
            if psum:
                t = es.enter_context(nc.psum_tensor(f"{name}{i}", shape, dtype))
            else:
                t = es.enter_context(nc.sbuf_tensor(f"{name}{i}", shape, dtype))
            ow = S.owner(owners[i]) if owners else None
            self.tiles.append((t, Buf(f"{name}{i}"), ow))
        self.i = 0

    def next(self):
        t = self.tiles[self.i % len(self.tiles)]
        self.i += 1
        return t


def bc(ap, axis, shape):
    return ap.unsqueeze(axis).broadcast_to(shape)


class _Stop(Exception):
    pass


_SUB = [99]


_HALT = [False]


def ckpt(k):
    if _SUB[0] == k:
        _HALT[0] = True


def build_program(dbg=False, stop=99):
    nc = bass.Bass("TRN2", target_bir_lowering=False)

    def din(name, shape, dt=F32):
        return nc.dram_tensor(name, shape, dt, kind="ExternalInput").ap()

    def dout(name, shape, dt=F32):
        return nc.dram_tensor(name, shape, dt, kind="ExternalOutput").ap()

    xw_d = din("xw", [TOK, D])
    xm_d = din("xm", [TOK, D])
    valid_d = din("validw", [128, NT])
    ropew_d = din("ropew", [128, NT * 256])
    ropem_d = din("ropem", [128, NT * 256])
    gamq_d = din("gamq", [128, 4])
    wk_d = din("wk", [128, 8])
    dp_d = din("dp", [128, 512])
    msk_d = din("msk", [128, 3 * 128])
    idf_d = din("idf", [128, 128])
    idb_d = din("idb", [128, 128], BF16)
    sel_d = din("sel", [16, 8 * 128])
    i16b_d = din("i16b", [128, 256])
    w_in_d = din("w_in", [D, 6672])
    w_out_d = din("w_out", [D, D])
    w_gate_d = din("w_gate", [D, DFF])
    w_up_d = din("w_up", [D, DFF])
    w_down_d = din("w_down", [DFF, D])
    g_premix_d = din("pre_mix_g", [1, D])
    g_postmix_d = din("post_mix_g", [1, D])
    g_preffn_d = din("pre_ffn_g", [1, D])
    g_postffn_d = din("post_ffn_g", [1, D])
    convw_d = din("conv_w", [4, 1536])
    convb_d = din("conv_b", [1, 1536])
    dtb_d = din("dt_bias", [1, 16])
    alog_d = din("a_log", [1, 16])
    dskip_d = din("d_skip", [1, 16])
    ssmg_d = din("ssm_norm_g", [1, 1024])
    retg_d = din("ret_norm_g", [1, 1024])
    sconv_d = din("st_conv", [16, 3, 1536])
    sssm_d = din("st_ssm", [16, 16, 64, 128])
    sret_d = din("st_ret", [16, 4, 256, 256])

    y_d = dout("y", [TOK, D])
    pconv_d = dout("p_conv", [3, 1536])
    pssm_d = dout("p_ssm", [16, 64, 128])
    pret_d = dout("p_ret", [4, 256, 256])
    oconv_d = dout("s_conv", [16, 3, 1536])
    ossm_d = dout("s_ssm", [16, 16, 64, 128])
    oret_d = dout("s_ret", [16, 4, 256, 256])
    h1_d = nc.dram_tensor("h1_scr", [TOK, D], F32).ap()
    dd_d = nc.dram_tensor("d_scr", [TOK, D], F32).ap()
    dbg_d = {}

    es = contextlib.ExitStack()
    with es:
        S = Sched(nc, es)

        def sb(st, name, shape, dt):
            _UID[0] += 1
            return st.enter_context(nc.sbuf_tensor(f"s_{name}_{_UID[0]}", shape, dt))

        identb = sb(es, "identb", [128, 128], BF16)
        identf = sb(es, "identf", [128, 128], F32)
        msk = sb(es, "msk", [128, 3, 128], F32)
        Umask, Lmask, ones = msk[:, 0, :], msk[:, 1, :], msk[:, 2, :]
        one1 = sb(es, "one1", [128, 1], F32)
        epsc = sb(es, "epsc", [128, 1], F32)
        valid = sb(es, "valid", [128, NT], F32)
        gamq = sb(es, "gamq", [128, 4], F32)
        wk = sb(es, "wk", [128, 8], F32)
        convw = sb(es, "convw", [128, 12, 4], F32)
        convb = sb(es, "convb", [128, 12], F32)
        dtb_b = sb(es, "dtb_b", [128, 16], F32)
        a_b = sb(es, "a_b", [128, 16], F32)
        dskip_b = sb(es, "dskip_b", [128, 16], F32)
        sel = sb(es, "sel", [16, 8, 128], F32)
        i16b = sb(es, "i16b", [128, 16, 16], F32)
        uT = sb(es, "uT", [128, 16, TOK], BF16)
        wp = Pool(S, nc, es, "wsl", [128, 16 * 512], BF16, 2, owners=["w0", "w1"])
        esm = contextlib.ExitStack()
        mixT = sb(esm, "mixT", [128, 16, TOK], BF16)
        b_uT = [Buf(f"uT{t}") for t in range(NT)]
        b_mixT = [Buf(f"mixT{t}") for t in range(NT)]
        b_const = Buf("const")
        oc = S.owner("const")

        def cload(dst, src):
            S.op("sp", lambda e, dst=dst, src=src: e.dma_start(out=dst, in_=src), pwrites=[b_const], dma=oc)

        cload(identb[:], idb_d)
        S.op("dve", lambda e: e.memset(one1[:], 1.0), pwrites=[b_const])
        S.op("dve", lambda e: e.memset(epsc[:], EPS), pwrites=[b_const])
        S.barrier()
        oc2 = S.owner("const2")
        b_ab = Buf("a_b")

        def cload2(dst, src, wr=None):
            S.op("pool", lambda e, dst=dst, src=src: e.dma_start(out=dst, in_=src), pwrites=[b_const] if wr is None else [], writes=[] if wr is None else [wr], dma=oc2)

        cload2(a_b[:], alog_d.partition_broadcast(128), wr=b_ab)
        cload2(identf[:], idf_d)
        cload2(msk[:].rearrange("p a b -> p (a b)"), msk_d)
        cload2(valid[:], valid_d)
        cload2(sel[:].rearrange("p a b -> p (a b)"), sel_d)
        cload2(i16b[:].rearrange("p a b -> p (a b)"), i16b_d)
        cload2(gamq[:], gamq_d)
        cload2(wk[:], wk_d)
        cload2(dtb_b[:], dtb_d.partition_broadcast(128))
        cload2(dskip_b[:], dskip_d.partition_broadcast(128))
        for ct in range(12):
            cload2(convw[:, ct, :], convw_d[:, ct * 128:(ct + 1) * 128].rearrange("i p -> p i"))
            cload2(convb[:, ct:ct + 1], convb_d[:, ct * 128:(ct + 1) * 128].rearrange("i p -> p i"))
        S.op("act", lambda e: e.activation(out=a_b[:], in_=a_b[:], func=AF.Exp), reads=[b_ab], writes=[b_ab])
        S.op("dve", lambda e: e.tensor_scalar(out=a_b[:], in0=a_b[:], scalar1=-1.0, scalar2=None, op0=ALU.mult), reads=[b_ab], writes=[b_ab])

        pf = Pool(S, nc, es, "pf", [128, 512], F32, 5, psum=True)
        pacc = Pool(S, nc, es, "pacc", [128, 512], F32, 1, psum=True)
        pb = Pool(S, nc, es, "pb", [128, 8, 128], BF16, 2, psum=True)

        cnt = {"ev": 0}

        def evac_copy(out, in_, reads, writes=(), pwrites=()):
            cnt["ev"] += 1
            if cnt["ev"] % 2:
                S.op("act", lambda e: e.activation(out=out, in_=in_, func=AF.Copy), reads=reads, writes=writes, pwrites=pwrites)
            else:
                S.op("dve", lambda e: e.tensor_copy(out=out, in_=in_), reads=reads, writes=writes, pwrites=pwrites)

        def rstd_from_ss(ss, b_ss, n, inv_n):
            S.op("act", lambda e: e.activation(out=ss[:n], in_=ss[:n], func=AF.Ln, bias=epsc[:n], scale=inv_n), reads=[b_ss], writes=[b_ss])
            S.op("act", lambda e: e.activation(out=ss[:n], in_=ss[:n], func=AF.Exp, scale=-0.5), reads=[b_ss], writes=[b_ss])

        def transposes_to_T(src, b_src, n, nk, dstT, b_dst, first_dst, kbase, c0, ident=None):
            k = 0
            first = first_dst
            while k < nk:
                m = min(8, nk - k)
                pt, b_pt, _ = pb.next()
                for j in range(m):
                    S.op("pe", lambda e, pt=pt, j=j, kk=k + j: e.transpose(out=pt[:, j, :n], in_=src[:n, kk * 128:(kk + 1) * 128], identity=identb[:n, :n]),
                         reads=[b_src], writes=[b_pt] if j == 0 else [], pwrites=[] if j == 0 else [b_pt])
                evac_copy(dstT[:, kbase + k:kbase + k + m, c0:c0 + n], pt[:, 0:m, :n], reads=[b_pt],
                          writes=[b_dst] if first else [], pwrites=[] if first else [b_dst])
                first = False
                k += m

        wpool_state = {}

        def load_w(wp, pieces, kt):
            wt, b_w, ow = wp.next()
            first = True
            for (src, width, off, tot) in pieces:
                view = wt[:, 0:kt * tot].rearrange("p (k c) -> p k c", c=tot)
                S.op("pool", lambda e, view=view, src=src, off=off, width=width: e.dma_start(
                    out=view[:, :, off:off + width], in_=src.rearrange("(k p) c -> p k c", p=128)),
                    writes=[b_w] if first else [], pwrites=[] if first else [b_w], dma=ow)
                first = False
            tot = pieces[0][3]
            return wt[:, 0:kt * tot].rearrange("p (k c) -> p k c", c=tot), b_w

        _DEFER = []

        def flush_defer():
            while _DEFER:
                _DEFER.pop(0)()

        def proj_tok(actT, b_act, wv, b_w, ncols, tiles, evac, kt=16):
            for t in tiles:
                n = rows(t)
                ps, b_ps, _ = pf.next()
                for k in range(kt):
                    S.op("pe", lambda e, ps=ps, k=k, t=t, n=n: e.matmul(ps[:n, 0:ncols], lhsT=actT[:, k, t * 128:t * 128 + n], rhs=wv[:, k, 0:ncols], start=(k == 0), stop=(k == kt - 1)),
                         reads=[b_act[t], b_w], writes=[b_ps] if k == 0 else [], pwrites=[] if k == 0 else [b_ps])
                while _DEFER:
                    _DEFER.pop(0)()
                d = evac(t, n, ps, b_ps)
                if d is not None:
                    _DEFER.append(d)

        def proj_feat(actT, b_act, wv, b_w, ncoltiles, groups, evac, kt=16):
            for ct in range(ncoltiles):
                for (c0, n) in groups:
                    ps, b_ps, _ = pf.next()
                    tl = sorted(set(range(c0 // 128, (c0 + n - 1) // 128 + 1)))
                    for k in range(kt):
                        S.op("pe", lambda e, ps=ps, k=k, ct=ct, c0=c0, n=n: e.matmul(ps[:, 0:n], lhsT=wv[:, k, ct * 128:(ct + 1) * 128], rhs=actT[:, k, c0:c0 + n], start=(k == 0), stop=(k == kt - 1)),
                             reads=[b_act[t] for t in tl] + [b_w], writes=[b_ps] if k == 0 else [], pwrites=[] if k == 0 else [b_ps])
                    evac(ct, c0, n, ps, b_ps)

        def norm_phase(x_dram, gain_dram, dstT, b_dst):
            with contextlib.ExitStack() as ph:
                xs_p = Pool(S, nc, ph, "xs", [128, D], F32, 4, owners=["xs0", "xs1", "xs2", "xs3"])
                u_p = Pool(S, nc, ph, "ub", [128, D], BF16, 3)
                ss_p = Pool(S, nc, ph, "ss", [128, 1], F32, 4)
                junk = sb(ph, "junk", [128, D], BF16)
                b_junk = Buf("junk")
                gt = sb(ph, "gt", [128, D], F32)
                b_gt = Buf("gt")
                S.op("sp", lambda e: e.dma_start(out=gt[:], in_=gain_dram.partition_broadcast(128)), writes=[b_gt], dma=S.owner("gt0"))
                def norm_tile(t):
                    n = rows(t)
                    xs, b_xs, ow = xs_p.next()
                    S.op("sp", lambda e, xs=xs, t=t, n=n: e.dma_start(out=xs[:n], in_=x_dram[t * 128:t * 128 + n, :]), writes=[b_xs], dma=ow)
                    ss, b_ss, _ = ss_p.next()
                    S.op("dve", lambda e, ss=ss: e.memset(ss[:], 0.0), writes=[b_ss])
                    S.op("act", lambda e, xs=xs, ss=ss, n=n: e.activation(out=junk[:n], in_=xs[:n], func=AF.Square, accum_out=ss[:n]),
                         reads=[b_xs, b_ss], writes=[b_junk, b_ss])
                    rstd_from_ss(ss, b_ss, n, 1.0 / D)
                    u, b_u, _ = u_p.next()
                    S.op("dve", lambda e, u=u, xs=xs, ss=ss, n=n: e.scalar_tensor_tensor(out=u[:n], in0=xs[:n], scalar=ss[:n, 0:1], in1=gt[:n], op0=ALU.mult, op1=ALU.mult),
                         reads=[b_xs, b_ss, b_gt], writes=[b_u])
                    yield
                    transposes_to_T(u, b_u, n, 16, dstT, b_dst[t], True, 0, t * 128)
                    yield
                gens = [norm_tile(t) for t in range(NT)]
                next(gens[0])
                for t in range(NT):
                    if t + 1 < NT:
                        next(gens[t + 1])
                    next(gens[t])
                S.barrier()

        def mixer_pass(full, st):
            hT, hTb, b_hT, b_hTb = st["hT"], st["hTb"], st["b_hT"], st["b_hTb"]
            Sst, Sb, b_S, b_Sb = st["S"], st["Sb"], st["b_S"], st["b_Sb"]
            convtail, b_ct = st["convtail"], st["b_ct"]
            ptail, b_pt_ = st["ptail"], st["b_ptail"]
            chunk_tiles = list(range(8)) if full else list(range(9))
            NSEQ = 1024 if full else 1040
            with contextlib.ExitStack() as ph:
                pssd = contextlib.ExitStack()
                dt_all = sb(pssd, "dt_all", [128, NT, 16], F32)
                dta_all = sb(pssd, "dta_all", [128, NT, 16], F32)
                exps = sb(pssd, "exps", [128, NT, 48], F32)
                b_dt = [Buf(f"dt{t}") for t in range(NT)]
                b_ex = [Buf(f"ex{t}") for t in range(NT)]
                tmp16_p = Pool(S, nc, pssd, "tmp16", [128, 16], F32, 2)

                wv, b_w = load_w(wp, [(w_in_d[:, C_DT:C_DT + 16], 16, 0, 16)], 16)

                def evac_dt(t, n, ps, b_ps):
                    tm, b_tm, _ = tmp16_p.next()
                    S.op("dve", lambda e: e.tensor_tensor(out=tm[:n], in0=ps[:n, 0:16], in1=dtb_b[:n], op=ALU.add), reads=[b_ps], writes=[b_tm])
                    S.op("act", lambda e: e.activation(out=tm[:n], in_=tm[:n], func=AF.Exp), reads=[b_tm], writes=[b_tm])
                    S.op("act", lambda e: e.activation(out=dt_all[:n, t, :], in_=tm[:n], func=AF.Ln, bias=one1[:n]), reads=[b_tm], writes=[b_dt[t]])
                    if not full:
                        S.op("dve", lambda e: e.tensor_scalar(out=dt_all[:n, t, :], in0=dt_all[:n, t, :], scalar1=valid[:n, t:t + 1], scalar2=None, op0=ALU.mult),
                             reads=[b_dt[t]], writes=[b_dt[t]])
                    S.op("dve", lambda e: e.tensor_tensor(out=dta_all[:n, t, :], in0=dt_all[:n, t, :], in1=a_b[:n], op=ALU.mult), reads=[b_dt[t]], writes=[b_dt[t]])
                    if t in chunk_tiles:
                        def deferred():
                            pe_, b_pe, _ = pf.next()
                            if full:
                                S.op("pe", lambda e: e.matmul(pe_[:n, 0:16], lhsT=Umask[:n, :n], rhs=dta_all[:n, t, :], start=True, stop=True), reads=[b_dt[t]], writes=[b_pe])
                            S.op("pe", lambda e: e.matmul(pe_[:n, 16:32], lhsT=Lmask[:n, :n], rhs=dta_all[:n, t, :], start=True, stop=True), reads=[b_dt[t]],
                                 writes=[] if full else [b_pe], pwrites=[b_pe] if full else [])
                            S.op("pe", lambda e: e.matmul(pe_[:, 32:48], lhsT=ones[:n, :], rhs=dta_all[:n, t, :], start=True, stop=True), reads=[b_dt[t]], pwrites=[b_pe])
                            lo = 0 if full else 16
                            S.op("act", lambda e: e.activation(out=exps[:n, t, lo:32], in_=pe_[:n, lo:32], func=AF.Exp), reads=[b_pe], writes=[b_ex[t]])
                            S.op("act", lambda e: e.activation(out=exps[:, t, 32:48], in_=pe_[:, 32:48], func=AF.Exp), reads=[b_pe], pwrites=[b_ex[t]])
                        return deferred
                    return None

                proj_tok(uT, b_uT, wv, b_w, 16, list(range(NT)), evac_dt)
                flush_defer()
                ckpt(1)
                if full:
                    t_p = Pool(S, nc, pssd, "tp", [128, 512], F32, 2)
                    mx_p = Pool(S, nc, pssd, "mxp", [128, 512], BF16, 2)
                    ss_p = Pool(S, nc, pssd, "ssg", [128, 1], F32, 2)
                    junk = sb(pssd, "junkg", [128, 512], BF16)
                    b_junk = Buf("junkg")
                    dI = sb(pssd, "dI", [128, 16, 128], BF16)
                    b_dI = Buf("dI")
                    for hh_ in range(16):
                        S.op("dve", lambda e, hh_=hh_: e.tensor_scalar(out=dI[:, hh_, :], in0=identb[:, :], scalar1=dskip_b[:, hh_:hh_ + 1], scalar2=None, op0=ALU.mult),
                             writes=[b_dI] if hh_ == 0 else [], pwrites=[] if hh_ == 0 else [b_dI])
                    szs = sb(pssd, "szs", [16, 2, 512], F32)
                    b_szs = [Buf("szs0"), Buf("szs1")]
                    histT = sb(pssd, "histT", [128, 12, 48], F32)
                    b_histT = Buf("histT")
                    xnew = sb(pssd, "xnew", [128, 12, 16], F32)
                    xbcs = sb(pssd, "xbcs", [128, 12, 16], F32)
                    b_xnew = [Buf(f"xnew{i}") for i in range(12)]
                    b_xbcs = [Buf(f"xbcs{i}") for i in range(12)]
                    decq = sb(pssd, "decq", [128, 8, 16], F32)
                    b_decq = Buf("decq")
                    dtaT = sb(pssd, "dtaT", [16, 16], F32)
                    b_dtaT = Buf("dtaT")
                    accs_p = Pool(S, nc, pssd, "accs", [128, 16], F32, 2)
                    hist_scope = contextlib.ExitStack()
                    hist_tok = sb(hist_scope, "hist_tok", [48, 1536], F32)
                    b_hist = Buf("hist_tok")
                    S.op("sp", lambda e: e.dma_start(out=hist_tok[:], in_=sconv_d.rearrange("t i c -> (t i) c")), writes=[b_hist], dma=S.owner("gt0"))
                    S.op("sp", lambda e: e.dma_start(out=oconv_d[:, 0:2, :], in_=sconv_d[:, 1:3, :]), dma=S.owner("st0"))
                    for q4 in range(3):
                        psh, b_psh, _ = pf.next()
                        for j in range(4):
                            ct = 4 * q4 + j
                            S.op("pe", lambda e, psh=psh, ct=ct, j=j: e.transpose(out=psh[:, 48 * j:48 * j + 48], in_=hist_tok[:48, ct * 128:(ct + 1) * 128], identity=identf[:48, :48]),
                                 reads=[b_hist], writes=[b_psh] if j == 0 else [], pwrites=[] if j == 0 else [b_psh])
                        S.op("dve", lambda e, psh=psh, q4=q4: e.tensor_copy(out=histT[:, 4 * q4:4 * q4 + 4, :], in_=psh[:, 0:192].rearrange("p (a b) -> p a b", b=48)), reads=[b_psh],
                             writes=[b_histT] if q4 == 0 else [], pwrites=[] if q4 == 0 else [b_histT])
                    psd, b_psd, _ = pf.next()
                    S.op("pe", lambda e: e.transpose(out=psd[:16, 0:16], in_=dta_all[:16, 8, :], identity=identf[:16, :16]), reads=[b_dt[8]], writes=[b_psd])
                    S.op("dve", lambda e: e.tensor_copy(out=dtaT[:, :], in_=psd[:16, 0:16]), reads=[b_psd], writes=[b_dtaT])
                    psq, b_psq, _ = pf.next()
                    for hp in range(8):
                        S.op("pe", lambda e, hp=hp: e.matmul(psq[:, 16 * hp:16 * hp + 16], lhsT=sel[:16, hp, :], rhs=dtaT[:16, :16], start=True, stop=True), reads=[b_dtaT],
                             writes=[b_psq] if hp == 0 else [], pwrites=[] if hp == 0 else [b_psq])
                    S.op("act", lambda e: e.activation(out=decq[:].rearrange("p a b -> p (a b)"), in_=psq[:, 0:128], func=AF.Exp), reads=[b_psq], writes=[b_decq])
                    S.barrier()
                    hist_scope.close()

                def sample_ssd(g, pg, gate_norm_ssd):
                    xs_s = sb(pg, "xs_s", [16, 768], F32)
                    b_xs_s = Buf("xs_s")
                    xdt_s = sb(pg, "xdt_s", [16, 512], BF16)
                    b_xdt_s = Buf("xdt_s")
                    Bsel = sb(pg, "Bsel", [16, 16, 128], BF16)
                    Csel = sb(pg, "Csel", [16, 16, 128], F32)
                    b_Bsel, b_Csel = Buf("Bsel"), Buf("Csel")
                    Cb = sb(pg, "Cb", [128, 16, 128], F32)
                    b_Cb = Buf("Cb")
                    ycol = sb(pg, "ycol", [128, 4, 16], F32)
                    b_ycol = Buf("ycol")
                    y_s = sb(pg, "y_s", [16, 512], F32)
                    b_y_s = Buf("y_s")
                    junk = sb(pg, "junks", [128, 128], F32)
                    b_junk = Buf("junks")
                    hb_p = Pool(S, nc, pg, "hb", [128, 8, 4, 128], F32, 2, owners=["sa0", "sa1"])
                    ps1, b_ps1, _ = pf.next()
                    for j in range(4):
                        S.op("pe", lambda e, j=j: e.transpose(out=ps1[:16, 128 * j:128 * j + 128], in_=xbcs[:, 4 * g + j, :], identity=identf[:, :]), reads=[b_xbcs[4 * g + j]],
                             writes=[b_ps1] if j == 0 else [], pwrites=[] if j == 0 else [b_ps1])
                    S.op("dve", lambda e: e.tensor_copy(out=xs_s[:, 0:512], in_=ps1[:16, :]), reads=[b_ps1], writes=[b_xs_s])
                    ps2, b_ps2, _ = pf.next()
                    S.op("pe", lambda e: e.transpose(out=ps2[:16, 0:128], in_=xbcs[:, 8 + g, :], identity=identf[:, :]), reads=[b_xbcs[8 + g]], writes=[b_ps2])
                    S.op("pe", lambda e: e.transpose(out=ps2[:16, 128:256], in_=xbcs[:, 10 + g, :], identity=identf[:, :]), reads=[b_xbcs[10 + g]], pwrites=[b_ps2])
                    S.op("dve", lambda e: e.tensor_copy(out=xs_s[:, 512:768], in_=ps2[:16, 0:256]), reads=[b_ps2], pwrites=[b_xs_s])
                    S.op("dve", lambda e: e.tensor_tensor(out=xdt_s[:, :].rearrange("p (h q) -> p h q", q=64), in0=xs_s[:, 0:512].rearrange("p (h q) -> p h q", q=64),
                                                          in1=bc(dt_all[:16, 8, 8 * g:8 * g + 8], 2, [16, 8, 64]), op=ALU.mult), reads=[b_xs_s, b_dt[8]], writes=[b_xdt_s])
                    S.op("dve", lambda e: e.tensor_tensor(out=Bsel[:, :, :], in0=bc(identf[:16, :16], 2, [16, 16, 128]), in1=bc(xs_s[:, 512:640], 1, [16, 16, 128]), op=ALU.mult), reads=[b_xs_s], writes=[b_Bsel])
                    S.op("dve", lambda e: e.tensor_tensor(out=Csel[:, :, :], in0=bc(identf[:16, :16], 2, [16, 16, 128]), in1=bc(xs_s[:, 640:768], 1, [16, 16, 128]), op=ALU.mult), reads=[b_xs_s], writes=[b_Csel])
                    for j in range(4):
                        psc_, b_psc_, _ = pf.next()
                        S.op("pe", lambda e, j=j, psc_=psc_: e.matmul(psc_[:, :].rearrange("p (a b) -> p a b", b=128), lhsT=ones[:16, :], rhs=Csel[:16, 4 * j:4 * j + 4, :], start=True, stop=True), reads=[b_Csel], writes=[b_psc_])
                        S.op("dve", lambda e, j=j, psc_=psc_: e.tensor_copy(out=Cb[:, 4 * j:4 * j + 4, :], in_=psc_[:, :].rearrange("p (a b) -> p a b", b=128)), reads=[b_psc_],
                             writes=[b_Cb] if j == 0 else [], pwrites=[] if j == 0 else [b_Cb])

                    def bload(bi):
                        hb, b_hb, ow = hb_p.next()
                        for tl in range(8):
                            S.op("sp", lambda e, tl=tl: e.dma_start(out=hb[:, tl], in_=sssm_d[8 * bi + tl, 8 * g:8 * g + 8].rearrange("(hp h2) p n -> (h2 p) hp n", h2=2)),
                                 writes=[b_hb] if tl == 0 else [], pwrites=[] if tl == 0 else [b_hb], dma=ow)
                        return hb, b_hb, ow
                    loaded = [bload(0), bload(1)]

                    def batch(bi):
                        hb, b_hb, ow = loaded[bi]

                        def tok(tl):
                            t = 8 * bi + tl
                            for hp in range(4):
                                pu, b_pu, _ = pf.next()
                                S.op("pe", lambda e, hp=hp, pu=pu: e.matmul(pu[:, 0:128], lhsT=xdt_s[:16, 128 * hp:128 * hp + 128], rhs=Bsel[:16, t, :], start=True, stop=True), reads=[b_xdt_s, b_Bsel], writes=[b_pu])
                                S.op("dve", lambda e, hp=hp, pu=pu: e.scalar_tensor_tensor(out=hb[:, tl, hp, :], in0=hb[:, tl, hp, :], scalar=decq[:, 4 * g + hp, t:t + 1], in1=pu[:, 0:128], op0=ALU.mult, op1=ALU.add),
                                     reads=[b_hb, b_decq, b_pu], pwrites=[b_hb])
                                S.op("dve", lambda e, hp=hp: e.scalar_tensor_tensor(out=junk[:, :], in0=hb[:, tl, hp, :], scalar=1.0, in1=Cb[:, t, :], op0=ALU.mult, op1=ALU.mult, accum_out=ycol[:, hp, t:t + 1]),
                                     reads=[b_hb, b_Cb], writes=[b_junk], pwrites=[b_ycol])
                        for tl in range(8):
                            tok(tl)
                        for tl in range(8):
                            S.op("sp", lambda e, tl=tl: e.dma_start(out=ossm_d[8 * bi + tl, 8 * g:8 * g + 8].rearrange("(hp h2) p n -> (h2 p) hp n", h2=2), in_=hb[:, tl]), reads=[b_hb], dma=ow)
                    for bi in range(2):
                        batch(bi)
                    psy, b_psy, _ = pf.next()
                    for hp in range(4):
                        S.op("pe", lambda e, hp=hp: e.transpose(out=psy[:16, 128 * hp:128 * hp + 128], in_=ycol[:, hp, :], identity=identf[:, :]), reads=[b_ycol],
                             writes=[b_psy] if hp == 0 else [], pwrites=[] if hp == 0 else [b_psy])
                    S.op("dve", lambda e: e.tensor_copy(out=y_s[:, :], in_=psy[:16, :]), reads=[b_psy], writes=[b_y_s])
                    ssmg_s = sb(pg, "ssmg_s", [16, 512], F32)
                    b_ssmg_s = Buf("ssmg_s")
                    S.op("sp", lambda e: e.dma_start(out=ssmg_s[:], in_=ssmg_d[:, 512 * g:512 * g + 512].partition_broadcast(16)), writes=[b_ssmg_s], dma=S.owner("gt0"))
                    gate_norm_ssd(8, 16, y_s, b_y_s, xs_s[:, 0:512], b_xs_s, g=g, szt=szs[:, g, :], b_szt=b_szs[g], ssmg=ssmg_s, b_ssmg=b_ssmg_s)

                def ssd_group(g):
                    with contextlib.ExitStack() as pg:
                        xbcg = sb(pg, "xbcg", [128, 6, TOK], BF16)
                        b_xbc = [Buf(f"xbc{i}") for i in range(6)]
                        pxc = contextlib.ExitStack()
                        pre_p = Pool(S, nc, pxc, "pre", [128, TOK + 3], F32, 2)
                        acc_p = Pool(S, nc, pxc, "acc", [128, TOK], F32, 2)
                        pre_cur = {}

                        def evac_xbc_factory(ctglob_list, slot_list):
                            def evac(ct, c0, n, ps, b_ps):
                                ctg = ctglob_list[ct]
                                sl = slot_list[ct]
                                if c0 == 0:
                                    pre, b_pre, _ = pre_p.next()
                                    pre_cur[ct] = (pre, b_pre)
                                    if full:
                                        S.op("dve", lambda e: e.tensor_copy(out=pre[:, 0:3], in_=convtail[:, ctg, :]), reads=[b_ct], writes=[b_pre])
                                    else:
                                        S.op("dve", lambda e: e.memset(pre[:, 0:3], 0.0), writes=[b_pre])
                                pre, b_pre = pre_cur[ct]
                                S.op("act", lambda e: e.activation(out=pre[:, 3 + c0:3 + c0 + n], in_=ps[:, 0:n], func=AF.Copy), reads=[b_ps], pwrites=[b_pre])
                                if c0 + n == TOK:
                                    N = NSEQ
                                    acc, b_acc, _ = acc_p.next()
                                    S.op("dve", lambda e: e.tensor_scalar(out=acc[:, 0:N], in0=pre[:, 0:N], scalar1=convw[:, ctg, 0:1], scalar2=convb[:, ctg:ctg + 1], op0=ALU.mult, op1=ALU.add),
                                         reads=[b_pre], writes=[b_acc])
                                    for i in range(1, 4):
                                        S.op("dve", lambda e, i=i: e.scalar_tensor_tensor(out=acc[:, 0:N], in0=pre[:, i:i + N], scalar=convw[:, ctg, i:i + 1], in1=acc[:, 0:N], op0=ALU.mult, op1=ALU.add),
                                             reads=[b_pre, b_acc], writes=[b_acc])
                                    S.op("act", lambda e: e.activation(out=xbcg[:, sl, 0:N], in_=acc[:, 0:N], func=AF.Silu), reads=[b_acc], writes=[b_xbc[sl]])
                                    if full:
                                        S.op("dve", lambda e: e.tensor_copy(out=ptail[:, ctg, :], in_=pre[:, N:N + 3]), reads=[b_pre], pwrites=[b_pt_])
                                        xn = pre[:, 1027:1043]
                                        S.op("dve", lambda e: e.tensor_copy(out=xnew[:, ctg, :], in_=xn), reads=[b_pre], writes=[b_xnew[ctg]])
                                        accs, b_accs, _ = accs_p.next()
                                        S.op("dve", lambda e: e.tensor_scalar(out=accs[:, :], in0=xn, scalar1=convw[:, ctg, 3:4], scalar2=convb[:, ctg:ctg + 1], op0=ALU.mult, op1=ALU.add),
                                             reads=[b_pre], writes=[b_accs])
                                        for i in range(3):
                                            S.op("dve", lambda e, i=i: e.scalar_tensor_tensor(out=accs[:, :], in0=histT[:, ctg, :].rearrange("p (t i) -> p t i", i=3)[:, :, i], scalar=convw[:, ctg, i:i + 1],
                                                                                             in1=accs[:, :], op0=ALU.mult, op1=ALU.add), reads=[b_histT, b_accs], writes=[b_accs])
                                        S.op("act", lambda e: e.activation(out=xbcs[:, ctg, :], in_=accs[:, :], func=AF.Silu), reads=[b_accs], writes=[b_xbcs[ctg]])
                                    else:
                                        S.op("dve", lambda e: e.tensor_copy(out=convtail[:, ctg, :], in_=pre[:, N:N + 3]), reads=[b_pre], pwrites=[b_ct])
                            return evac

                        groups = [(0, 352), (352, 352), (704, 336)]
                        wv, b_w = load_w(wp, [(w_in_d[:, C_X + 512 * g:C_X + 512 * g + 512], 512, 0, 512)], 16)
                        proj_feat(uT, b_uT, wv, b_w, 4, groups, evac_xbc_factory([4 * g + i for i in range(4)], [0, 1, 2, 3]))
                        if full:
                            wv, b_w = load_w(wp, [(w_in_d[:, C_B + 128 * g:C_B + 128 * g + 128], 128, 0, 256), (w_in_d[:, C_C + 128 * g:C_C + 128 * g + 128], 128, 128, 256)], 16)
                            proj_feat(uT, b_uT, wv, b_w, 2, groups, evac_xbc_factory([8 + g, 10 + g], [4, 5]))
                        else:
                            wv, b_w = load_w(wp, [(w_in_d[:, C_B + 128 * g:C_B + 128 * g + 128], 128, 0, 256), (w_in_d[:, C_C + 128 * g:C_C + 128 * g + 128], 128, 128, 256)], 16)
                            proj_feat(uT, b_uT, wv, b_w, 1, groups, evac_xbc_factory([8 + g], [4]))
                            psc_, b_psc_, _ = pf.next()
                            for k in range(16):
                                S.op("pe", lambda e, k=k: e.matmul(psc_[:, 0:16], lhsT=wv[:, k, 128:256], rhs=uT[:, k, 1024:1040], start=(k == 0), stop=(k == 15)),
                                     reads=[b_uT[8], b_w], writes=[b_psc_] if k == 0 else [], pwrites=[] if k == 0 else [b_psc_])
                            S.op("act", lambda e: e.activation(out=convtail[:, 10 + g, :], in_=psc_[:, 13:16], func=AF.Copy), reads=[b_psc_], pwrites=[b_ct])
                        ckpt(2)
                        S.barrier()
                        pxc.close()
                        sz = None
                        if full:
                            sz = sb(pg, "sz", [128, NT, 512], F32)
                            b_sz = [Buf(f"sz{t}") for t in range(NT)]
                            ssmg_c = sb(pg, "ssmg", [128, 512], F32)
                            b_ssmg_c = Buf("ssmg")
                            S.op("sp", lambda e: e.dma_start(out=ssmg_c[:], in_=ssmg_d[:, 512 * g:512 * g + 512].partition_broadcast(128)), writes=[b_ssmg_c], dma=S.owner("gt0"))
                            wv, b_w = load_w(wp, [(w_in_d[:, C_Z + 512 * g:C_Z + 512 * g + 512], 512, 0, 512)], 16)

                            def evac_z(t, n, ps, b_ps):
                                S.op("act", lambda e: e.activation(out=sz[:n, t, :], in_=ps[:n, :], func=AF.Silu), reads=[b_ps], writes=[b_sz[t]])
                                if t == 8:
                                    S.op("act", lambda e: e.activation(out=szs[:16, g, :], in_=ps[:16, :], func=AF.Silu), reads=[b_ps], writes=[b_szs[g]])
                            proj_tok(uT, b_uT, wv, b_w, 512, list(range(NT)), evac_z)

                        xtok_p = Pool(S, nc, pg, "xtok", [128, 512], BF16, 3)
                        xdt_p = Pool(S, nc, pg, "xdt", [128, 512], BF16, 2)
                        xw_p = Pool(S, nc, pg, "xwp", [128, 512], BF16, 2)
                        bm_p = Pool(S, nc, pg, "bmt", [128, 128], BF16, 2)
                        if full:
                            R_p = Pool(S, nc, pg, "Rp", [128, 8, 128], F32, 1)
                            es_p = Pool(S, nc, pg, "esg", [128, 8, 128], F32, 1)
                            wT_p = Pool(S, nc, pg, "wTp", [128, 8, 128], BF16, 2)
                            cb_p = Pool(S, nc, pg, "cbp", [128, 128], F32, 2)
                            y_p = Pool(S, nc, pg, "yp", [128, 512], F32, 2)

                        def gate_norm_ssd(t, n, y, b_y, xtok, b_xtok, g=g, szt=None, b_szt=None, ssmg=None, b_ssmg=None):
                            folded = szt is None
                            if szt is None:
                                szt, b_szt, ssmg, b_ssmg = sz[:, t, :], b_sz[t], ssmg_c, b_ssmg_c
                            if folded:
                                return gate_norm_tail(t, n, y, b_y, g, szt, b_szt, ssmg, b_ssmg)
                            tt, b_tt, _ = t_p.next()
                            S.op("pool", lambda e: e.tensor_tensor(out=tt[:n].rearrange("p (h q) -> p h q", q=64), in0=xtok[:n].rearrange("p (h q) -> p h q", q=64),
                                                                   in1=bc(dskip_b[:n, 8 * g:8 * g + 8], 2, [n, 8, 64]), op=ALU.mult), reads=[b_xtok], writes=[b_tt])
                            S.op("dve", lambda e: e.tensor_tensor(out=y[:n], in0=y[:n], in1=tt[:n], op=ALU.add), reads=[b_y, b_tt], writes=[b_y])
                            gate_norm_tail(t, n, y, b_y, g, szt, b_szt, ssmg, b_ssmg)

                        def gate_norm_tail(t, n, y, b_y, g, szt, b_szt, ssmg, b_ssmg):
                            S.op("dve", lambda e: e.tensor_tensor(out=y[:n], in0=y[:n], in1=szt[:n], op=ALU.mult), reads=[b_y, b_szt], writes=[b_y])
                            ss, b_ss, _ = ss_p.next()
                            S.op("dve", lambda e: e.memset(ss[:], 0.0), writes=[b_ss])
                            S.op("act", lambda e: e.activation(out=junk[:n], in_=y[:n], func=AF.Square, accum_out=ss[:n]), reads=[b_y, b_ss], writes=[b_junk, b_ss])
                            rstd_from_ss(ss, b_ss, n, 1.0 / 512)
                            mx, b_mx, _ = mx_p.next()
                            S.op("dve", lambda e: e.scalar_tensor_tensor(out=mx[:n], in0=y[:n], scalar=ss[:n, 0:1], in1=ssmg[:n], op0=ALU.mult, op1=ALU.mult),
                                 reads=[b_y, b_ss, b_ssmg], writes=[b_mx])
                            transposes_to_T(mx, b_mx, n, 4, mixT, b_mixT[t], (g == 0), 4 * g, t * 128)

                        def ssd_chunk(t):
                            c = rows(t)
                            c0 = t * 128
                            pt, b_ptt, _ = pb.next()
                            for j in range(5):
                                S.op("pe", lambda e, j=j: e.transpose(out=pt[:c, j, :], in_=xbcg[:, j, c0:c0 + c], identity=identb[:, :]),
                                     reads=[b_xbc[j]], writes=[b_ptt] if j == 0 else [], pwrites=[] if j == 0 else [b_ptt])
                            xtok = b_xtok = None
                            if full:
                                xtok, b_xtok, _ = xtok_p.next()
                                S.op("dve", lambda e: e.tensor_copy(out=xtok[:c].rearrange("p (a b) -> p a b", b=128), in_=pt[:c, 0:4, :]), reads=[b_ptt], writes=[b_xtok])
                            xdt, b_xdt, _ = xdt_p.next()
                            S.op("dve", lambda e: e.tensor_tensor(out=xdt[:c].rearrange("p (h q) -> p h q", q=64), in0=pt[:c, 0:4, :].rearrange("p a (h2 q) -> p (a h2) q", q=64),
                                                                  in1=bc(dt_all[:c, t, 8 * g:8 * g + 8], 2, [c, 8, 64]), op=ALU.mult), reads=[b_ptt, b_dt[t]], writes=[b_xdt])
                            ckpt(30)
                            bmt, b_bmt, _ = bm_p.next()
                            S.op("dve", lambda e: e.tensor_copy(out=bmt[:c], in_=pt[:c, 4, :]), reads=[b_ptt], writes=[b_bmt])
                            ckpt(300)
                            xwt, b_xwt, _ = xw_p.next()
                            S.op("dve", lambda e: e.tensor_tensor(out=xwt[:c].rearrange("p (h q) -> p h q", q=64), in0=xdt[:c].rearrange("p (h q) -> p h q", q=64),
                                                                  in1=bc(exps[:c, t, 16 + 8 * g:16 + 8 * g + 8], 2, [c, 8, 64]), op=ALU.mult), reads=[b_xdt, b_ex[t]], writes=[b_xwt])
                            if full:
                                pc, b_pc, _ = pf.next()
                                S.op("pe", lambda e: e.matmul(pc[:c, 0:c], lhsT=xbcg[:, 4, c0:c0 + c], rhs=xbcg[:, 5, c0:c0 + c], start=True, stop=True), reads=[b_xbc[4], b_xbc[5]], writes=[b_pc])
                                cb, b_cb, _ = cb_p.next()
                                S.op("dve", lambda e: e.tensor_tensor(out=cb[:c, :c], in0=pc[:c, 0:c], in1=Umask[:c, :c], op=ALU.mult), reads=[b_pc], writes=[b_cb])
                                R, b_R, _ = R_p.next()
                                S.op("pool", lambda e: e.tensor_tensor(out=R[:c, :, :c], in0=bc(Umask[:c, :c], 1, [c, 8, c]), in1=bc(dta_all[:c, t, 8 * g:8 * g + 8], 2, [c, 8, c]), op=ALU.mult),
                                     reads=[b_dt[t]], writes=[b_R])
                                esg, b_esg, _ = es_p.next()
                                for hh in range(2):
                                    psg, b_psg, _ = pf.next()
                                    S.op("pe", lambda e, hh=hh, psg=psg: e.matmul(psg[:c, 0:4 * c].rearrange("p (h i) -> p h i", i=c), lhsT=Lmask[:c, :c], rhs=R[:c, 4 * hh:4 * hh + 4, :c], start=True, stop=True),
                                         reads=[b_R], writes=[b_psg])
                                    S.op("act", lambda e, hh=hh, psg=psg: e.activation(out=esg[:c, 4 * hh:4 * hh + 4, :c], in_=psg[:c, 0:4 * c].rearrange("p (h i) -> p h i", i=c), func=AF.Exp),
                                         reads=[b_psg], writes=[b_esg] if hh == 0 else [], pwrites=[] if hh == 0 else [b_esg])
                            yield
                            if full:
                                wT, b_wT, _ = wT_p.next()
                                S.op("dve", lambda e: e.tensor_tensor(out=wT[:c, :, :c], in0=esg[:c, :, :c], in1=bc(cb[:c, :c], 1, [c, 8, c]), op=ALU.mult), reads=[b_esg, b_cb], writes=[b_wT])
                            yield
                            if full:
                                pya, b_pya, _ = pf.next()
                                for h in range(8):
                                    S.op("pe", lambda e, h=h: e.matmul(pya[:c, 64 * h:64 * h + 64], lhsT=wT[:c, h, :c], rhs=xdt[:c, 64 * h:64 * h + 64], start=True, stop=False),
                                         reads=[b_wT, b_xdt], writes=[b_pya] if h == 0 else [], pwrites=[] if h == 0 else [b_pya])
                                    S.op("pe", lambda e, h=h: e.matmul(pya[:c, 64 * h:64 * h + 64], lhsT=dI[:c, 8 * g + h, :c], rhs=xtok[:c, 64 * h:64 * h + 64], start=False, stop=True),
                                         reads=[b_dI, b_xtok], pwrites=[b_pya])
                                pyb, b_pyb, _ = pf.next()
                                S.op("pe", lambda e: e.matmul(pyb[:c, :], lhsT=xbcg[:, 5, c0:c0 + c], rhs=hTb[:, g, :], start=True, stop=True), reads=[b_xbc[5], b_hTb[g]], writes=[b_pyb])
                                tt, b_tt, _ = t_p.next()
                                S.op("dve", lambda e: e.tensor_tensor(out=tt[:c].rearrange("p (h q) -> p h q", q=64), in0=pyb[:c, :].rearrange("p (h q) -> p h q", q=64),
                                                                      in1=bc(exps[:c, t, 8 * g:8 * g + 8], 2, [c, 8, 64]), op=ALU.mult), reads=[b_pyb, b_ex[t]], writes=[b_tt])
                                y, b_y, _ = y_p.next()
                                S.op("dve", lambda e: e.tensor_tensor(out=y[:c], in0=pya[:c, :], in1=tt[:c], op=ALU.add), reads=[b_pya, b_tt], writes=[b_y])
                            ckpt(31)
                            pu, b_pu, _ = pf.next()
                            S.op("pe", lambda e: e.matmul(pu[:, :], lhsT=bmt[:c, :], rhs=xwt[:c, :], start=True, stop=True), reads=[b_bmt, b_xwt], writes=[b_pu])
                            S.op("pool", lambda e: e.tensor_tensor(out=hT[:, g, :].rearrange("p (h q) -> p h q", q=64), in0=hT[:, g, :].rearrange("p (h q) -> p h q", q=64),
                                                                  in1=bc(exps[:, t, 32 + 8 * g:32 + 8 * g + 8], 2, [128, 8, 64]), op=ALU.mult), reads=[b_hT[g], b_ex[t]], writes=[b_hT[g]])
                            ckpt(32)
                            S.op("dve", lambda e: e.tensor_tensor(out=hT[:, g, :], in0=hT[:, g, :], in1=pu[:, :], op=ALU.add), reads=[b_hT[g], b_pu], writes=[b_hT[g]])
                            S.op("act", lambda e: e.activation(out=hTb[:, g, :], in_=hT[:, g, :], func=AF.Copy), reads=[b_hT[g]], writes=[b_hTb[g]])
                            yield
                            if full:
                                gate_norm_ssd(t, c, y, b_y, xtok, b_xtok)
                            yield
                        gens = [ssd_chunk(t) for t in chunk_tiles]
                        nG = len(gens)
                        next(gens[0])
                        next(gens[0])
                        for i in range(nG):
                            if i + 1 < nG:
                                next(gens[i + 1])
                            next(gens[i])
                            if i + 1 < nG:
                                next(gens[i + 1])
                            if i >= 1:
                                next(gens[i - 1])
                        next(gens[nG - 1])
                        S.barrier()
                        ckpt(4)
                        if full:
                            gate_fns[g] = gate_norm_ssd
                gate_fns = {}
                for g in range(2):
                    ssd_group(g)
                ckpt(5)
                if full:
                    for g in range(2):
                        with contextlib.ExitStack() as psm:
                            sample_ssd(g, psm, gate_fns[g])
                            S.barrier()
                    with contextlib.ExitStack() as psm:
                        xn_tok = sb(psm, "xn_tok", [16, 1536], F32)
                        b_xn_tok = Buf("xn_tok")
                        for q4 in range(3):
                            psx, b_psx, _ = pf.next()
                            for j in range(4):
                                ct = 4 * q4 + j
                                S.op("pe", lambda e, psx=psx, ct=ct, j=j: e.transpose(out=psx[:16, 128 * j:128 * j + 128], in_=xnew[:, ct, :], identity=identf[:, :]), reads=[b_xnew[ct]],
                                     writes=[b_psx] if j == 0 else [], pwrites=[] if j == 0 else [b_psx])
                            S.op("dve", lambda e, psx=psx, q4=q4: e.tensor_copy(out=xn_tok[:, 512 * q4:512 * q4 + 512], in_=psx[:16, :]), reads=[b_psx],
                                 writes=[b_xn_tok] if q4 == 0 else [], pwrites=[] if q4 == 0 else [b_xn_tok])
                        S.op("sp", lambda e: e.dma_start(out=oconv_d[:, 2, :], in_=xn_tok[:, :]), reads=[b_xn_tok], dma=S.owner("st0"))
                        S.barrier()
                S.barrier()
                pssd.close()

                if full:
                    qks_all = sb(ph, "qks_all", [16, 4, 3, 256], F32)
                    b_qks_all = [Buf(f"qks{h}") for h in range(4)]
                    sgs = sb(ph, "sgs", [16, 4, 256], F32)
                    b_sgs = [Buf(f"sgs{h}") for h in range(4)]
                gate_env = {}

                def ret_gate1(c, py, b_py):
                    yr, b_yr, _ = gate_env["yr_p"].next()
                    S.op("dve", lambda e: e.tensor_copy(out=yr[:c], in_=py[:c, 0:256]), reads=[b_py], writes=[b_yr])
                    return yr, b_yr

                def ret_gate(h, t, c, py, b_py, sgx, b_sgx):
                    yr, b_yr = ret_gate1(c, py, b_py)
                    ret_gate2(h, t, c, yr, b_yr, sgx, b_sgx)

                def ret_gate2(h, t, c, yr, b_yr, sgx, b_sgx):
                    ge = gate_env
                    c0 = t * 128
                    ss, b_ss, _ = ge["ss_p"].next()
                    S.op("dve", lambda e: e.memset(ss[:], 0.0), writes=[b_ss])
                    jr, b_jr, rg, b_rg = ge["junkr"], ge["b_junkr"], ge["retg"], ge["b_retg"]
                    S.op("act", lambda e: e.activation(out=jr[:c], in_=yr[:c], func=AF.Square, accum_out=ss[:c]), reads=[b_yr, b_ss], writes=[b_jr, b_ss])
                    rstd_from_ss(ss, b_ss, c, 1.0 / 256)
                    t2, b_t2, _ = ge["t2_p"].next()
                    S.op("dve", lambda e: e.scalar_tensor_tensor(out=t2[:c], in0=yr[:c], scalar=ss[:c, 0:1], in1=rg[:c, 256 * h:256 * h + 256], op0=ALU.mult, op1=ALU.mult),
                         reads=[b_yr, b_ss, b_rg], writes=[b_t2])
                    m2, b_m2, _ = ge["m2_p"].next()
                    S.op("dve", lambda e: e.tensor_tensor(out=m2[:c], in0=t2[:c], in1=sgx[:c], op=ALU.mult), reads=[b_t2, b_sgx], writes=[b_m2])
                    transposes_to_T(m2, b_m2, c, 2, mixT, b_mixT[t], False, 8 + 2 * h, c0)

                def sample_ret_all(psr):
                    retg_s = sb(psr, "retg_s", [16, 1024], F32)
                    b_retg_s = Buf("retg_s")
                    S.op("sp", lambda e: e.dma_start(out=retg_s[:], in_=retg_d.partition_broadcast(16)), writes=[b_retg_s], dma=S.owner("gt0"))
                    gate_env.update(ss_p=Pool(S, nc, psr, "ssr2", [128, 1], F32, 2), t2_p=Pool(S, nc, psr, "t2p2", [128, 256], F32, 2), yr_p=Pool(S, nc, psr, "yrp2", [128, 256], F32, 2), m2_p=Pool(S, nc, psr, "m2p2", [128, 256], BF16, 2),
                                    junkr=sb(psr, "junkr2", [128, 256], BF16), b_junkr=Buf("junkr2"), retg=retg_s, b_retg=b_retg_s)
                    qTs_p = Pool(S, nc, psr, "qTs", [128, 2, 16], F32, 2)
                    qTsel_p = Pool(S, nc, psr, "qTsel", [128, 2, 16, 16], BF16, 2)
                    vsel_p = Pool(S, nc, psr, "vsel", [16, 16, 256], BF16, 2)
                    kb_p = Pool(S, nc, psr, "kbp", [16, 256], BF16, 2)
                    snb_p = Pool(S, nc, psr, "snb", [128, 512], BF16, 3)
                    sb_p = Pool(S, nc, psr, "sbat", [128, 8, 2, 256], F32, 2, owners=["sa0", "sa1"])

                    rloaded = {}

                    def rload(h, bi):
                        sbt, b_sbt, ow = sb_p.next()
                        for tl in range(8):
                            S.op("sp", lambda e, tl=tl: e.dma_start(out=sbt[:, tl], in_=sret_d[8 * bi + tl, h].rearrange("(a p) v -> p a v", p=128)),
                                 writes=[b_sbt] if tl == 0 else [], pwrites=[] if tl == 0 else [b_sbt], dma=ow)
                        rloaded[(h, bi)] = (sbt, b_sbt, ow)
                    rload(0, 0)

                    def head(h):
                        qks = qks_all[:, h]
                        b_qks = b_qks_all[h]
                        qTs, b_qTs, _ = qTs_p.next()
                        qTsel, b_qTsel, _ = qTsel_p.next()
                        psq_, b_psq_, _ = pf.next()
                        for dt_ in range(2):
                            S.op("pe", lambda e, dt_=dt_: e.transpose(out=psq_[:, 16 * dt_:16 * dt_ + 16], in_=qks[:16, 0, 128 * dt_:128 * dt_ + 128], identity=identf[:16, :16]), reads=[b_qks],
                                 writes=[b_psq_] if dt_ == 0 else [], pwrites=[] if dt_ == 0 else [b_psq_])
                        S.op("dve", lambda e: e.tensor_copy(out=qTs[:].rearrange("p a b -> p (a b)"), in_=psq_[:, 0:32]), reads=[b_psq_], writes=[b_qTs])
                        for dt_ in range(2):
                            S.op("dve", lambda e, dt_=dt_: e.tensor_tensor(out=qTsel[:, dt_, :, :], in0=bc(qTs[:, dt_, :], 1, [128, 16, 16]), in1=i16b[:, :, :], op=ALU.mult), reads=[b_qTs],
                                 writes=[b_qTsel] if dt_ == 0 else [], pwrites=[] if dt_ == 0 else [b_qTsel])
                        po, b_po, _ = pacc.next()
                        kb, b_kb, _ = kb_p.next()
                        S.op("act", lambda e: e.activation(out=kb[:, :], in_=qks[:16, 1, :], func=AF.Copy), reads=[b_qks], writes=[b_kb])
                        vsel, b_vsel, _ = vsel_p.next()
                        S.op("dve", lambda e: e.tensor_tensor(out=vsel[:, :, :], in0=bc(identf[:16, :16], 2, [16, 16, 256]), in1=bc(qks[:16, 2, :], 1, [16, 16, 256]), op=ALU.mult), reads=[b_qks], writes=[b_vsel])

                        def batch(bi):
                            sbt, b_sbt, ow = rloaded.pop((h, bi))
                            nk = 2 * h + bi + 1
                            if nk < 8:
                                rload(nk // 2, nk % 2)

                            def tok(tl):
                                t = 8 * bi + tl
                                pu, b_pu, _ = pf.next()
                                vt, b_vt = vsel[:, t, :], b_vsel
                                for dt_ in range(2):
                                    S.op("pe", lambda e, dt_=dt_: e.matmul(pu[:, 256 * dt_:256 * dt_ + 256], lhsT=kb[:16, 128 * dt_:128 * dt_ + 128], rhs=vt[:16, :], start=True, stop=True), reads=[b_kb, b_vt],
                                         writes=[b_pu] if dt_ == 0 else [], pwrites=[] if dt_ == 0 else [b_pu])
                                if pend:
                                    pend.pop(0)()
                                S.op("dve", lambda e: e.scalar_tensor_tensor(out=sbt[:, tl].rearrange("p a b -> p (a b)"), in0=sbt[:, tl].rearrange("p a b -> p (a b)"), scalar=GAM[h], in1=pu[:, :], op0=ALU.mult, op1=ALU.add),
                                     reads=[b_sbt, b_pu], pwrites=[b_sbt])
                                snb, b_snb, _ = snb_p.next()
                                S.op("act", lambda e: e.activation(out=snb[:, :], in_=sbt[:, tl].rearrange("p a b -> p (a b)"), func=AF.Copy), reads=[b_sbt], writes=[b_snb])

                                def readout():
                                    for dt_ in range(2):
                                        S.op("pe", lambda e, dt_=dt_: e.matmul(po[:16, 0:256], lhsT=qTsel[:, dt_, t, :], rhs=snb[:, 256 * dt_:256 * dt_ + 256], start=(t == 0 and dt_ == 0), stop=(t == 15 and dt_ == 1)),
                                             reads=[b_qTsel, b_snb], writes=[b_po] if (t == 0 and dt_ == 0) else [], pwrites=[] if (t == 0 and dt_ == 0) else [b_po])
                                pend.append(readout)
                            for tl in range(8):
                                tok(tl)
                            if bi == 1:
                                while pend:
                                    pend.pop(0)()
                            for tl in range(8):
                                S.op("sp", lambda e, tl=tl: e.dma_start(out=oret_d[8 * bi + tl, h].rearrange("(a p) v -> p a v", p=128), in_=sbt[:, tl]), reads=[b_sbt], dma=ow)
                        pend = []
                        for bi in range(2):
                            batch(bi)
                        ret_gate(h, 8, 16, po, b_po, sgs[:, h, :], b_sgs[h])
                    for h in range(4):
                        head(h)

                with contextlib.ExitStack() as pr:
                    rope = sb(pr, "rope", [128, NT, 2, 128], F32)
                    b_rope = Buf("rope")
                    S.op("sp", lambda e: e.dma_start(out=rope[:].rearrange("p a b c -> p (a b c)"), in_=(ropem_d if full else ropew_d)), writes=[b_rope], dma=S.owner("gt0"))
                    dp = sb(pr, "dp", [128, 4, 128], F32)
                    S.op("sp", lambda e: e.dma_start(out=dp[:].rearrange("p a b -> p (a b)"), in_=dp_d), pwrites=[b_rope], dma=S.owner("gt0"))
                    retg = sb(pr, "retg", [128, 1024], F32)
                    S.op("sp", lambda e: e.dma_start(out=retg[:], in_=retg_d.partition_broadcast(128)), pwrites=[b_rope], dma=S.owner("gt0"))
                    ktok = sb(pr, "ktok", [128, NT, 256], BF16)
                    vw = sb(pr, "vw", [128, NT, 256], BF16)
                    b_ktok = [Buf(f"ktok{t}") for t in range(NT)]
                    b_vw = [Buf(f"vw{t}") for t in range(NT)]
                    rot_p = Pool(S, nc, pr, "rot", [128, 2, 2, 128], F32, 2)
                    rt_p = Pool(S, nc, pr, "rtp", [128, 4, 2, 128], F32, 1)
                    if full:
                        qsT = sb(pr, "qsT", [128, 2, TOK], BF16)
                        kT = sb(pr, "kT", [128, 2, TOK], BF16)
                        vtok = sb(pr, "vtok", [128, NT, 256], BF16)
                        sg = sb(pr, "sg", [128, NT, 256], F32)
                        b_qsT = [Buf(f"qsT{t}") for t in range(NT)]
                        b_kT = [Buf(f"kT{t}") for t in range(NT)]
                        b_vtok = [Buf(f"vtok{t}") for t in range(NT)]
                        b_sg = [Buf(f"sg{t}") for t in range(NT)]
                        qtok_p = Pool(S, nc, pr, "qtok", [128, 256], BF16, 2)
                        sc_p = Pool(S, nc, pr, "scp", [128, 128], BF16, 2)
                        t2_p = Pool(S, nc, pr, "t2p", [128, 256], F32, 2)
                        yr_p = Pool(S, nc, pr, "yrp", [128, 256], F32, 3)
                        m2_p = Pool(S, nc, pr, "m2p", [128, 256], BF16, 2)
                        ss_p = Pool(S, nc, pr, "ssr", [128, 1], F32, 2)
                        junkr = sb(pr, "junkr", [128, 256], BF16)
                        b_junkr = Buf("junkr")
                        gate_env.update(ss_p=ss_p, t2_p=t2_p, yr_p=yr_p, m2_p=m2_p, junkr=junkr, b_junkr=b_junkr, retg=retg, b_retg=b_rope)

                    def rotary(ps, b_ps, col0, nq, n, t, rot, b_rot):
                        src = ps[:n, col0:col0 + nq * 256].rearrange("p (a h f) -> p a h f", h=2, f=128)
                        x1, x2 = src[:, :, 0, :], src[:, :, 1, :]
                        cos = bc(rope[:n, t, 0, :], 1, [n, nq, 128])
                        sin = bc(rope[:n, t, 1, :], 1, [n, nq, 128])
                        rt, b_rt, _ = rt_p.next()
                        S.op("dve", lambda e: e.tensor_tensor(out=rt[:n, 0, 0:nq, :], in0=x1, in1=cos, op=ALU.mult), reads=[b_ps, b_rope], writes=[b_rt])
                        S.op("dve", lambda e: e.tensor_tensor(out=rt[:n, 1, 0:nq, :], in0=x2, in1=sin, op=ALU.mult), reads=[b_ps, b_rope], pwrites=[b_rt])
                        S.op("dve", lambda e: e.tensor_tensor(out=rt[:n, 2, 0:nq, :], in0=x1, in1=sin, op=ALU.mult), reads=[b_ps, b_rope], pwrites=[b_rt])
                        S.op("dve", lambda e: e.tensor_tensor(out=rt[:n, 3, 0:nq, :], in0=x2, in1=cos, op=ALU.mult), reads=[b_ps, b_rope], pwrites=[b_rt])
                        S.op("dve", lambda e: e.tensor_tensor(out=rot[:n, 0:nq, 0, :], in0=rt[:n, 0, 0:nq, :], in1=rt[:n, 1, 0:nq, :], op=ALU.subtract), reads=[b_rt], writes=[b_rot])
                        S.op("dve", lambda e: e.tensor_tensor(out=rot[:n, 0:nq, 1, :], in0=rt[:n, 2, 0:nq, :], in1=rt[:n, 3, 0:nq, :], op=ALU.add), reads=[b_rt], pwrites=[b_rot])

                    def ret_head(h):
                      with contextlib.ExitStack() as prh:
                        gc = GAM[h]
                        if full:
                            qks = qks_all[:, h]
                            b_qks = b_qks_all[h]
                            wv, b_w = load_w(wp, [(w_in_d[:, C_Q + 256 * h:C_Q + 256 * h + 256], 256, 0, 512), (w_in_d[:, C_K + 256 * h:C_K + 256 * h + 256], 256, 256, 512)], 16)

                            def evac_qk(t, n, ps, b_ps):
                                rot, b_rot, _ = rot_p.next()
                                rotary(ps, b_ps, 0, 2, n, t, rot, b_rot)
                                S.op("act", lambda e: e.activation(out=ktok[:n, t, :], in_=rot[:n, 1].rearrange("p a b -> p (a b)"), func=AF.Copy), reads=[b_rot], writes=[b_ktok[t]])
                                if t == 8:
                                    S.op("dve", lambda e: e.tensor_copy(out=qks[:16, 0, :], in_=rot[:16, 0].rearrange("p a b -> p (a b)")), reads=[b_rot], writes=[b_qks])
                                    S.op("dve", lambda e: e.tensor_scalar(out=qks[:16, 1, :], in0=rot[:16, 1].rearrange("p a b -> p (a b)"), scalar1=0.0625, scalar2=None, op0=ALU.mult), reads=[b_rot], pwrites=[b_qks])
                                if t < 8:
                                    qtok, b_qtok, _ = qtok_p.next()
                                    S.op("dve", lambda e: e.tensor_scalar(out=qtok[:n], in0=rot[:n, 0].rearrange("p a b -> p (a b)"), scalar1=gamq[:n, h:h + 1], scalar2=None, op0=ALU.mult),
                                         reads=[b_rot], writes=[b_qtok])
                                    def deferred():
                                        transposes_to_T(qtok, b_qtok, n, 2, qsT, b_qsT[t], True, 0, t * 128)
                                        transposes_to_T(ktok[:, t, :], b_ktok[t], n, 2, kT, b_kT[t], True, 0, t * 128)
                                    return deferred
                                return None
                            proj_tok(uT, b_uT, wv, b_w, 512, list(range(NT)), evac_qk)
                            wv, b_w = load_w(wp, [(w_in_d[:, C_V + 256 * h:C_V + 256 * h + 256], 256, 0, 512), (w_in_d[:, C_G + 256 * h:C_G + 256 * h + 256], 256, 256, 512)], 16)

                            def evac_vg(t, n, ps, b_ps):
                                S.op("act", lambda e: e.activation(out=vtok[:n, t, :], in_=ps[:n, 0:256], func=AF.Copy), reads=[b_ps], writes=[b_vtok[t]])
                                if t == 8:
                                    S.op("act", lambda e: e.activation(out=qks[:16, 2, :], in_=ps[:16, 0:256], func=AF.Copy), reads=[b_ps], pwrites=[b_qks])
                                if t < 8:
                                    S.op("act", lambda e: e.activation(out=vw[:n, t, :], in_=ps[:n, 0:256], func=AF.Copy, scale=wk[:n, h:h + 1]), reads=[b_ps], writes=[b_vw[t]])
                                S.op("act", lambda e: e.activation(out=sg[:n, t, :], in_=ps[:n, 256:512], func=AF.Silu), reads=[b_ps], writes=[b_sg[t]])
                                if t == 8:
                                    S.op("act", lambda e: e.activation(out=sgs[:16, h, :], in_=ps[:16, 256:512], func=AF.Silu), reads=[b_ps], writes=[b_sgs[h]])
                            proj_tok(uT, b_uT, wv, b_w, 512, list(range(NT)), evac_vg)
                            flush_defer()
                        else:
                            wv, b_w = load_w(wp, [(w_in_d[:, C_K + 256 * h:C_K + 256 * h + 256], 256, 0, 512), (w_in_d[:, C_V + 256 * h:C_V + 256 * h + 256], 256, 256, 512)], 16)

                            def evac_kv(t, n, ps, b_ps):
                                rot, b_rot, _ = rot_p.next()
                                rotary(ps, b_ps, 0, 1, n, t, rot, b_rot)
                                S.op("act", lambda e: e.activation(out=ktok[:n, t, :], in_=rot[:n, 0].rearrange("p a b -> p (a b)"), func=AF.Copy), reads=[b_rot], writes=[b_ktok[t]])
                                wcol = h if t < 8 else 4 + h
                                S.op("dve", lambda e: e.tensor_scalar(out=vw[:n, t, :], in0=ps[:n, 256:512], scalar1=wk[:n, wcol:wcol + 1], scalar2=None, op0=ALU.mult), reads=[b_ps], writes=[b_vw[t]])
                            proj_tok(uT, b_uT, wv, b_w, 512, list(range(NT)), evac_kv)

                        def ret_chunk(t):
                            c = rows(t)
                            c0 = t * 128
                            if full:
                                psc, b_psc, _ = pf.next()
                                for dt_ in range(2):
                                    S.op("pe", lambda e, dt_=dt_: e.matmul(psc[:c, 0:c], lhsT=kT[:, dt_, c0:c0 + c], rhs=qsT[:, dt_, c0:c0 + c], start=(dt_ == 0), stop=(dt_ == 1)),
                                         reads=[b_kT[t], b_qsT[t]], writes=[b_psc] if dt_ == 0 else [], pwrites=[] if dt_ == 0 else [b_psc])
                                sc, b_sc, _ = sc_p.next()
                                S.op("dve", lambda e: e.tensor_tensor(out=sc[:c, :c], in0=psc[:c, 0:c], in1=dp[:c, h, :c], op=ALU.mult), reads=[b_psc, b_rope], writes=[b_sc])
                            yield
                            if full:
                                py, b_py, _ = pf.next()
                                S.op("pe", lambda e: e.matmul(py[:c, 0:256], lhsT=sc[:c, :c], rhs=vtok[:c, t, :], start=True, stop=False), reads=[b_sc, b_vtok[t]], writes=[b_py])
                                for dt_ in range(2):
                                    S.op("pe", lambda e, dt_=dt_: e.matmul(py[:c, 0:256], lhsT=qsT[:, dt_, c0:c0 + c], rhs=Sb[:, h, dt_, :], start=False, stop=(dt_ == 1)),
                                         reads=[b_qsT[t], b_Sb[h]], pwrites=[b_py])
                                yr, b_yr = ret_gate1(c, py, b_py)
                            pu, b_pu, _ = pf.next()
                            for dt_ in range(2):
                                S.op("pe", lambda e, dt_=dt_: e.matmul(pu[:, 256 * dt_:256 * dt_ + 256], lhsT=ktok[:c, t, 128 * dt_:128 * dt_ + 128], rhs=vw[:c, t, :], start=True, stop=True),
                                     reads=[b_ktok[t], b_vw[t]], writes=[b_pu] if dt_ == 0 else [], pwrites=[] if dt_ == 0 else [b_pu])
                            gcc = gc ** c
                            S.op("dve", lambda e: e.scalar_tensor_tensor(out=Sst[:, h].rearrange("p a b -> p (a b)"), in0=Sst[:, h].rearrange("p a b -> p (a b)"), scalar=gcc, in1=pu[:, :], op0=ALU.mult, op1=ALU.add),
                                 reads=[b_S[h], b_pu], writes=[b_S[h]])
                            S.op("act", lambda e: e.activation(out=Sb[:, h].rearrange("p a b -> p (a b)"), in_=Sst[:, h].rearrange("p a b -> p (a b)"), func=AF.Copy), reads=[b_S[h]], writes=[b_Sb[h]])
                            yield
                            if full:
                                ret_gate2(h, t, c, yr, b_yr, sg[:, t, :], b_sg[t])
                            yield
                        gens = [ret_chunk(t) for t in chunk_tiles]
                        nG = len(gens)
                        next(gens[0])
                        for i in range(nG):
                            if i + 1 < nG:
                                next(gens[i + 1])
                            next(gens[i])
                            if i >= 1:
                                next(gens[i - 1])
                        next(gens[nG - 1])
                    for h in range(4):
                        ret_head(h)
                S.barrier()
                if full:
                    with contextlib.ExitStack() as psr:
                        sample_ret_all(psr)
                        S.barrier()

        with contextlib.ExitStack() as stA:
            st = {}
            st["hT"] = sb(stA, "hT", [128, 2, 512], F32)
            st["hTb"] = sb(stA, "hTb", [128, 2, 512], BF16)
            st["S"] = sb(stA, "Sst", [128, 4, 2, 256], F32)
            st["Sb"] = sb(stA, "Sb", [128, 4, 2, 256], BF16)
            st["convtail"] = sb(stA, "convtail", [128, 12, 3], F32)
            st["ptail"] = sb(stA, "ptail", [128, 12, 3], F32)
            st["b_hT"] = [Buf("hT0"), Buf("hT1")]
            st["b_hTb"] = [Buf("hTb0"), Buf("hTb1")]
            st["b_S"] = [Buf(f"S{h}") for h in range(4)]
            st["b_Sb"] = [Buf(f"Sb{h}") for h in range(4)]
            st["b_ct"] = Buf("convtail")
            st["b_ptail"] = Buf("ptail")
            S.op("dve", lambda e: e.memset(st["hT"][:].rearrange("p a b -> p (a b)"), 0.0), writes=st["b_hT"])
            S.op("dve", lambda e: e.memset(st["hTb"][:].rearrange("p a b -> p (a b)"), 0.0), writes=st["b_hTb"])
            S.op("dve", lambda e: e.memset(st["S"][:].rearrange("p a b c -> p (a b c)"), 0.0), writes=st["b_S"])
            S.op("dve", lambda e: e.memset(st["Sb"][:].rearrange("p a b c -> p (a b c)"), 0.0), writes=st["b_Sb"])
            S.barrier()

            if stop > 0:
                norm_phase(xw_d, g_premix_d, uT, b_uT)
            if stop > 1:
                mixer_pass(False, st)
            if stop > 2:
                norm_phase(xm_d, g_premix_d, uT, b_uT)
                mixer_pass(True, st)

            with contextlib.ExitStack() as ph:
                osb = sb(ph, "osb", [128, 8, 128], F32)
                b_osb = Buf("osb")
                for i in range(8):
                    ps, b_ps, _ = pf.next()
                    g, j = i // 4, i % 4
                    S.op("pe", lambda e, ps=ps, g=g, j=j: e.transpose(out=ps[:, 0:128], in_=st["hT"][:, g, 128 * j:128 * j + 128], identity=identf[:, :]), reads=[st["b_hT"][g]], writes=[b_ps])
                    evac_copy(osb[:, i, :], ps[:, 0:128], reads=[b_ps], writes=[b_osb] if i == 0 else [], pwrites=[] if i == 0 else [b_osb])
                S.op("sp", lambda e: e.dma_start(out=pssm_d.rearrange("(i h2) p n -> (h2 p) i n", h2=2), in_=osb[:]), reads=[b_osb], dma=S.owner("st0"))
                for h in range(4):
                    S.op("sp", lambda e, h=h: e.dma_start(out=pret_d[h].rearrange("(a p) v -> p a v", p=128), in_=st["S"][:, h]), reads=[st["b_S"][h]], dma=S.owner("st0"))
                pcs = sb(ph, "pcs", [3, 1536], F32)
                b_pcs = Buf("pcs")
                for q4 in range(3):
                    ps, b_ps, _ = pf.next()
                    for j in range(4):
                        ct = 4 * q4 + j
                        S.op("pe", lambda e, ps=ps, ct=ct, j=j: e.transpose(out=ps[:3, 128 * j:128 * j + 128], in_=st["ptail"][:, ct, :], identity=identf[:, :]),
                             reads=[st["b_ptail"]], writes=[b_ps] if j == 0 else [], pwrites=[] if j == 0 else [b_ps])
                    evac_copy(pcs[:, 512 * q4:512 * q4 + 512], ps[:3, :], reads=[b_ps], writes=[b_pcs] if q4 == 0 else [], pwrites=[] if q4 == 0 else [b_pcs])
                S.op("sp", lambda e: e.dma_start(out=pconv_d, in_=pcs[:]), reads=[b_pcs], dma=S.owner("st0"))
                S.barrier()

        def out_proj_phase():
            with contextlib.ExitStack() as ph:
                o_all = sb(ph, "o_all", [128, NT, D], F32)
                b_o = [Buf(f"o{t}") for t in range(NT)]
                uview = uT[:].rearrange("p a b -> p (a b)")[:, 0:16384].bitcast(F32)
                xslots = [(uview[:, 2048 * i:2048 * i + 2048], Buf(f"xu{i}"), S.owner(f"xu{i}")) for i in range(4)]
                xloaded = {}

                def xload(t):
                    xs, b_xs, ow = xslots[t % len(xslots)]
                    n = rows(t)
                    S.op("sp", lambda e: e.dma_start(out=xs[:n], in_=xm_d[t * 128:t * 128 + n, :]), writes=[b_xs], dma=ow)
                    xloaded[t] = (xs, b_xs)
                for t in range(4):
                    xload(t)
                with contextlib.ExitStack() as p3:
                    ss_p = Pool(S, nc, p3, "ss", [128, 1], F32, 4)
                    junk = sb(p3, "junk", [128, D], BF16)
                    b_junk = Buf("junk")
                    gpm = sb(p3, "gpm", [128, D], F32)
                    b_g = Buf("gains")
                    S.op("sp", lambda e: e.dma_start(out=gpm[:], in_=g_postmix_d.partition_broadcast(128)), writes=[b_g], dma=S.owner("gt0"))

                    def tileA(t):
                        n = rows(t)
                        xs, b_xs = xloaded.pop(t)
                        o = o_all[:, t, :]
                        ss, b_ss, _ = ss_p.next()
                        S.op("dve", lambda e: e.memset(ss[:], 0.0), writes=[b_ss])
                        S.op("act", lambda e: e.activation(out=junk[:n], in_=o[:n], func=AF.Square, accum_out=ss[:n]), reads=[b_o[t], b_ss], writes=[b_junk, b_ss])
                        rstd_from_ss(ss, b_ss, n, 1.0 / D)
                        S.op("dve", lambda e: e.scalar_tensor_tensor(out=o[:n], in0=o[:n], scalar=ss[:n, 0:1], in1=gpm[:n], op0=ALU.mult, op1=ALU.mult), reads=[b_o[t], b_ss, b_g], writes=[b_o[t]])
                        S.op("dve", lambda e: e.tensor_tensor(out=o[:n], in0=o[:n], in1=xs[:n], op=ALU.add), reads=[b_o[t], b_xs], writes=[b_o[t]])
                        S.op("sp", lambda e: e.dma_start(out=h1_d[t * 128:t * 128 + n, :], in_=o[:n]), reads=[b_o[t]], dma=S.owner("st0"))

                    for cb in range(4):
                        wv, b_w = load_w(wp, [(w_out_d[:, 512 * cb:512 * cb + 512], 512, 0, 512)], 16)

                        def evac(t, n, ps, b_ps, cb=cb):
                            if cb < 3:
                                evac_copy(o_all[:n, t, 512 * cb:512 * cb + 512], ps[:n, :], reads=[b_ps], writes=[b_o[t]] if cb == 0 else [], pwrites=[] if cb == 0 else [b_o[t]])
                            else:
                                S.op("act", lambda e: e.activation(out=o_all[:n, t, 512 * cb:512 * cb + 512], in_=ps[:n, :], func=AF.Copy), reads=[b_ps], pwrites=[b_o[t]])
                                tileA(t)
                                if t + 4 < NT:
                                    xload(t + 4)
                        proj_tok(mixT, b_mixT, wv, b_w, 512, list(range(NT)), evac)
                    S.barrier()
                with contextlib.ExitStack() as p4:
                    u_p = Pool(S, nc, p4, "ub", [128, D], BF16, 3)
                    ss_p = Pool(S, nc, p4, "ss", [128, 1], F32, 4)
                    junk = sb(p4, "junk", [128, D], BF16)
                    b_junk = Buf("junk")
                    gpf = sb(p4, "gpf", [128, D], F32)
                    b_g = Buf("gains")
                    S.op("sp", lambda e: e.dma_start(out=gpf[:], in_=g_preffn_d.partition_broadcast(128)), writes=[b_g], dma=S.owner("gt0"))

                    def tileB(t):
                        n = rows(t)
                        o = o_all[:, t, :]
                        ss2, b_ss2, _ = ss_p.next()
                        S.op("dve", lambda e: e.memset(ss2[:], 0.0), writes=[b_ss2])
                        S.op("act", lambda e: e.activation(out=junk[:n], in_=o[:n], func=AF.Square, accum_out=ss2[:n]), reads=[b_o[t], b_ss2], writes=[b_junk, b_ss2])
                        rstd_from_ss(ss2, b_ss2, n, 1.0 / D)
                        u, b_u, _ = u_p.next()
                        S.op("dve", lambda e: e.scalar_tensor_tensor(out=u[:n], in0=o[:n], scalar=ss2[:n, 0:1], in1=gpf[:n], op0=ALU.mult, op1=ALU.mult), reads=[b_o[t], b_ss2, b_g], writes=[b_u])
                        yield
                        transposes_to_T(u, b_u, n, 16, uT, b_uT[t], True, 0, t * 128)
                        yield
                    gens = [tileB(t) for t in range(NT)]
                    next(gens[0])
                    for t in range(NT):
                        if t + 1 < NT:
                            next(gens[t + 1])
                        next(gens[t])
                    S.barrier()

        def ffn_phase():
            groups = [(0, 352), (352, 352), (704, 336)]
            with contextlib.ExitStack() as ph:
                actT = sb(ph, "actT", [128, 44, TOK], BF16)
                b_actg = [Buf(f"actg{i}") for i in range(3)]
                with contextlib.ExitStack() as p2:
                    sg_p = Pool(S, nc, p2, "sgp", [128, 512], F32, 3)

                    def block(j):
                        wv, b_w = load_w(wp, [(w_gate_d[:, 256 * j:256 * j + 256], 256, 0, 512), (w_up_d[:, 256 * j:256 * j + 256], 256, 256, 512)], 16)
                        for ct in range(2):
                            for gi, (c0, n) in enumerate(groups):
                                tl = sorted(set(range(c0 // 128, (c0 + n - 1) // 128 + 1)))
                                pg_, b_pg, _ = pf.next()
                                pu_, b_pu, _ = pf.next()
                                for (ps, b_ps, off) in ((pg_, b_pg, 0), (pu_, b_pu, 256)):
                                    for k in range(16):
                                        S.op("pe", lambda e, ps=ps, k=k, ct=ct, c0=c0, n=n, off=off: e.matmul(ps[:, 0:n], lhsT=wv[:, k, off + ct * 128:off + ct * 128 + 128], rhs=uT[:, k, c0:c0 + n], start=(k == 0), stop=(k == 15)),
                                             reads=[b_uT[t] for t in tl] + [b_w], writes=[b_ps] if k == 0 else [], pwrites=[] if k == 0 else [b_ps])
                                sgt, b_sgt, _ = sg_p.next()
                                S.op("act", lambda e, pg_=pg_, sgt=sgt, n=n: e.activation(out=sgt[:, 0:n], in_=pg_[:, 0:n], func=AF.Silu), reads=[b_pg], writes=[b_sgt])
                                first = (j == 0 and ct == 0 and gi == 0)
                                S.op("dve", lambda e, pu_=pu_, sgt=sgt, n=n, c0=c0, ct=ct: e.tensor_tensor(out=actT[:, 2 * j + ct, c0:c0 + n], in0=pu_[:, 0:n], in1=sgt[:, 0:n], op=ALU.mult), reads=[b_pu, b_sgt],
                                     writes=[b_actg[0]] if first else [], pwrites=[] if first else [b_actg[0]])
                    for j in range(22):
                        block(j)
                    S.barrier(full=True)
                with contextlib.ExitStack() as p3:
                    wp2 = Pool(S, nc, p3, "wsd", [128, 44 * 256], BF16, 2, owners=["wd0", "wd1"])
                    dst_p = Pool(S, nc, p3, "dst", [128, 256], F32, 2, owners=["ds0", "ds1"])
                    b_actt = [b_actg[0] for t in range(NT)]
                    uview = uT[:].rearrange("p a b -> p (a b)")[:, 0:16384].bitcast(F32)
                    hslots = [(uview[:, 2048 * i:2048 * i + 2048], Buf(f"hv{i}"), S.owner(f"xu{i}")) for i in range(4)]
                    w0, w1 = wp.tiles[0][0], wp.tiles[1][0]
                    gtv = w0[:, 0:4096].bitcast(F32)
                    junkv = w0[:, 4096:6144]
                    ssv = w0[:, 6144:6176].bitcast(F32)
                    w1v = w1[:, 0:8192].bitcast(F32)
                    dslots = [(w1v[:, 2048 * i:2048 * i + 2048], Buf(f"dv{i}"), S.owner(f"dl{i}"), None) for i in range(2)]
                    wd0, b_wd0 = wp2.tiles[0][0], wp2.tiles[0][1]
                    wd0v = wd0[:, 0:8192].bitcast(F32)
                    dslots += [(wd0v[:, 2048 * i:2048 * i + 2048], Buf(f"dv{2 + i}"), S.owner(f"dl{2 + i}"), b_wd0) for i in range(2)]
                    b_gtv, b_junkv = Buf("gtv"), Buf("junkv")
                    b_ssv = [Buf(f"ssv{i}") for i in range(4)]
                    b_dd = [Buf(f"dd{t}") for t in range(NT)]
                    S.op("sp", lambda e: e.dma_start(out=gtv, in_=g_postffn_d.partition_broadcast(128)), writes=[b_gtv], dma=S.owner("gt0"))
                    hloaded, dloaded = {}, {}

                    def hload(t):
                        hl, b_hl, ow = hslots[t % 4]
                        n = rows(t)
                        S.op("sp", lambda e: e.dma_start(out=hl[:n], in_=h1_d[t * 128:t * 128 + n, :]), writes=[b_hl], dma=ow)
                        hloaded[t] = (hl, b_hl)

                    def dload(t):
                        dl, b_dl, ow, b_extra = dslots[t % 4]
                        n = rows(t)
                        S.op("sp", lambda e: e.dma_start(out=dl[:n, 0:1792], in_=dd_d[t * 128:t * 128 + n, 0:1792]), reads=[b_dd[t]],
                             writes=[b_dl] + ([b_extra] if b_extra is not None else []), dma=ow)
                        dloaded[t] = (dl, b_dl, ow)
                    for t in range(4):
                        hload(t)

                    def dblock(cb):
                        wv, b_w = load_w(wp2, [(w_down_d[:, 256 * cb:256 * cb + 256], 256, 0, 256)], 44)

                        def evac(t, n, ps, b_ps):
                            dst, b_dst, ow = dst_p.next()
                            S.op("dve", lambda e: e.tensor_copy(out=dst[:n], in_=ps[:n, 0:256]), reads=[b_ps], writes=[b_dst])
                            S.op("sp", lambda e: e.dma_start(out=dd_d[t * 128:t * 128 + n, 256 * cb:256 * cb + 256], in_=dst[:n]), reads=[b_dst], dma=ow,
                                 writes=[b_dd[t]] if cb == 0 else [], pwrites=[] if cb == 0 else [b_dd[t]])

                        def evac_last(t, n, ps, b_ps):
                            dl, b_dl, ow_d = dloaded.pop(t)
                            hl, b_hl = hloaded.pop(t)
                            S.op("dve", lambda e: e.tensor_copy(out=dl[:n, 1792:2048], in_=ps[:n, 0:256]), reads=[b_ps], pwrites=[b_dl])
                            ss, b_ss = ssv[:, t % 4:t % 4 + 1], b_ssv[t % 4]
                            S.op("dve", lambda e: e.memset(ss, 0.0), writes=[b_ss])
                            S.op("act", lambda e: e.activation(out=junkv[:n], in_=dl[:n], func=AF.Square, accum_out=ss[:n]), reads=[b_dl, b_ss], writes=[b_junkv, b_ss])
                            rstd_from_ss(ss, b_ss, n, 1.0 / D)
                            S.op("dve", lambda e: e.scalar_tensor_tensor(out=dl[:n], in0=dl[:n], scalar=ss[:n, 0:1], in1=gtv[:n], op0=ALU.mult, op1=ALU.mult), reads=[b_dl, b_ss, b_gtv], writes=[b_dl])
                            S.op("dve", lambda e: e.tensor_tensor(out=dl[:n], in0=dl[:n], in1=hl[:n], op=ALU.add), reads=[b_dl, b_hl], writes=[b_dl])
                            S.op("pool", lambda e: e.dma_start(out=y_d[t * 128:t * 128 + n, :], in_=dl[:n]), reads=[b_dl], dma=ow_d)
                            if t + 4 < NT:
                                dload(t + 4)
                            if t + 4 < NT:
                                hload(t + 4)
                        if cb == 7:
                            for t_ in range(4):
                                dload(t_)
                        proj_tok(actT, b_actt, wv, b_w, 256, list(range(NT)), evac_last if cb == 7 else evac, kt=44)
                    for cb in range(8):
                        dblock(cb)
                    S.barrier(full=True)

        def final_phase():
            with contextlib.ExitStack() as ph:
                h_p = Pool(S, nc, ph, "hl", [128, D], F32, 4, owners=["xs0", "xs1", "xs2", "xs3"])
                d_p = Pool(S, nc, ph, "dl", [128, D], F32, 4, owners=["dl0", "dl1", "dl2", "dl3"])
                ss_p = Pool(S, nc, ph, "ss", [128, 1], F32, 4)
                junk = sb(ph, "junk", [128, D], BF16)
                b_junk = Buf("junk")
                gt = sb(ph, "gt", [128, D], F32)
                b_gt = Buf("gt")
                S.op("sp", lambda e: e.dma_start(out=gt[:], in_=g_postffn_d.partition_broadcast(128)), writes=[b_gt], dma=S.owner("gt0"))

                floaded = {}

                def fload(t):
                    n = rows(t)
                    hl, b_hl, ow_h = h_p.next()
                    dl, b_dl, ow_d = d_p.next()
                    S.op("sp", lambda e: e.dma_start(out=hl[:n], in_=h1_d[t * 128:t * 128 + n, :]), writes=[b_hl], dma=ow_h)
                    S.op("sp", lambda e: e.dma_start(out=dl[:n], in_=dd_d[t * 128:t * 128 + n, :]), writes=[b_dl], dma=ow_d)
                    floaded[t] = (hl, b_hl, ow_h, dl, b_dl, ow_d)

                def tile(t):
                    n = rows(t)
                    hl, b_hl, ow_h, dl, b_dl, ow_d = floaded.pop(t)
                    ss, b_ss, _ = ss_p.next()
                    S.op("dve", lambda e: e.memset(ss[:], 0.0), writes=[b_ss])
                    S.op("act", lambda e: e.activation(out=junk[:n], in_=dl[:n], func=AF.Square, accum_out=ss[:n]), reads=[b_dl, b_ss], writes=[b_junk, b_ss])
                    rstd_from_ss(ss, b_ss, n, 1.0 / D)
                    S.op("dve", lambda e: e.scalar_tensor_tensor(out=dl[:n], in0=dl[:n], scalar=ss[:n, 0:1], in1=gt[:n], op0=ALU.mult, op1=ALU.mult), reads=[b_dl, b_ss, b_gt], writes=[b_dl])
                    S.op("dve", lambda e: e.tensor_tensor(out=dl[:n], in0=dl[:n], in1=hl[:n], op=ALU.add), reads=[b_dl, b_hl], writes=[b_dl])
                    S.op("sp", lambda e: e.dma_start(out=y_d[t * 128:t * 128 + n, :], in_=dl[:n]), reads=[b_dl], dma=ow_d)
                for t in range(3):
                    fload(t)
                for t in range(NT):
                    tile(t)
                    if t + 3 < NT:
                        fload(t + 3)
                S.barrier()

        if stop > 3:
            out_proj_phase()
        S.barrier()
        if not dbg:
            esm.close()
        if stop > 4:
            ffn_phase()

        _HALT[0] = False
        if dbg:
            dbg_d["mixT"] = dout("dbg_mixT", [128, 16 * TOK], BF16)
            S.op("sp", lambda e: e.dma_start(out=dbg_d["mixT"], in_=mixT[:].rearrange("p a b -> p (a b)")), reads=b_mixT, dma=S.owner("st0"))
            dbg_d["uT"] = dout("dbg_uT", [128, 16 * TOK], BF16)
            S.op("sp", lambda e: e.dma_start(out=dbg_d["uT"], in_=uT[:].rearrange("p a b -> p (a b)")), reads=b_uT, dma=S.owner("st0"))
            S.barrier()

        S.finalize()
        with nc.allow_non_contiguous_dma(reason="small strided parameter / state layouts"):
            S.emit()
    return nc


def _tables():
    f32 = np.float32
    lg = np.log1p(-(2.0 ** (-5.0 - np.arange(4)))).astype(np.float64)
    i = np.arange(128)
    gamq = np.exp((i[:, None] + 1.0) * lg[None, :]).astype(f32)
    wk128 = (np.exp((127.0 - i[:, None]) * lg[None, :]) / 16.0).astype(f32)
    wk16 = np.zeros((128, 4), f32)
    wk16[:16] = (np.exp((15.0 - np.arange(16)[:, None]) * lg[None, :]) / 16.0).astype(f32)
    wk = np.concatenate([wk128, wk16], axis=1)
    dp = np.zeros((128, 4, 128), f32)
    for h in range(4):
        m = (i[:, None] <= i[None, :])
        dp[:, h, :] = np.where(m, np.exp(-(i[:, None] + 1.0) * lg[h]) / 16.0, 0.0)
    U = (i[:, None] <= i[None, :]).astype(f32)
    L = (i[:, None] > i[None, :]).astype(f32)
    msk = np.concatenate([U, L, np.ones((128, 128), f32)], axis=1)
    sel = np.zeros((16, 8, 128), f32)
    for hp in range(8):
        for q in range(128):
            sel[2 * hp + q // 64, hp, q] = 1.0
    i16b = np.broadcast_to(np.eye(16, dtype=f32).reshape(1, 256), (128, 256)).copy()
    return gamq, wk, dp.reshape(128, 512), msk, sel.reshape(16, 1024), i16b


def _rope_table(pos):
    f32 = np.float32
    half = 128
    inv_freq = (f32(10000.0) ** (-(np.arange(half, dtype=f32)) / f32(half))).astype(f32)
    ang = (pos.astype(f32)[:, None] * inv_freq[None, :]).astype(f32)
    cs = np.stack([np.cos(ang), np.sin(ang)], axis=1).astype(f32)
    out = np.zeros((NT * 128, 2, 128), f32)
    out[:TOK] = cs
    return np.ascontiguousarray(out.reshape(NT, 128, 2, 128).transpose(1, 0, 2, 3).reshape(128, NT * 256))


_CACHE = {}


def kernel(x_prompt, x_sample, state_conv, state_ssm, state_ret, meta_tokens, pre_mix_g, post_mix_g,
           pre_ffn_g, post_ffn_g, w_in, conv_w, conv_b, dt_bias, a_log, d_skip, ssm_norm_g, ret_norm_g,
           w_out, w_gate, w_up, w_down, _dbg=False, _stop=99, _cores=None, _trace=False):
    f32 = np.float32
    A = lambda a: np.ascontiguousarray(np.asarray(a, dtype=f32))
    x_prompt, x_sample, meta_tokens = A(x_prompt), A(x_sample), A(meta_tokens)
    gamq, wk, dp, msk, sel, i16b = _tables()
    shared = dict(
        gamq=gamq, wk=wk, dp=dp, msk=msk, sel=sel, i16b=i16b, idf=np.eye(128, dtype=f32), idb=np.eye(128).astype(ml_dtypes.bfloat16),
        w_in=A(w_in[0]), w_out=A(w_out[0]), w_gate=A(w_gate[0]), w_up=A(w_up[0]), w_down=A(w_down[0]),
        pre_mix_g=A(pre_mix_g), post_mix_g=A(post_mix_g), pre_ffn_g=A(pre_ffn_g), post_ffn_g=A(post_ffn_g),
        conv_w=A(conv_w[0]), conv_b=A(conv_b), dt_bias=A(dt_bias), a_log=A(a_log), d_skip=A(d_skip),
        ssm_norm_g=A(ssm_norm_g), ret_norm_g=A(ret_norm_g))
    in_maps = []
    for c in range(8):
        b, half = c // 2, c % 2
        hp = np.concatenate([meta_tokens, x_prompt[b]], axis=0)
        if half == 0:
            xw = np.concatenate([np.zeros((1024, D), f32), meta_tokens], axis=0)
            posw = np.concatenate([np.zeros(1024, f32), np.arange(16, dtype=f32)])
            validw = np.concatenate([np.zeros(1024, f32), np.ones(128, f32)])
            main = hp[16:1040]
            posm = np.arange(16, 1040, dtype=f32)
        else:
            xw = hp[0:1040]
            posw = np.arange(1040, dtype=f32)
            validw = np.ones(1152, f32)
            main = hp[1040:2064]
            posm = np.arange(1040, 2064, dtype=f32)
        xm = np.concatenate([main, x_sample[16 * c:16 * c + 16, 0]], axis=0)
        posm = np.concatenate([posm, np.full(16, 16384.0, f32)])
        m = dict(shared)
        m.update(xw=A(xw), xm=A(xm), validw=A(validw.reshape(NT, 128).T),
                 ropew=_rope_table(posw), ropem=_rope_table(posm),
                 st_conv=A(state_conv[0, 16 * c:16 * c + 16]), st_ssm=A(state_ssm[0, 16 * c:16 * c + 16]),
                 st_ret=A(state_ret[0, 16 * c:16 * c + 16]))
        in_maps.append(m)
    key = ("nc", _dbg, _stop, _SUB[0])
    if key not in _CACHE:
        _CACHE[key] = build_program(dbg=_dbg, stop=_stop)
    nc = _CACHE[key]
    if _cores is not None:
        res = run_bass_kernel_spmd(nc, [in_maps[c] for c in _cores], core_ids=list(range(len(_cores))), trace=_trace)
        if _trace:
            print("exec_time_ns", res.exec_time_ns)
        return res.results
    res = run_bass_kernel_spmd(nc, in_maps, core_ids=list(range(8)))
    R = res.results
    y_prompt = np.zeros((4, 2048, D), f32)
    y_sample = np.zeros((128, 1, D), f32)
    p_conv = np.zeros((1, 4, 3, 1536), f32)
    p_ssm = np.zeros((1, 4, 16, 64, 128), f32)
    p_ret = np.zeros((1, 4, 4, 256, 256), f32)
    s_conv = np.zeros((1, 128, 3, 1536), f32)
    s_ssm = np.zeros((1, 128, 16, 64, 128), f32)
    s_ret = np.zeros((1, 128, 4, 256, 256), f32)
    for c in range(8):
        b, half = c // 2, c % 2
        y = np.asarray(R[c]["y"])
        y_prompt[b, 1024 * half:1024 * half + 1024] = y[:1024]
        y_sample[16 * c:16 * c + 16, 0] = y[1024:1040]
        if half == 1:
            p_conv[0, b] = np.asarray(R[c]["p_conv"])
            p_ssm[0, b] = np.asarray(R[c]["p_ssm"])
            p_ret[0, b] = np.asarray(R[c]["p_ret"])
        s_conv[0, 16 * c:16 * c + 16] = np.asarray(R[c]["s_conv"])
        s_ssm[0, 16 * c:16 * c + 16] = np.asarray(R[c]["s_ssm"])
        s_ret[0, 16 * c:16 * c + 16] = np.asarray(R[c]["s_ret"])
    if _dbg:
        return R
    return (y_prompt, y_sample, p_conv, p_ssm, p_ret, s_conv, s_ssm, s_ret)
```
